# Optimizing a Trainium2 kernel written in Bass

```python
import math
import jax, jax.numpy as jnp
from jax import lax
import numpy as np

D_MODEL = 1024
BATCH = 4
SEQ = 8192
DEPTH = 2

MIX_WIDTH = D_MODEL
HEAD_DIM = 64
N_HEADS_DIFF = MIX_WIDTH // (4 * HEAD_DIM)
N_HEADS_FOX = MIX_WIDTH // (2 * HEAD_DIM)
DIFF_WIDTH = N_HEADS_DIFF * 2 * HEAD_DIM
FOX_WIDTH = N_HEADS_FOX * HEAD_DIM
IN_WIDTH = 3 * DIFF_WIDTH + 3 * FOX_WIDTH + N_HEADS_FOX
ROT_DIM = HEAD_DIM // 4
ROPE_THETA = 500000.0
Q_BLOCK = 128
D_FF = ((8 * D_MODEL // 3 + 255) // 256) * 256
EPS = 1e-6
NEG_INF = -1e30

kernel_name = "hymba_diff_fox_sandwich_block"


def rms_norm(x, g):
    xf = x.astype(jnp.float32)
    y = xf * lax.rsqrt(jnp.mean(xf * xf, axis=-1, keepdims=True) + EPS)
    return (y * g.astype(jnp.float32)).astype(x.dtype)


def rotary_tables(positions, dtype):
    inv_freq = 1.0 / (ROPE_THETA ** (jnp.arange(0, ROT_DIM, 2, dtype=jnp.float32) / ROT_DIM))
    ang = positions.astype(jnp.float32)[..., None] * inv_freq
    cos = jnp.cos(ang)[:, :, None, :].astype(dtype)
    sin = jnp.sin(ang)[:, :, None, :].astype(dtype)
    return cos, sin


def partial_rotary(t, cos, sin):
    half = ROT_DIM // 2
    t1, t2, rest = t[..., :half], t[..., half:ROT_DIM], t[..., ROT_DIM:]
    return jnp.concatenate([t1 * cos - t2 * sin, t2 * cos + t1 * sin, rest], axis=-1)


def diff_attention(q, k, v, lam):
    b, s = q.shape[0], q.shape[1]
    nb = s // Q_BLOCK
    scale = HEAD_DIM ** -0.5
    q_blocks = q.reshape(b, nb, Q_BLOCK, N_HEADS_DIFF, 2, HEAD_DIM).swapaxes(0, 1)
    k_pos = jnp.arange(s)

    def block(args):
        i, qi = args
        q_pos = i * Q_BLOCK + jnp.arange(Q_BLOCK)
        causal = q_pos[:, None] >= k_pos[None, :]
        logits = jnp.einsum('bqhmd,bkhmd->bhmqk', qi, k).astype(jnp.float32) * scale
        logits = jnp.where(causal, logits, NEG_INF)
        p = jax.nn.softmax(logits, axis=-1)
        p = p[:, :, 0] - lam * p[:, :, 1]
        return jnp.einsum('bhqk,bkhe->bqhe', p.astype(v.dtype), v)

    o = lax.map(block, (jnp.arange(nb), q_blocks))
    return o.swapaxes(0, 1).reshape(b, s, N_HEADS_DIFF, 2 * HEAD_DIM)


def forgetting_attention(q, k, v, log_f):
    b, s = q.shape[0], q.shape[1]
    nb = s // Q_BLOCK
    scale = HEAD_DIM ** -0.5
    c = jnp.cumsum(log_f, axis=1)
    c_k = c.transpose(0, 2, 1)[:, :, None, :]
    q_blocks = q.reshape(b, nb, Q_BLOCK, N_HEADS_FOX, HEAD_DIM).swapaxes(0, 1)
    c_blocks = c.reshape(b, nb, Q_BLOCK, N_HEADS_FOX).swapaxes(0, 1)
    k_pos = jnp.arange(s)

    def block(args):
        i, qi, ci = args
        q_pos = i * Q_BLOCK + jnp.arange(Q_BLOCK)
        causal = q_pos[:, None] >= k_pos[None, :]
        decay = ci.transpose(0, 2, 1)[..., None] - c_k
        logits = jnp.einsum('bqhd,bkhd->bhqk', qi, k).astype(jnp.float32) * scale + decay
        logits = jnp.where(causal, logits, NEG_INF)
        p = jax.nn.softmax(logits, axis=-1)
        return jnp.einsum('bhqk,bkhd->bqhd', p.astype(v.dtype), v)

    o = lax.map(block, (jnp.arange(nb), q_blocks, c_blocks))
    return o.swapaxes(0, 1).reshape(b, s, N_HEADS_FOX, HEAD_DIM)


def setup_inputs(seed: int = 0) -> dict:
    key = jax.random.key(seed)
    ks = jax.random.split(key, 16)
    f32 = jnp.float32

    def nrm(k, shape, scale):
        return jax.random.normal(k, shape, f32) * scale

    def gain(k, n):
        return 1.0 + 0.05 * jax.random.normal(k, (DEPTH, n), f32)

    return {
        "x": nrm(ks[0], (BATCH, SEQ, D_MODEL), 1.0),
        "positions": jnp.broadcast_to(jnp.arange(SEQ, dtype=jnp.int32), (BATCH, SEQ)),
        "attn_pre_g": gain(ks[1], D_MODEL),
        "w_in": nrm(ks[2], (DEPTH, D_MODEL, IN_WIDTH), D_MODEL ** -0.5),
        "forget_bias": jax.random.uniform(ks[3], (DEPTH, N_HEADS_FOX), f32, 1.0, 6.0),
        "lam_q1": nrm(ks[4], (DEPTH, HEAD_DIM), 0.1),
        "lam_k1": nrm(ks[5], (DEPTH, HEAD_DIM), 0.1),
        "lam_q2": nrm(ks[6], (DEPTH, HEAD_DIM), 0.1),
        "lam_k2": nrm(ks[7], (DEPTH, HEAD_DIM), 0.1),
        "diff_sub_g": gain(ks[8], 2 * HEAD_DIM),
        "w_out": nrm(ks[9], (DEPTH, MIX_WIDTH, D_MODEL), MIX_WIDTH ** -0.5),
        "attn_post_g": gain(ks[10], D_MODEL),
        "ffn_pre_g": gain(ks[11], D_MODEL),
        "w_gate": nrm(ks[12], (DEPTH, D_MODEL, D_FF), D_MODEL ** -0.5),
        "w_up": nrm(ks[13], (DEPTH, D_MODEL, D_FF), D_MODEL ** -0.5),
        "w_down": nrm(ks[14], (DEPTH, D_FF, D_MODEL), D_FF ** -0.5),
        "ffn_post_g": gain(ks[15], D_MODEL),
    }


def reference(x, positions, attn_pre_g, w_in, forget_bias, lam_q1, lam_k1, lam_q2, lam_k2,
              diff_sub_g, w_out, attn_post_g, ffn_pre_g, w_gate, w_up, w_down, ffn_post_g):
    b, s, _ = x.shape
    f32 = jnp.float32
    cos, sin = rotary_tables(positions, x.dtype)
    split_at = [DIFF_WIDTH, 2 * DIFF_WIDTH, 3 * DIFF_WIDTH,
                3 * DIFF_WIDTH + FOX_WIDTH, 3 * DIFF_WIDTH + 2 * FOX_WIDTH,
                3 * DIFF_WIDTH + 3 * FOX_WIDTH]
    for l in range(DEPTH):
        h = rms_norm(x, attn_pre_g[l])
        proj = h @ w_in[l]
        qa, ka, va, qb, kb, vb, f_logit = jnp.split(proj, split_at, axis=-1)

        qa = partial_rotary(qa.reshape(b, s, 2 * N_HEADS_DIFF, HEAD_DIM), cos, sin)
        ka = partial_rotary(ka.reshape(b, s, 2 * N_HEADS_DIFF, HEAD_DIM), cos, sin)
        qa = qa.reshape(b, s, N_HEADS_DIFF, 2, HEAD_DIM)
        ka = ka.reshape(b, s, N_HEADS_DIFF, 2, HEAD_DIM)
        va = va.reshape(b, s, N_HEADS_DIFF, 2 * HEAD_DIM)
        lam_init = 0.8 - 0.6 * math.exp(-0.3 * l)
        lam = (jnp.exp(jnp.sum(lam_q1[l].astype(f32) * lam_k1[l].astype(f32)))
               - jnp.exp(jnp.sum(lam_q2[l].astype(f32) * lam_k2[l].astype(f32)))
               + lam_init)
        oa = diff_attention(qa, ka, va, lam)
        oa = rms_norm(oa, diff_sub_g[l]) * (1.0 - lam_init)

        log_f = jax.nn.log_sigmoid(f_logit.astype(f32) + forget_bias[l].astype(f32))
        ob = forgetting_attention(qb.reshape(b, s, N_HEADS_FOX, HEAD_DIM),
                                  kb.reshape(b, s, N_HEADS_FOX, HEAD_DIM),
                                  vb.reshape(b, s, N_HEADS_FOX, HEAD_DIM), log_f)

        mixed = jnp.concatenate([oa.reshape(b, s, DIFF_WIDTH), ob.reshape(b, s, FOX_WIDTH)], axis=-1)
        x = x + rms_norm(mixed @ w_out[l], attn_post_g[l])

        h = rms_norm(x, ffn_pre_g[l])
        y = (jax.nn.silu(h @ w_gate[l]) * (h @ w_up[l])) @ w_down[l]
        x = x + rms_norm(y, ffn_post_g[l])
    return x
```

```python
import math
from contextlib import ExitStack
import numpy as np
import ml_dtypes
import concourse.bass as bass
import concourse.mybir as mybir
from concourse.bass_utils import run_bass_kernel_spmd

F32 = mybir.dt.float32
BF16 = mybir.dt.bfloat16
I32 = mybir.dt.int32
AF = mybir.ActivationFunctionType
ALU = mybir.AluOpType

D = 1024
S = 8192
T = 4096
NT = 32
NCH = 8
DEPTH = 2
DFF = 2816
NFC = 22
INW = 3080
EPS = 1e-6
NEG = -30000.0
ENG = ("pe", "act", "dve", "pool", "sp")
SAME_ENGINE_SYNC = True


class Rec:
    __slots__ = ("eng", "fn", "deps", "needs_inc", "incval", "dma", "dma_thr", "kind", "bar")

    def __init__(self, eng, fn, dma=None, kind="op"):
        self.eng = eng
        self.fn = fn
        self.deps = ()
        self.needs_inc = False
        self.incval = None
        self.dma = dma
        self.dma_thr = None
        self.kind = kind
        self.bar = None


class Prog:
    def __init__(self, nc):
        self.nc = nc
        self.recs = []
        self.streams = {e: [] for e in ENG}
        self.lastw = {}
        self.readers = {}
        self.dma_cum = {}
        self.dma_keys = []
        self.mode = "dry"
        self.seq = 0
        self.cur = None
        self.eng = None
        self.known = {}
        self.esem = None
        self.dsem = None

    def _wait(self, sem, key, val):
        if self.known.get(key, 0) >= val:
            return
        self.known[key] = val
        self.eng.wait_ge(sem, val)

    def op(self, eng, fn, reads=(), writes=(), dma=None, kind="op", extra=()):
        if self.mode != "dry":
            rec = self.recs[self.seq]
            self.seq += 1
            assert rec.eng == eng and rec.kind == kind
            if eng != self.cur:
                return rec
            for d in rec.deps:
                if d.dma is not None:
                    self._wait(self.dsem[d.dma], ("d", d.dma), d.dma_thr)
                else:
                    if d.eng == eng and (eng == "pe" or not SAME_ENGINE_SYNC):
                        continue
                    self._wait(self.esem[d.eng], ("e", d.eng), d.incval)
            ins = fn(self.eng)
            if rec.dma is not None:
                if rec.kind == "cc":
                    ins.then_inc(self.dsem[rec.dma])
                else:
                    ins.then_inc(self.dsem[rec.dma], 16)
            elif rec.needs_inc:
                ins.then_inc(self.esem[eng], 1)
            return rec
        rec = Rec(eng, None, dma, kind)
        deps = set()
        for r in reads:
            w = self.lastw.get(r)
            if w is not None:
                deps.add(w)
        for w_ in writes:
            w = self.lastw.get(w_)
            if w is not None:
                deps.add(w)
            for rd in self.readers.get(w_, ()):
                deps.add(rd)
        for x_ in extra:
            deps.add(x_)
        deps.discard(rec)
        rec.deps = deps
        for d in deps:
            d.needs_inc = True
        if dma is not None:
            if dma not in self.dma_cum:
                self.dma_cum[dma] = 0
                self.dma_keys.append(dma)
            self.dma_cum[dma] += (1 if kind == "cc" else 16)
            rec.dma_thr = self.dma_cum[dma]
        for r in reads:
            self.readers.setdefault(r, []).append(rec)
        for w_ in writes:
            self.lastw[w_] = rec
            self.readers[w_] = []
        self.streams[eng].append(rec)
        self.recs.append(rec)
        return rec

    def barrier(self):
        if self.mode != "dry":
            rec = self.recs[self.seq]
            self.seq += 1
            assert rec.kind == "bar"
            for e2 in ENG:
                last = rec.bar["last"][e2]
                if last is not None and e2 != self.cur:
                    self._wait(self.esem[e2], ("e", e2), last.incval)
            for k, v in rec.bar["dma"].items():
                self._wait(self.dsem[k], ("d", k), v)
            return
        snap = {"last": {}, "dma": {k: v for k, v in self.dma_cum.items() if not (isinstance(k, tuple) and k[0] == "cv")}}
        for e in ENG:
            last = None
            for r in reversed(self.streams[e]):
                if r.dma is None:
                    last = r
                    break
            if last is not None:
                last.needs_inc = True
            snap["last"][e] = last
        rec = Rec(None, None, kind="bar")
        rec.bar = snap
        self.recs.append(rec)
        self.lastw = {k: v for k, v in self.lastw.items() if isinstance(k, tuple) and k[0] == "cvres"}
        self.readers = {}

    def run(self, gen):
        nc = self.nc
        self.mode = "dry"
        gen()
        for e in ENG:
            c = 0
            for r in self.streams[e]:
                if r.dma is None and r.needs_inc:
                    c += 1
                    r.incval = c
        with ExitStack() as es:
            self.esem = {e: es.enter_context(nc.semaphore("s_" + e)) for e in ENG}
            self.dsem = {k: es.enter_context(nc.semaphore("d_%d" % i)) for i, k in enumerate(self.dma_keys)}
            block = es.enter_context(nc.Block())

            def mk(e):
                def body(eng):
                    self.mode = "emit"
                    self.seq = 0
                    self.cur = e
                    self.eng = eng
                    self.known = {}
                    gen()
                    assert self.seq == len(self.recs)
                return body

            block.tensor(mk("pe"))
            block.scalar(mk("act"))
            block.vector(mk("dve"))
            block.gpsimd(mk("pool"))
            block.sync(mk("sp"))


def own_blocks(r):
    out = []
    for j in range(16):
        if r == 0:
            out += [4 * j, 4 * j + 3]
        else:
            out += [4 * j + 1, 4 * j + 2]
    return out


def make_masks(r):
    def G(rank, idx):
        j, e = idx // 2, idx % 2
        return 4 * j + (3 * e if rank == 0 else 1 + e)
    tri = np.where(np.arange(128)[:, None] <= np.arange(128)[None, :], 0.0, NEG).astype(np.float32)
    M = np.zeros((128, 8, 512), np.float32)
    for X in range(2):
        for m in range(4):
            for n in range(4):
                kg, qg = G(X, m), G(r, n)
                if kg < qg:
                    blk = 0.0
                elif kg == qg:
                    blk = tri
                else:
                    blk = NEG
                M[:, X * 4 + m, n * 128:(n + 1) * 128] = blk
    return M.astype(ml_dtypes.bfloat16)


def build(debug=False, nlayers=DEPTH, stop_after=None):
    nc = bass.Bass("TRN2", target_bir_lowering=False)
    P = Prog(nc)

    def din(name, shape, dt=F32):
        return nc.dram_tensor(name, list(shape), dt, kind="ExternalInput")

    x_in = din("x", [T, D])
    pos_in = din("pos", [128, NT], I32)
    masks_in = din("masks", [128, 8, 512], BF16)
    sel_in = din("sel", [8, 2])
    invf_in = din("invf", [128, 8])
    attn_pre_g = din("attn_pre_g", [DEPTH, D])
    w_in = din("w_in", [DEPTH, D, INW])
    forget_bias = din("forget_bias", [DEPTH, 8])
    lam_q1 = din("lam_q1", [DEPTH, 64])
    lam_k1 = din("lam_k1", [DEPTH, 64])
    lam_q2 = din("lam_q2", [DEPTH, 64])
    lam_k2 = din("lam_k2", [DEPTH, 64])
    diff_sub_g = din("diff_sub_g", [DEPTH, 128])
    w_out = din("w_out", [DEPTH, D, D])
    attn_post_g = din("attn_post_g", [DEPTH, D])
    ffn_pre_g = din("ffn_pre_g", [DEPTH, D])
    w_gate = din("w_gate", [DEPTH, D, DFF])
    w_up = din("w_up", [DEPTH, D, DFF])
    w_down = din("w_down", [DEPTH, DFF, D])
    ffn_post_g = din("ffn_post_g", [DEPTH, D])
    out = nc.dram_tensor("out", [T, D], F32, kind="ExternalOutput")

    def scr(name, shape, dt):
        return nc.dram_tensor(name, list(shape), dt)

    win_b = scr("win_b", [DEPTH, 128, 8, INW], BF16)
    wout_b = scr("wout_b", [DEPTH, 128, 8, D], BF16)
    wg_b = scr("wg_b", [DEPTH, 128, 8, DFF], BF16)
    wu_b = scr("wu_b", [DEPTH, 128, 8, DFF], BF16)
    wd_b = scr("wd_b", [DEPTH, 128, NFC, D], BF16)
    xs = scr("xs", [T, D], F32)
    x1s = scr("x1s", [T, D], F32)
    QT = scr("QT", [8, 128, T], BF16)
    KTo = scr("KTo", [4, 1024, 1024], BF16)
    KTa = scr("KTa", [4, 2048, 1024], BF16)
    VDo = scr("VDo", [T, 512], BF16)
    VDa = scr("VDa", [4, 2048, 512], BF16)
    VFo = scr("VFo", [T, 520], BF16)
    VFa = scr("VFa", [4, 2048, 520], BF16)
    LTo = scr("LTo", [8, T], F32)
    LTa = scr("LTa", [16, T], F32)
    CK = scr("CK", [8, 3, 2 * T], BF16)
    CQ = scr("CQ", [8, 3, T], BF16)
    MT = scr("MT", [D, T], BF16)

    dbg = {}
    if debug:
        for nm, shp, dt in (("d_QT", [8, 128, T], BF16), ("d_KTa", [4, 2048, 1024], BF16), ("d_VDa", [4, 2048, 512], BF16),
                            ("d_VFa", [4, 2048, 520], BF16), ("d_LTa", [16, T], F32), ("d_CK", [8, 3, 2 * T], BF16),
                            ("d_CQ", [8, 3, T], BF16), ("d_MT", [D, T], BF16), ("d_x1", [T, D], F32), ("d_xs", [T, D], F32),
                            ("d_cos", [128, NT, 8], F32), ("d_sin", [128, NT, 8], F32), ("d_ang", [128, NT, 8], F32)):
            dbg[nm] = nc.dram_tensor(nm, shp, dt, kind="ExternalOutput")

    op = P.op
    uid = [0]

    def nuid():
        uid[0] += 1
        return uid[0]

    acache = []
    acur = [0]

    def alloc(mk, name, stack):
        if P.mode == "dry":
            t = stack.enter_context(mk("%s_%d" % (name, nuid())))
            acache.append(t)
            return t
        t = acache[acur[0]]
        acur[0] += 1
        return t

    def gen():
        acur[0] = 0
        with ExitStack() as gs:
            def sb(name, shape, dt, stack=gs):
                return alloc(lambda nm: nc.sbuf_tensor(nm, list(shape), dt), name, stack)

            ident = sb("ident", [128, 128], BF16)
            ones32 = sb("ones32", [128, 128], F32)
            onesb = sb("onesb", [128, 128], BF16)
            c_eps = sb("c_eps", [128, 1], F32)
            c_one = sb("c_one", [128, 1], F32)
            sel = sb("sel_sb", [8, 2], F32)
            posi = sb("posi", [128, NT], I32)
            posf = sb("posf", [128, NT], F32)
            invf = sb("invf_sb", [128, 8], F32)
            ang = sb("ang", [128, NT, 8], F32)
            cosT = sb("cosT", [128, NT, 8], F32)
            sinT = sb("sinT", [128, NT, 8], F32)
            c_pi = sb("c_pi", [128, 1], F32)

            op("pool", lambda e: e.memset(ones32[:], 1.0), writes=["ones32"])
            op("pool", lambda e: e.memset(onesb[:], 1.0), writes=["onesb"])
            op("pool", lambda e: e.memset(c_eps[:], EPS), writes=["c_eps"])
            op("pool", lambda e: e.memset(c_one[:], 1.0), writes=["c_one"])
            op("pool", lambda e: e.memset(c_pi[:], math.pi), writes=["c_pi"])
            op("pool", lambda e: e.memset(ident[:], 1.0), writes=["ident"])
            op("pool", lambda e: e.affine_select(out=ident[:], in_=ident[:], pattern=[[-1, 128]], compare_op=ALU.is_equal,
                                                 fill=0.0, base=0, channel_multiplier=1), reads=["ident"], writes=["ident"])
            op("sp", lambda e: e.dma_start(out=sel[:], in_=sel_in[:, :]), writes=["sel"], dma="c0_1")
            op("sp", lambda e: e.dma_start(out=posi[:], in_=pos_in[:, :]), writes=["posi"], dma="c0_2")
            op("sp", lambda e: e.dma_start(out=invf[:], in_=invf_in[:, :]), writes=["invf"], dma="c0_3")
            op("dve", lambda e: e.tensor_copy(out=posf[:], in_=posi[:]), reads=["posi"], writes=["posf"])
            op("dve", lambda e: e.tensor_tensor(out=ang[:], in0=posf[:].unsqueeze(2).to_broadcast([128, NT, 8]),
                                                in1=invf[:].unsqueeze(1).to_broadcast([128, NT, 8]), op=ALU.mult),
               reads=["posf", "invf"], writes=["ang"])
            TWO_PI = 2.0 * math.pi
            angi = sb("angi", [128, NT, 8], I32)
            kf = sb("kf", [128, NT, 8], F32)
            mk = sb("mk", [128, NT, 8], F32)

            def sin_of(dst, shift, tag):
                op("dve", lambda e: e.tensor_scalar(out=mk[:], in0=ang[:], scalar1=shift, scalar2=1.0 / TWO_PI, op0=ALU.add, op1=ALU.mult),
                   reads=["ang"], writes=["mk"])
                op("dve", lambda e: e.tensor_copy(out=angi[:], in_=mk[:]), reads=["mk"], writes=["angi"])
                op("dve", lambda e: e.tensor_copy(out=kf[:], in_=angi[:]), reads=["angi"], writes=["kf"])
                op("dve", lambda e: e.tensor_scalar(out=dst[:], in0=ang[:], scalar1=shift, scalar2=None, op0=ALU.add), reads=["ang"], writes=[tag])
                op("dve", lambda e: e.scalar_tensor_tensor(out=dst[:], in0=kf[:], scalar=-TWO_PI, in1=dst[:], op0=ALU.mult, op1=ALU.add),
                   reads=["kf", tag], writes=[tag])
                op("dve", lambda e: e.tensor_scalar(out=mk[:], in0=dst[:], scalar1=math.pi, scalar2=-TWO_PI, op0=ALU.is_gt, op1=ALU.mult),
                   reads=[tag], writes=["mk"])
                op("dve", lambda e: e.tensor_tensor(out=dst[:], in0=dst[:], in1=mk[:], op=ALU.add), reads=[tag, "mk"], writes=[tag])
                op("dve", lambda e: e.tensor_scalar(out=mk[:], in0=dst[:], scalar1=-math.pi, scalar2=TWO_PI, op0=ALU.is_lt, op1=ALU.mult),
                   reads=[tag], writes=["mk"])
                op("dve", lambda e: e.tensor_tensor(out=dst[:], in0=dst[:], in1=mk[:], op=ALU.add), reads=[tag, "mk"], writes=[tag])
                op("dve", lambda e: e.tensor_scalar(out=dst[:], in0=dst[:], scalar1=-3.1415925, scalar2=3.1415925, op0=ALU.max, op1=ALU.min),
                   reads=[tag], writes=[tag])
                op("act", lambda e: e.activation(out=dst[:], in_=dst[:], func=AF.Sin), reads=[tag], writes=[tag])

            sin_of(sinT, 0.0, "sinT")
            sin_of(cosT, 0.5 * math.pi, "cosT")
            if debug:
                op("sp", lambda e: e.dma_start(out=dbg["d_cos"].ap(), in_=cosT[:]), reads=["cosT"], dma="dbg")
                op("sp", lambda e: e.dma_start(out=dbg["d_sin"].ap(), in_=sinT[:]), reads=["sinT"], dma="dbg")
                op("sp", lambda e: e.dma_start(out=dbg["d_ang"].ap(), in_=ang[:]), reads=["ang"], dma="dbg")

            def conv_in(l, rd=()):
                for kc in range(8):
                    op("pool", lambda e, kc=kc: e.dma_start(out=win_b[l, :, kc, :], in_=w_in[l, kc * 128:(kc + 1) * 128, :], max_dma_last_dim=8192),
                       extra=list(rd), writes=[("cvres", "win", l, kc)], dma=("cv", l, 0))

            def conv_rest(l, rd=()):
                for kc in range(8):
                    op("pool", lambda e, kc=kc: e.dma_start(out=wout_b[l, :, kc, :], in_=w_out[l, kc * 128:(kc + 1) * 128, :], max_dma_last_dim=8192),
                       extra=list(rd), writes=[("cvres", "wout", l, kc)], dma=("cv", l, 1))
                for kc in range(8):
                    op("pool", lambda e, kc=kc: e.dma_start(out=wg_b[l, :, kc, :], in_=w_gate[l, kc * 128:(kc + 1) * 128, :], max_dma_last_dim=8192),
                       extra=list(rd), writes=[("cvres", "wg", l, kc)], dma=("cv", l, 2))
                    op("pool", lambda e, kc=kc: e.dma_start(out=wu_b[l, :, kc, :], in_=w_up[l, kc * 128:(kc + 1) * 128, :], max_dma_last_dim=8192),
                       extra=list(rd), writes=[("cvres", "wu", l, kc)], dma=("cv", l, 4))
                for fc in range(NFC):
                    op("pool", lambda e, fc=fc: e.dma_start(out=wd_b[l, :, fc, :], in_=w_down[l, fc * 128:(fc + 1) * 128, :], max_dma_last_dim=8192),
                       extra=list(rd), writes=[("cvres", "wd", l, fc)], dma=("cv", l, 3))

            conv_in(0)
            if stop_after == 'conv':
                return

            for l in range(nlayers):
                xsrc = x_in if l == 0 else xs
                xdst = xs if l == 0 else out
                lam_init = 0.8 - 0.6 * math.exp(-0.3 * l)

                GRP = [[0, 1], [2, 3], [4, 5], [6, 7]]
                with ExitStack() as st:
                    winb = sb("winb", [128, 8, INW], BF16, st)
                    gpre = sb("gpre", [128, D], F32, st)
                    negb = sb("negb", [8, 1], F32, st)
                    xt = [sb("xt%d" % i, [128, D], F32, st) for i in range(2)]
                    junk = sb("junk", [128, D], BF16, st)
                    ss = [sb("ss%d" % i, [128, 1], F32, st) for i in range(2)]
                    rstd = [sb("rstd%d" % i, [128, 1], F32, st) for i in range(2)]
                    hb = [sb("hb%d" % i, [128, D], BF16, st) for i in range(2)]
                    hT = [sb("hT%d" % i, [128, 8, 128], BF16, st) for i in range(2)]
                    q32 = [sb("q32_%d" % i, [128, 8, 64], F32, st) for i in range(2)]
                    ra = [sb("ra%d" % i, [128, 8, 8], F32, st) for i in range(2)]
                    rb = [sb("rb%d" % i, [128, 8, 8], F32, st) for i in range(2)]
                    qtok = [sb("qtok%d" % i, [128, 2048], BF16, st) for i in range(2)]
                    qTs = [sb("qTs%d" % i, [128, 8, 512], BF16, st) for i in range(2)]
                    kTs = [sb("kTs%d" % i, [128, 8, 512], BF16, st) for i in range(2)]
                    vds = [sb("vds%d" % i, [128, 512], BF16, st) for i in range(2)]
                    vfs = [sb("vfs%d" % i, [128, 8, 65], BF16, st) for i in range(2)]
                    lst = [sb("lst%d" % i, [8, 512], F32, st) for i in range(2)]
                    etmp = sb("etmp", [8, 128], F32, st)
                    PS = [alloc(lambda nm: nc.psum_tensor(nm, [128, 1024], F32), "psA%d_%d" % (l, i), st) for i in range(4)]

                    def bank(b):
                        return PS[b // 2][:, (b % 2) * 512:(b % 2 + 1) * 512]

                    op("sp", lambda e: e.dma_start(out=winb[:], in_=win_b[l, :, :, :]), reads=[("cvres", "win", l, kc) for kc in range(8)], writes=["winb"], dma="winb")
                    op("sp", lambda e: e.dma_start(out=gpre[:], in_=attn_pre_g[l].partition_broadcast(128)), writes=["gpre"], dma="winb_g")
                    op("sp", lambda e: e.dma_start(out=negb[:], in_=forget_bias[l].rearrange("(h o) -> h o", o=1)), writes=["negb"], dma="winb_n")
                    op("dve", lambda e: e.tensor_scalar(out=negb[:], in0=negb[:], scalar1=-1.0, scalar2=None, op0=ALU.mult),
                       reads=["negb"], writes=["negb"])
                    for i in range(2):
                        op("dve", lambda e, i=i: e.memset(vfs[i][:], 1.0), writes=[("vfs", i)])

                    def load_x(t):
                        i = t % 2
                        op("sp", lambda e: e.dma_start(out=xt[i][:], in_=xsrc[t * 128:(t + 1) * 128, :]), writes=[("xt", i)], dma=("xt", i))

                    def a_norm(t):
                        i = t % 2
                        op("act", lambda e: e.activation(out=junk[:], in_=xt[i][:], func=AF.Square, accum_out=ss[i][:]),
                           reads=[("xt", i)], writes=[("ss", i)])
                        op("act", lambda e: e.activation(out=rstd[i][:], in_=ss[i][:], func=AF.Ln, scale=1.0 / D, bias=c_eps[:]),
                           reads=[("ss", i), "c_eps"], writes=[("rstd", i)])
                        op("act", lambda e: e.activation(out=rstd[i][:], in_=rstd[i][:], func=AF.Exp, scale=-0.5),
                           reads=[("rstd", i)], writes=[("rstd", i)])
                        op("dve", lambda e: e.scalar_tensor_tensor(out=hb[i][:], in0=xt[i][:], scalar=rstd[i][:], in1=gpre[:],
                                                                   op0=ALU.mult, op1=ALU.mult),
                           reads=[("xt", i), ("rstd", i), "gpre"], writes=[("hb", i)])

                    def a_proj(t):
                        i = t % 2
                        trv = bank(0).bitcast(BF16).rearrange("p (a b) -> p a b", a=8)
                        for kc in range(8):
                            op("pe", lambda e, kc=kc: e.transpose(trv[:, kc, :], hb[i][:, kc * 128:(kc + 1) * 128], ident[:]),
                               reads=[("hb", i), "ident"], writes=["bk0"])
                        op("dve", lambda e: e.tensor_copy(out=hT[i][:], in_=trv), reads=["bk0"], writes=[("hT", i)])
                        for cg in range(6):
                            for kc in range(8):
                                op("pe", lambda e, cg=cg, kc=kc: e.matmul(bank(1 + cg), lhsT=hT[i][:, kc, :], rhs=winb[:, kc, cg * 512:(cg + 1) * 512],
                                                                         start=(kc == 0), stop=(kc == 7)),
                                   reads=[("hT", i), "winb"], writes=["bk%d" % (1 + cg)])
                        for kc in range(8):
                            op("pe", lambda e, kc=kc: e.matmul(bank(7)[0:8, 0:128], lhsT=winb[:, kc, 3072:3080], rhs=hT[i][:, kc, :],
                                                               start=(kc == 0), stop=(kc == 7)),
                               reads=[("hT", i), "winb"], writes=["bk7"])

                    def a_evac(t):
                        i = t % 2
                        c, tc = t // 4, t % 4
                        ci = c % 2
                        op("act", lambda e: e.activation(out=vds[i][:], in_=bank(3), func=AF.Copy), reads=["bk3"], writes=[("vds", i)])
                        op("pool", lambda e: e.dma_start(out=VDo[t * 128:(t + 1) * 128, :], in_=vds[i][:]), reads=[("vds", i)], writes=[("VDo", t)], dma=("vds", i, t // 8))
                        op("act", lambda e: e.activation(out=vfs[i][:, :, 0:64], in_=bank(6).rearrange("p (h d) -> p h d", h=8), func=AF.Copy),
                           reads=["bk6"], writes=[("vfs", i)])
                        op("pool", lambda e: e.dma_start(out=VFo[t * 128:(t + 1) * 128, :], in_=vfs[i][:].rearrange("p h d -> p (h d)")),
                           reads=[("vfs", i)], writes=[("VFo", t)], dma=("vfs", i, t // 8))
                        op("act", lambda e: e.activation(out=qtok[i][:, 512:1024], in_=bank(4), func=AF.Copy, scale=0.125),
                           reads=["bk4"], writes=[("qtokb", i)])
                        op("dve", lambda e: e.tensor_copy(out=qtok[i][:, 1536:2048], in_=bank(5)), reads=["bk5"], writes=[("qtokd", i)])
                        for which, bk, scale, col0 in ((0, 1, 0.125, 0), (1, 2, 1.0, 1024)):
                            tmp = q32[which]
                            dst = qtok[i][:, col0:col0 + 512].rearrange("p (m d) -> p m d", m=8)
                            bkv = bank(bk).rearrange("p (m d) -> p m d", m=8)
                            cs = cosT[:, t, :].unsqueeze(1).to_broadcast([128, 8, 8])
                            sn = sinT[:, t, :].unsqueeze(1).to_broadcast([128, 8, 8])
                            wk = ("qtoka", i) if which == 0 else ("qtokc", i)
                            op("act", lambda e, dst=dst, bkv=bkv, scale=scale: e.activation(out=dst, in_=bkv, func=AF.Copy, scale=scale),
                               reads=["bk%d" % bk], writes=[wk])
                            op("act", lambda e, tmp=tmp, bkv=bkv, scale=scale: e.activation(out=tmp[:, :, 0:16], in_=bkv[:, :, 0:16], func=AF.Copy, scale=scale),
                               reads=["bk%d" % bk], writes=[("q32", which)])
                            op("dve", lambda e, tmp=tmp, cs=cs, which=which: e.tensor_tensor(out=ra[which][:], in0=tmp[:, :, 0:8], in1=cs, op=ALU.mult),
                               reads=[("q32", which)], writes=[("ra", which)])
                            op("dve", lambda e, tmp=tmp, sn=sn, which=which: e.tensor_tensor(out=rb[which][:], in0=tmp[:, :, 8:16], in1=sn, op=ALU.mult),
                               reads=[("q32", which)], writes=[("rb", which)])
                            op("dve", lambda e, dst=dst, which=which: e.tensor_tensor(out=dst[:, :, 0:8], in0=ra[which][:], in1=rb[which][:], op=ALU.subtract),
                               reads=[("ra", which), ("rb", which)], writes=[wk])
                            op("dve", lambda e, tmp=tmp, cs=cs, which=which: e.tensor_tensor(out=ra[which][:], in0=tmp[:, :, 8:16], in1=cs, op=ALU.mult),
                               reads=[("q32", which)], writes=[("ra", which)])
                            op("dve", lambda e, tmp=tmp, sn=sn, which=which: e.tensor_tensor(out=rb[which][:], in0=tmp[:, :, 0:8], in1=sn, op=ALU.mult),
                               reads=[("q32", which)], writes=[("rb", which)])
                            op("dve", lambda e, dst=dst, which=which: e.tensor_tensor(out=dst[:, :, 8:16], in0=ra[which][:], in1=rb[which][:], op=ALU.add),
                               reads=[("ra", which), ("rb", which)], writes=[wk])
                        op("act", lambda e: e.activation(out=etmp[:], in_=bank(7)[0:8, 0:128], func=AF.Exp, scale=-1.0, bias=negb[:]),
                           reads=["bk7", "negb"], writes=["etmp"])
                        op("act", lambda e: e.activation(out=lst[ci][:, tc * 128:(tc + 1) * 128], in_=etmp[:], func=AF.Ln, scale=1.0, bias=c_one[0:8, :]),
                           reads=["etmp", "c_one"], writes=[("lst", ci)])

                    def a_trqk(t):
                        i = t % 2
                        c, tc = t // 4, t % 4
                        ci = c % 2
                        qk_keys = [("qtoka", i), ("qtokb", i), ("qtokc", i), ("qtokd", i)]
                        for which, bk, stg in ((0, 0, qTs), (1, 7, kTs)):
                            tv = bank(bk).bitcast(BF16).rearrange("p (a b) -> p a b", a=8)
                            for pr in range(8):
                                op("pe", lambda e, tv=tv, pr=pr, which=which: e.transpose(tv[:, pr, :], qtok[i][:, which * 1024 + pr * 128: which * 1024 + (pr + 1) * 128], ident[:]),
                                   reads=qk_keys + ["ident"], writes=["bk%d" % bk])
                            op("dve" if which == 0 else "act",
                               (lambda e, tv=tv, stg=stg: e.tensor_copy(out=stg[ci][:, :, tc * 128:(tc + 1) * 128], in_=tv)) if which == 0 else
                               (lambda e, tv=tv, stg=stg: e.activation(out=stg[ci][:, :, tc * 128:(tc + 1) * 128], in_=tv, func=AF.Copy)),
                               reads=["bk%d" % bk], writes=[("stg", which, ci)])
                        if tc == 3:
                            op("pool", lambda e: e.dma_start(out=QT[:, :, c * 512:(c + 1) * 512].rearrange("a p t -> p a t"), in_=qTs[ci][:]),
                               reads=[("stg", 0, ci)], dma=("stg", 0, ci))
                            op("pool", lambda e: e.dma_start(out=KTo[c // 2, :, (c % 2) * 512:(c % 2 + 1) * 512].rearrange("(a p) t -> p a t", p=128), in_=kTs[ci][:]),
                               reads=[("stg", 1, ci)], writes=[("KTo", c)], dma=("stg", 1, ci, c // 2))
                            op("pool", lambda e: e.dma_start(out=LTo[:, c * 512:(c + 1) * 512], in_=lst[ci][:]),
                               reads=[("lst", ci)], writes=[("LTo", c)], dma=("lst", ci))
                            if c % 2 == 1:
                                pc = c // 2
                                op("pool", lambda e: e.collective_compute("AllGather", ALU.bypass, replica_groups=GRP,
                                                                          ins=[KTo[pc, :, :]], outs=[KTa[pc, :, :]]),
                                   reads=[("KTo", 2 * pc), ("KTo", 2 * pc + 1)], dma=("ag", 0, pc), kind="cc")
                                op("pool", lambda e: e.collective_compute("AllGather", ALU.bypass, replica_groups=GRP,
                                                                          ins=[VDo[pc * 1024:(pc + 1) * 1024, :]], outs=[VDa[pc, :, :]]),
                                   reads=[("VDo", tt) for tt in range(8 * pc, 8 * pc + 8)], dma=("ag", 1, pc), kind="cc")
                                op("pool", lambda e: e.collective_compute("AllGather", ALU.bypass, replica_groups=GRP,
                                                                          ins=[VFo[pc * 1024:(pc + 1) * 1024, :]], outs=[VFa[pc, :, :]]),
                                   reads=[("VFo", tt) for tt in range(8 * pc, 8 * pc + 8)], dma=("ag", 2, pc), kind="cc")
                            if c == NCH - 1:
                                op("pool", lambda e: e.collective_compute("AllGather", ALU.bypass, replica_groups=GRP, ins=[LTo[:, :]], outs=[LTa[:, :]]),
                                   reads=[("LTo", cc_) for cc_ in range(NCH)], dma=("ag", 3, 0), kind="cc")

                    load_x(0)
                    load_x(1)
                    a_norm(0)
                    for t in range(NT):
                        a_proj(t)
                        if t + 1 < NT:
                            a_norm(t + 1)
                        if t + 2 < NT:
                            load_x(t + 2)
                        a_evac(t)
                        if t >= 1:
                            a_trqk(t - 1)
                    a_trqk(NT - 1)
                    P.barrier()

                if debug and l == nlayers - 1:
                    op("sp", lambda e: e.dma_start(out=dbg["d_cos"].ap(), in_=cosT[:]), reads=["cosT"], dma="dbg")
                    op("sp", lambda e: e.dma_start(out=dbg["d_sin"].ap(), in_=sinT[:]), reads=["sinT"], dma="dbg")
                    P.barrier()
                if stop_after == 'A' and l == nlayers - 1:
                    return
                if stop_after == 'AG' and l == nlayers - 1:
                    return
                with ExitStack() as st:
                    cs_ = sb("cs", [8, S], F32, st)
                    cn = sb("cn", [8, S], F32, st)
                    prt = [sb("prt%d" % i, [8, S], BF16, st) for i in range(3)]
                    r1 = cs_
                    cq = [sb("cqp%d" % i, [8, T], BF16, st) for i in range(3)]
                    cqt = sb("cqt", [8, 16, 128], BF16, st)
                    gmap = ((0, 0, 0), (0, 1, 3), (1, 0, 1), (1, 1, 2))
                    csv = cs_[:].rearrange("h (j q p) -> h j q p", q=4, p=128)
                    for (rk, e_, q_) in gmap:
                        op("sp", lambda e, rk=rk, e_=e_, q_=q_: e.dma_start(
                            out=csv[:, :, q_, :], in_=LTa[rk * 8:(rk + 1) * 8, :].rearrange("h (j e p) -> h j e p", e=2, p=128)[:, :, e_, :]),
                           writes=["cs"], dma="cs")
                    op("dve", lambda e: e.tensor_tensor_scan(out=cn[:], data0=ones32[0:8, 0:1].to_broadcast([8, S]), data1=cs_[:], initial=0.0, op0=ALU.mult, op1=ALU.add),
                       reads=["cs", "ones32"], writes=["cn"])
                    op("dve", lambda e: e.tensor_copy(out=prt[0][:], in_=cn[:]), reads=["cn"], writes=["p0"])
                    op("dve", lambda e: e.tensor_tensor(out=r1[:], in0=cn[:], in1=prt[0][:], op=ALU.subtract), reads=["cn", "p0"], writes=["cs"])
                    op("dve", lambda e: e.tensor_copy(out=prt[1][:], in_=r1[:]), reads=["cs"], writes=["p1"])
                    op("dve", lambda e: e.tensor_tensor(out=cn[:], in0=r1[:], in1=prt[1][:], op=ALU.subtract), reads=["cs", "p1"], writes=["cn"])
                    op("dve", lambda e: e.tensor_copy(out=prt[2][:], in_=cn[:]), reads=["cn"], writes=["p2"])
                    for pi in range(3):
                        pv = prt[pi][:].rearrange("h (j q p) -> h j q p", q=4, p=128)
                        for (rk, e_, q_) in gmap:
                            op("sp", lambda e, pi=pi, pv=pv, rk=rk, e_=e_, q_=q_: e.dma_start(
                                out=CK[:, pi, rk * T:(rk + 1) * T].rearrange("h (j e p) -> h j e p", e=2, p=128)[:, :, e_, :], in_=pv[:, :, q_, :]),
                               reads=["p%d" % pi], dma="ckst")
                        cqv = cq[pi][:].rearrange("h (j e p) -> h j e p", e=2, p=128)
                        for e_ in range(2):
                            q0 = 0 if e_ == 0 else 3
                            q1 = 1 if e_ == 0 else 2
                            op("dve", lambda e, pv=pv, q0=q0: e.tensor_scalar(out=cqt[:], in0=pv[:, :, q0, :], scalar1=sel[:, 0:1], scalar2=None, op0=ALU.mult),
                               reads=["p%d" % pi, "sel"], writes=["cqt"])
                            op("dve", lambda e, pv=pv, q1=q1, cqv=cqv, e_=e_: e.scalar_tensor_tensor(out=cqv[:, :, e_, :], in0=pv[:, :, q1, :], scalar=sel[:, 1:2], in1=cqt[:],
                                                                                                     op0=ALU.mult, op1=ALU.add),
                               reads=["p%d" % pi, "sel", "cqt"], writes=[("cq", pi)])
                        op("sp", lambda e, pi=pi: e.dma_start(out=CQ[:, pi, :], in_=cq[pi][:]), reads=[("cq", pi)], dma="ckst")
                    if debug and l == nlayers - 1:
                        for nm, t_ in (("d_QT", QT), ("d_KTa", KTa), ("d_VDa", VDa), ("d_VFa", VFa), ("d_LTa", LTa)):
                            op("sp", lambda e, nm=nm, t_=t_: e.dma_start(out=dbg[nm].ap(), in_=t_.ap()), dma="dbg")
                    P.barrier()
                    if debug and l == nlayers - 1:
                        for nm, t_ in (("d_CK", CK), ("d_CQ", CQ)):
                            op("sp", lambda e, nm=nm, t_=t_: e.dma_start(out=dbg[nm].ap(), in_=t_.ap()), dma="dbg")
                        P.barrier()

                if stop_after == 'S' and l == nlayers - 1:
                    return
                with ExitStack() as st:
                    KP = [sb("KP%d" % i, [128, 2 * T], BF16, st) for i in range(4)]
                    QP = [sb("QP%d" % i, [128, T], BF16, st) for i in range(4)]
                    VV = [sb("VV%d" % i, [128, 64, 128], BF16, st) for i in range(2)]
                    PT = [sb("PT%d" % i, [128, 512], BF16, st) for i in range(8)]
                    masks = sb("masks_sb", [128, 8, 512], BF16, st)
                    op("sp", lambda e: e.dma_start(out=masks[:], in_=masks_in[:, :, :]), writes=["masks"], dma="c0_0")
                    lamv = sb("lamv", [1, 4, 64], F32, st)
                    lamt = sb("lamt", [1, 64], F32, st)
                    lams = sb("lams", [1, 4], F32, st)
                    nlam = sb("nlam", [1, 1], F32, st)
                    nlam128 = sb("nlam128", [128, 1], F32, st)
                    lacc = [sb("lacc%d" % i, [128, 512], F32, st) for i in range(2)]
                    rrf = [sb("rrf%d" % i, [128, 512], F32, st) for i in range(2)]
                    gsub = sb("gsub", [128, 1], F32, st)
                    rr = [sb("rr%d" % i, [128, 512], F32, st) for i in range(2)]
                    bcs = [sb("bcs%d" % i, [128, 512], F32, st) for i in range(2)]
                    Dt = sb("Dt", [128, 512], F32, st)
                    D2 = sb("D2", [128, 512], F32, st)
                    mo = [sb("mo%d" % i, [128, 512], BF16, st) for i in range(2)]
                    PS = [alloc(lambda nm: nc.psum_tensor(nm, [128, 1024], F32), "psB%d_%d" % (l, i), st) for i in range(4)]

                    def bank(b):
                        return PS[b // 2][:, (b % 2) * 512:(b % 2 + 1) * 512]
                    SB = (0, 1, 2)

                    for i_, tns in enumerate((lam_q1, lam_k1, lam_q2, lam_k2)):
                        op("sp", lambda e, i_=i_, tns=tns: e.dma_start(out=lamv[:, i_, :], in_=tns[l:l + 1, :]), writes=["lamv"], dma="lam")
                    op("sp", lambda e: e.dma_start(out=gsub[:], in_=diff_sub_g[l].rearrange("(p o) -> p o", o=1)), writes=["gsub"], dma="lam_g")
                    op("dve", lambda e: e.tensor_scalar(out=gsub[:], in0=gsub[:], scalar1=(1.0 - lam_init), scalar2=None, op0=ALU.mult),
                       reads=["gsub"], writes=["gsub"])
                    for j_ in range(2):
                        op("dve", lambda e, j_=j_: e.tensor_tensor(out=lamt[:], in0=lamv[:, 2 * j_, :], in1=lamv[:, 2 * j_ + 1, :], op=ALU.mult),
                           reads=["lamv"], writes=["lamt"])
                        op("act", lambda e, j_=j_: e.activation(out=lamt[:], in_=lamt[:], func=AF.Copy, accum_out=lams[:, j_:j_ + 1]),
                           reads=["lamt"], writes=["lamt", "lams"])
                    op("act", lambda e: e.activation(out=lams[:, 2:4], in_=lams[:, 0:2], func=AF.Exp), reads=["lams"], writes=["lams"])
                    op("dve", lambda e: e.tensor_tensor(out=nlam[:], in0=lams[:, 3:4], in1=lams[:, 2:3], op=ALU.subtract), reads=["lams"], writes=["nlam"])
                    op("dve", lambda e: e.tensor_scalar(out=nlam[:], in0=nlam[:], scalar1=-lam_init, scalar2=None, op0=ALU.add), reads=["nlam"], writes=["nlam"])

                    def load_map(mi):
                        i = mpos[mi] % 4
                        pr, hf = mi // 2, mi % 2
                        r0 = 64 if (mi < 8 and hf == 1) else 0
                        for pc in range(4):
                            op("sp", lambda e, pc=pc: e.dma_start(out=KP[i][r0:r0 + 64, :].rearrange("p (r c t) -> p r c t", r=2, c=4)[:, :, pc, :],
                                                                  in_=KTa[pc, :, :].rearrange("(r q) t -> q r t", r=2)[pr * 128 + hf * 64:pr * 128 + (hf + 1) * 64, :, :]),
                               writes=[("KP", i, 0, pc)] + ([("KP", i, 1), ("KP", i, 2)] if (r0 and pc == 3) else []), dma=("KP", i))
                        op("sp", lambda e: e.dma_start(out=QP[i][r0:r0 + 64, :], in_=QT[pr, hf * 64:(hf + 1) * 64, :]),
                           writes=[("QP", i, 0)] + ([("QP", i, 1), ("QP", i, 2)] if r0 else []), dma=("QP", i))
                        if mi >= 8:
                            h = mi - 8
                            op("dve", lambda e: e.memset(KP[i][64:70, :], 1.0), writes=[("KP", i, 1), ("KP", i, 2)])
                            op("dve", lambda e: e.memset(QP[i][64:70, :], 1.0), writes=[("QP", i, 1), ("QP", i, 2)])
                            op("sp", lambda e: e.dma_start(out=KP[i][64:67, :], in_=CK[h, :, :]), reads=[("KP", i, 2)], writes=[("KP", i, 1)], dma=("KP", i))
                            op("sp", lambda e: e.dma_start(out=QP[i][67:70, :], in_=CQ[h, :, :]), reads=[("QP", i, 2)], writes=[("QP", i, 1)], dma=("QP", i))

                    def load_v(vi, diff, h):
                        i = vi % 2
                        for X in range(2):
                            for c_ in range(4):
                                b0 = X * 32 + c_ * 8
                                if diff:
                                    op("sp", lambda e, X=X, c_=c_, b0=b0: e.dma_start(
                                        out=VV[i][:, b0:b0 + 8, :],
                                        in_=VDa[c_, X * 1024:(X + 1) * 1024, h * 128:(h + 1) * 128].rearrange("(b p) d -> p b d", p=128)),
                                       writes=[("VV", i, b0)], dma=("VV", i))
                                else:
                                    op("sp", lambda e, X=X, c_=c_, b0=b0: e.dma_start(
                                        out=VV[i][:, b0:b0 + 8, 0:65],
                                        in_=VFa[c_, X * 1024:(X + 1) * 1024, h * 65:(h + 1) * 65].rearrange("(b p) d -> p b d", p=128)),
                                       writes=[("VV", i, b0)], dma=("VV", i))

                    units = [("d", h) for h in range(4)] + [("f", h) for h in range(8)]
                    map_seq = []
                    for (kind, h) in units:
                        map_seq += ([2 * h, 2 * h + 1] if kind == "d" else [8 + h])
                    mpos = {mi: k for k, mi in enumerate(map_seq)}
                    LA = 2

                    op("pe", lambda e: e.matmul(bank(7)[:, 0:1], lhsT=ones32[0:1, :], rhs=nlam[0:1, 0:1], start=True, stop=True),
                       reads=["nlam", "ones32"], writes=[("ACC", 7)])
                    op("dve", lambda e: e.tensor_copy(out=nlam128[:], in_=bank(7)[:, 0:1]), reads=[("ACC", 7)], writes=["nlam128"])
                    for i in range(2):
                        op("dve", lambda e, i=i: e.memset(VV[i][:], 0.0), writes=[("VV", i, b0) for b0 in range(0, 64, 8)])

                    deferred = []

                    def post_fox(h, chunk, j, accb, now):
                        op("dve", lambda e: e.reciprocal(out=rrf[j][64:65, :], in_=bank(accb)[64:65, :]), reads=[("ACC", accb)], writes=[("rrf", j)])

                        def later():
                            op("pe", lambda e: e.matmul(bank(7)[0:64, :], lhsT=ones32[64:65, 0:64], rhs=rrf[j][64:65, :], start=True, stop=True),
                               reads=[("rrf", j), "ones32"], writes=[("ACC", 7)])
                            op("dve", lambda e: e.tensor_copy(out=bcs[j][0:64, :], in_=bank(7)[0:64, :]), reads=[("ACC", 7)], writes=[("bcs", j)])
                            op("dve", lambda e: e.tensor_tensor(out=mo[j][0:64, :], in0=bank(accb)[0:64, :], in1=bcs[j][0:64, :], op=ALU.mult),
                               reads=[("ACC", accb), ("bcs", j)], writes=[("mo", j)])
                            op("sp", lambda e: e.dma_start(out=MT[512 + h * 64:512 + (h + 1) * 64, chunk * 512:(chunk + 1) * 512], in_=mo[j][0:64, :]),
                               reads=[("mo", j)], dma=("mo", j))
                        deferred.append((now + 4, later))

                    def post_diff(h, chunk, j, now):
                        op("dve", lambda e: e.reciprocal(out=rr[0][:], in_=bank(5)), reads=[("ACC", 5)], writes=[("rr", 0)])
                        op("dve", lambda e: e.tensor_tensor(out=Dt[:], in0=bank(3), in1=rr[0][:], op=ALU.mult), reads=[("ACC", 3), ("rr", 0)], writes=["Dt"])
                        op("dve", lambda e: e.reciprocal(out=rr[1][:], in_=bank(6)), reads=[("ACC", 6)], writes=[("rr", 1)])
                        op("dve", lambda e: e.tensor_tensor(out=D2[:], in0=bank(4), in1=rr[1][:], op=ALU.mult), reads=[("ACC", 4), ("rr", 1)], writes=["D2"])
                        op("dve", lambda e: e.scalar_tensor_tensor(out=Dt[:], in0=D2[:], scalar=nlam128[:, 0:1], in1=Dt[:], op0=ALU.mult, op1=ALU.add),
                           reads=["Dt", "D2", "nlam128"], writes=["Dt"])
                        op("dve", lambda e: e.tensor_tensor(out=D2[:], in0=Dt[:], in1=Dt[:], op=ALU.mult), reads=["Dt"], writes=["D2"])

                        def later():
                            op("pe", lambda e: e.matmul(bank(7), lhsT=ones32[:, :], rhs=D2[:], start=True, stop=True),
                               reads=["D2", "ones32"], writes=[("ACC", 7)])
                            op("act", lambda e: e.activation(out=rr[0][:], in_=bank(7), func=AF.Ln, scale=1.0 / 128, bias=c_eps[:]),
                               reads=[("ACC", 7), "c_eps"], writes=[("rr", 0)])
                            op("act", lambda e: e.activation(out=rr[0][:], in_=rr[0][:], func=AF.Exp, scale=-0.5), reads=[("rr", 0)], writes=[("rr", 0)])
                            op("dve", lambda e: e.scalar_tensor_tensor(out=mo[j][:], in0=Dt[:], scalar=gsub[:, 0:1], in1=rr[0][:], op0=ALU.mult, op1=ALU.mult),
                               reads=["Dt", "gsub", ("rr", 0)], writes=[("mo", j)])
                            op("sp", lambda e: e.dma_start(out=MT[h * 128:(h + 1) * 128, chunk * 512:(chunk + 1) * 512], in_=mo[j][:]),
                               reads=[("mo", j)], dma=("mo", j))
                        deferred.append((now + 8, later))

                    stream = []
                    for ui_, (kind, h) in enumerate(units):
                        my_maps = [2 * h, 2 * h + 1] if kind == "d" else [8 + h]
                        for chunk in range(NCH):
                            allu = [(X, kb) for X in range(2) for kb in range(4 * chunk + 4)]
                            full = [(X, kb) for (X, kb) in allu if kb - 4 * chunk <= 0]
                            part = [(X, kb) for (X, kb) in allu if kb - 4 * chunk > 0]
                            ul = full[:-1] + part + full[-1:]
                            for uj, (X, kb) in enumerate(ul):
                                for mj, mi in enumerate(my_maps):
                                    stream.append(dict(ui=ui_, kind=kind, h=h, mi=mi, mj=mj, chunk=chunk, X=X, kb=kb, first=(uj == 0), last=(uj == len(ul) - 1),
                                                       fin=(uj == len(ul) - 1 and mj == len(my_maps) - 1), c0=max(0, kb - 4 * chunk) * 128))
                    loaded_maps = 0
                    last_pv = [None]
                    loaded_v = 0
                    nS = len(stream)
                    pcount = 0
                    for idx in range(nS + LA):
                        if idx < nS:
                            u = stream[idx]
                            last_mi = (2 * u["h"] + 1) if u["kind"] == "d" else u["mi"]
                            want_m = min(len(map_seq), mpos[last_mi] + 3)
                            while loaded_maps < want_m:
                                load_map(map_seq[loaded_maps])
                                loaded_maps += 1
                            if loaded_v == 0:
                                load_v(0, units[0][0] == "d", units[0][1])
                                loaded_v = 1
                            def emit_qk(j_):
                                uu = stream[j_]
                                sl = mpos[uu["mi"]] % 4
                                s_ = j_ % 3
                                chunk, X, kb = uu["chunk"], uu["X"], uu["kb"]
                                if uu["kind"] == "d":
                                    r0 = 64 * uu["mj"]
                                    r1 = r0 + 64
                                else:
                                    r0, r1 = 0, 70
                                qv = QP[sl][r0:r1, chunk * 512:(chunk + 1) * 512]
                                kcol = X * T + kb * 128
                                m = kb - 4 * chunk
                                kq_reads = [("KP", sl, 0, pc_) for pc_ in range(4)] + [("KP", sl, 1), ("KP", sl, 2), ("QP", sl, 0), ("QP", sl, 1), ("QP", sl, 2)]
                                c0 = uu["c0"]
                                op("pe", lambda e: e.matmul(bank(s_)[:, c0:512], lhsT=KP[sl][r0:r1, kcol:kcol + 128], rhs=qv[:, c0:512], start=True, stop=(m < 0)),
                                   reads=kq_reads, writes=[("S", s_)])

                            def emit_mask_exp(j_):
                                uu = stream[j_]
                                s_ = j_ % 3
                                p_ = j_ % 8
                                chunk, X, kb = uu["chunk"], uu["X"], uu["kb"]
                                m = kb - 4 * chunk
                                c0 = uu["c0"]
                                if m >= 0:
                                    op("pe", lambda e: e.matmul(bank(s_)[:, c0:c0 + 128], lhsT=ident[:], rhs=masks[:, X * 4 + m, c0:c0 + 128], start=False, stop=True),
                                       reads=["ident", "masks"], writes=[("S", s_)])
                                op("act", lambda e: e.activation(out=PT[p_][:, c0:512], in_=bank(s_)[:, c0:512], func=AF.Exp), reads=[("S", s_)], writes=[("PT", p_)])

                            if u["kind"] == "d":
                                if u["mj"] == 0:
                                    emit_qk(idx)
                                    emit_qk(idx + 1)
                                    emit_mask_exp(idx)
                                    emit_mask_exp(idx + 1)
                            else:
                                emit_qk(idx)
                                emit_mask_exp(idx)
                        if idx == 100:
                            conv_rest(l, rd=[last_pv[0]])
                            if l + 1 < nlayers:
                                conv_in(l + 1, rd=[last_pv[0]])
                        if idx >= LA:
                            u = stream[idx - LA]
                            s_ = (idx - LA) % 3
                            p_ = (idx - LA) % 8
                            want_v = min(len(units), u["ui"] + 2)
                            while loaded_v < want_v:
                                load_v(loaded_v, units[loaded_v][0] == "d", units[loaded_v][1])
                                loaded_v += 1
                            vi = u["ui"] % 2
                            vb = u["X"] * 32 + u["kb"]
                            vkey = [("VV", vi, b0_) for b0_ in range(0, 64, 8)]
                            acc_bank = (3 + u["mj"]) if u["kind"] == "d" else (3 + u["chunk"] % 4)
                            first, last = u["first"], u["last"]
                            c0 = u["c0"]
                            last_pv[0] = op("pe", lambda e, p_=p_, vb=vb, vi=vi, acc_bank=acc_bank, first=first, last=last, c0=c0: e.matmul(
                                bank(acc_bank)[:, c0:512], lhsT=VV[vi][:, vb, :], rhs=PT[p_][:, c0:512], start=first, stop=last),
                               reads=[("PT", p_)] + vkey, writes=[("ACC", acc_bank)])
                            if u["kind"] == "d":
                                l_bank = 5 + u["mj"]
                                mj = u["mj"]
                                if mj == 0:
                                    op("pe", lambda e, p_=p_, l_bank=l_bank, first=first, last=last, c0=c0: e.matmul(bank(l_bank)[:, c0:512], lhsT=onesb[:, :], rhs=PT[p_][:, c0:512],
                                                                                                                  start=first, stop=last),
                                       reads=[("PT", p_), "onesb"], writes=[("ACC", l_bank)])
                                elif first:
                                    op("dve", lambda e, p_=p_, mj=mj: e.tensor_copy(out=lacc[mj][:], in_=PT[p_][:]), reads=[("PT", p_)], writes=[("lacc", mj)])
                                else:
                                    op("dve", lambda e, p_=p_, mj=mj, c0=c0: e.tensor_tensor(out=lacc[mj][:, c0:512], in0=lacc[mj][:, c0:512], in1=PT[p_][:, c0:512], op=ALU.add),
                                       reads=[("PT", p_), ("lacc", mj)], writes=[("lacc", mj)])
                                if last and mj == 1:
                                    op("pe", lambda e, l_bank=l_bank, mj=mj: e.matmul(bank(l_bank), lhsT=ones32[:, :], rhs=lacc[mj][:], start=True, stop=True),
                                       reads=[("lacc", mj), "ones32"], writes=[("ACC", l_bank)])
                            if u["fin"]:
                                j = pcount % 2
                                pcount += 1
                                if u["kind"] == "f":
                                    post_fox(u["h"], u["chunk"], j, acc_bank, idx)
                                else:
                                    post_diff(u["h"], u["chunk"], j, idx)
                        while deferred and (deferred[0][0] <= idx or idx == nS + LA - 1):
                            deferred.pop(0)[1]()
                    if debug and l == nlayers - 1:
                        P.barrier()
                        op("sp", lambda e: e.dma_start(out=dbg["d_MT"].ap(), in_=MT.ap()), dma="dbg")
                    P.barrier()

                if stop_after == 'B' and l == nlayers - 1:
                    return
                with ExitStack() as st:
                    wout = sb("wout", [128, 8, D], BF16, st)
                    wdn = sb("wdn", [128, NFC, D], BF16, st)
                    gpost = sb("gpost", [128, D], F32, st)
                    gfpre = sb("gfpre", [128, D], F32, st)
                    gfpost = sb("gfpost", [128, D], F32, st)
                    mt = [sb("mt%d" % i, [128, 8, 512], BF16, st) for i in range(2)]
                    xt = [sb("xtc%d" % i, [128, D], F32, st) for i in range(2)]
                    x1k = [sb("x1k%d" % i, [128, D], F32, st) for i in range(8)]
                    junk = sb("junkc", [128, D], BF16, st)
                    ss = [sb("ssc%d" % i, [128, 1], F32, st) for i in range(4)]
                    rstd = [sb("rstdc%d" % i, [128, 1], F32, st) for i in range(4)]
                    hb = [sb("hbc%d" % i, [128, D], BF16, st) for i in range(2)]
                    hT = [sb("hTc%d" % i, [128, 8, 512], BF16, st) for i in range(2)]
                    AT = sb("AT", [128, NFC, 512], BF16, st)
                    wgp = [sb("wgp%d" % i, [128, 8, 256], BF16, st) for i in range(2)]
                    wup = [sb("wup%d" % i, [128, 8, 256], BF16, st) for i in range(2)]
                    sg = [sb("sg%d" % i, [128, 512], BF16, st) for i in range(2)]
                    PS = [alloc(lambda nm: nc.psum_tensor(nm, [128, 1024], F32), "psC%d_%d" % (l, i), st) for i in range(4)]

                    def bank(b):
                        return PS[b // 2][:, (b % 2) * 512:(b % 2 + 1) * 512]
                    op("sp", lambda e: e.dma_start(out=wout[:], in_=wout_b[l, :, :, :]), reads=[("cvres", "wout", l, kc) for kc in range(8)], writes=["wout"], dma="wc_wout")
                    op("sp", lambda e: e.dma_start(out=wdn[:], in_=wd_b[l, :, :, :]), reads=[("cvres", "wd", l, fc) for fc in range(NFC)], writes=["wdn"], dma="wc_wdn")
                    op("sp", lambda e: e.dma_start(out=gpost[:], in_=attn_post_g[l].partition_broadcast(128)), writes=["gpost"], dma="wc_gpost")
                    op("sp", lambda e: e.dma_start(out=gfpre[:], in_=ffn_pre_g[l].partition_broadcast(128)), writes=["gfpre"], dma="wc_gfpre")
                    op("sp", lambda e: e.dma_start(out=gfpost[:], in_=ffn_post_g[l].partition_broadcast(128)), writes=["gfpost"], dma="wc_gfpost")

                    def rms(src_ap, src_reads, i):
                        op("act", lambda e: e.activation(out=junk[:], in_=src_ap, func=AF.Square, accum_out=ss[i][:]), reads=src_reads, writes=[("ss", i)])
                        op("act", lambda e: e.activation(out=rstd[i][:], in_=ss[i][:], func=AF.Ln, scale=1.0 / D, bias=c_eps[:]),
                           reads=[("ss", i), "c_eps"], writes=[("rstd", i)])
                        op("act", lambda e: e.activation(out=rstd[i][:], in_=rstd[i][:], func=AF.Exp, scale=-0.5), reads=[("rstd", i)], writes=[("rstd", i)])

                    def load_mt(c):
                        op("sp", lambda e: e.dma_start(out=mt[c % 2][:], in_=MT[:, c * 512:(c + 1) * 512].rearrange("(a p) t -> p a t", p=128)),
                           writes=[("mt", c % 2)], dma=("mt", c % 2))

                    def s1_outproj(c, t):
                        tg = 4 * c + t
                        i = t % 2
                        op("sp", lambda e: e.dma_start(out=xt[i][:], in_=xsrc[tg * 128:(tg + 1) * 128, :]), writes=[("xt", i)], dma=("xt", i))
                        for half in range(2):
                            for kc in range(8):
                                op("pe", lambda e, half=half, kc=kc: e.matmul(bank(half), lhsT=mt[c % 2][:, kc, t * 128:(t + 1) * 128],
                                                                             rhs=wout[:, kc, half * 512:(half + 1) * 512], start=(kc == 0), stop=(kc == 7)),
                                   reads=[("mt", c % 2), "wout"], writes=[("bk", half)])

                    def s1_chain(c, t):
                        i = t % 2
                        xk = x1k[(c % 2) * 4 + t]
                        xkey = ("x1k", (c % 2) * 4 + t)
                        rms(PS[0][:], [("bk", 0), ("bk", 1)], 0)
                        op("dve", lambda e: e.scalar_tensor_tensor(out=xk[:], in0=PS[0][:], scalar=rstd[0][:], in1=gpost[:], op0=ALU.mult, op1=ALU.mult),
                           reads=[("bk", 0), ("bk", 1), ("rstd", 0), "gpost"], writes=[xkey])
                        op("dve", lambda e: e.tensor_tensor(out=xk[:], in0=xk[:], in1=xt[i][:], op=ALU.add), reads=[xkey, ("xt", i)], writes=[xkey])
                        rms(xk[:], [xkey], 1)
                        op("dve", lambda e: e.scalar_tensor_tensor(out=hb[i][:], in0=xk[:], scalar=rstd[1][:], in1=gfpre[:], op0=ALU.mult, op1=ALU.mult),
                           reads=[xkey, ("rstd", 1), "gfpre"], writes=[("hb", i)])

                    def s1_transpose(c, t):
                        i = t % 2
                        trv = bank(4).bitcast(BF16).rearrange("p (a b) -> p a b", a=8)
                        for kc in range(8):
                            op("pe", lambda e, kc=kc: e.transpose(trv[:, kc, :], hb[i][:, kc * 128:(kc + 1) * 128], ident[:]),
                               reads=[("hb", i), "ident"], writes=[("bk", 4)])
                        op("act", lambda e: e.activation(out=hT[c % 2][:, :, t * 128:(t + 1) * 128], in_=trv, func=AF.Copy), reads=[("bk", 4)], writes=[("hT", c % 2)])

                    def s2_piece(c, g):
                        gi = g % 2
                        op("sp", lambda e: e.dma_start(out=wgp[gi][:], in_=wg_b[l, :, :, g * 256:(g + 1) * 256]), reads=[("cvres", "wg", l, kc) for kc in range(8)], writes=[("wgp", gi)], dma=("wgp", gi))
                        op("sp", lambda e: e.dma_start(out=wup[gi][:], in_=wu_b[l, :, :, g * 256:(g + 1) * 256]), reads=[("cvres", "wu", l, kc) for kc in range(8)], writes=[("wup", gi)], dma=("wup", gi))
                        for fcl in range(2):
                            fc = 2 * g + fcl
                            gb, ub = ((5, 6) if fc % 2 == 0 else (7, 4))
                            for kc in range(8):
                                op("pe", lambda e, gb=gb, kc=kc, fcl=fcl: e.matmul(bank(gb), lhsT=wgp[gi][:, kc, fcl * 128:(fcl + 1) * 128], rhs=hT[c % 2][:, kc, :],
                                                                                   start=(kc == 0), stop=(kc == 7)),
                                   reads=[("wgp", gi), ("hT", c % 2)], writes=[("bk", gb)])
                            for kc in range(8):
                                op("pe", lambda e, ub=ub, kc=kc, fcl=fcl: e.matmul(bank(ub), lhsT=wup[gi][:, kc, fcl * 128:(fcl + 1) * 128], rhs=hT[c % 2][:, kc, :],
                                                                                   start=(kc == 0), stop=(kc == 7)),
                                   reads=[("wup", gi), ("hT", c % 2)], writes=[("bk", ub)])
                            sgi = fc % 2
                            op("act", lambda e, gb=gb, sgi=sgi: e.activation(out=sg[sgi][:], in_=bank(gb), func=AF.Silu), reads=[("bk", gb)], writes=[("sg", sgi)])
                            op("dve", lambda e, ub=ub, sgi=sgi, fc=fc: e.tensor_tensor(out=AT[:, fc, :], in0=bank(ub), in1=sg[sgi][:], op=ALU.mult),
                               reads=[("bk", ub), ("sg", sgi)], writes=["AT"])

                    def s3_mm(c, t):
                        yb = 2 * (t % 2)
                        for half in range(2):
                            for fc in range(NFC):
                                op("pe", lambda e, half=half, fc=fc: e.matmul(bank(yb + half), lhsT=AT[:, fc, t * 128:(t + 1) * 128],
                                                                             rhs=wdn[:, fc, half * 512:(half + 1) * 512], start=(fc == 0), stop=(fc == NFC - 1)),
                                   reads=["AT", "wdn"], writes=[("bk", yb + half)])

                    def s3_post(c, t):
                        tg = 4 * c + t
                        yb = 2 * (t % 2)
                        i = t % 2
                        xk = x1k[(c % 2) * 4 + t]
                        xkey = ("x1k", (c % 2) * 4 + t)
                        rms(PS[t % 2][:], [("bk", yb), ("bk", yb + 1)], 2 + i)
                        op("dve", lambda e: e.scalar_tensor_tensor(out=xt[i][:], in0=PS[t % 2][:], scalar=rstd[2 + i][:], in1=gfpost[:], op0=ALU.mult, op1=ALU.mult),
                           reads=[("bk", yb), ("bk", yb + 1), ("rstd", 2 + i), "gfpost"], writes=[("xt", i)])
                        op("dve", lambda e: e.tensor_tensor(out=xt[i][:], in0=xt[i][:], in1=xk[:], op=ALU.add), reads=[xkey, ("xt", i)], writes=[("xt", i)])
                        op("pool", lambda e: e.dma_start(out=xdst[tg * 128:(tg + 1) * 128, :], in_=xt[i][:]), reads=[("xt", i)], dma=("xo", i))

                    load_mt(0)
                    for t in range(4):
                        s1_outproj(0, t)
                        s1_chain(0, t)
                        s1_transpose(0, t)
                    for c in range(NCH):
                        nxt = c + 1 if c + 1 < NCH else None
                        if nxt is not None:
                            load_mt(nxt)
                        for g in range(11):
                            s2_piece(c, g)
                            if nxt is not None:
                                if g in (0, 2, 4, 6):
                                    tt = g // 2
                                    s1_outproj(nxt, tt)
                                    s1_chain(nxt, tt)
                                if g in (2, 4, 6, 8):
                                    s1_transpose(nxt, g // 2 - 1)
                        s3_mm(c, 0)
                        for t in range(4):
                            if t + 1 < 4:
                                s3_mm(c, t + 1)
                            s3_post(c, t)
                    P.barrier()
    P.run(gen)
    return nc


_NC_CACHE = {}


def kernel(**inputs):
    x = np.asarray(inputs["x"], dtype=np.float32)
    pos = np.asarray(inputs["positions"]).astype(np.int32)
    B = x.shape[0]
    assert B == 4 and x.shape[1] == S and x.shape[2] == D
    inv_freq = (1.0 / (np.float32(500000.0) ** (np.arange(0, 16, 2, dtype=np.float32) / np.float32(16)))).astype(np.float32)
    invf = np.ascontiguousarray(np.broadcast_to(inv_freq[None, :], (128, 8))).astype(np.float32)
    if "nc" not in _NC_CACHE:
        _NC_CACHE["nc"] = build()
    nc = _NC_CACHE["nc"]
    shared = {}
    for k in ("attn_pre_g", "w_in", "forget_bias", "lam_q1", "lam_k1", "lam_q2", "lam_k2", "diff_sub_g", "w_out",
              "attn_post_g", "ffn_pre_g", "w_gate", "w_up", "w_down", "ffn_post_g"):
        shared[k] = np.ascontiguousarray(np.asarray(inputs[k], dtype=np.float32))
    in_maps = []
    idxs = []
    for core in range(8):
        b, r = core // 2, core % 2
        blocks = own_blocks(r)
        tok = np.concatenate([np.arange(g * 128, (g + 1) * 128) for g in blocks])
        idxs.append((b, tok))
        m = dict(shared)
        m["x"] = np.ascontiguousarray(x[b][tok])
        m["pos"] = np.ascontiguousarray(pos[b][tok].reshape(NT, 128).T)
        m["masks"] = make_masks(r)
        selv = np.zeros((8, 2), np.float32)
        selv[:, r] = -1.0
        m["sel"] = selv
        m["invf"] = invf
        in_maps.append(m)
    res = run_bass_kernel_spmd(nc, in_maps, core_ids=list(range(8)))
    out = np.empty((B, S, D), np.float32)
    for core in range(8):
        b, tok = idxs[core]
        out[b][tok] = np.asarray(res.results[core]["out"], dtype=np.float32)
    return out
```

```python
import math
from contextlib import ExitStack
import numpy as np
import ml_dtypes
import concourse.bass as bass
import concourse.mybir as mybir
from concourse.bass_utils import run_bass_kernel_spmd

F32 = mybir.dt.float32
BF16 = mybir.dt.bfloat16
I32 = mybir.dt.int32
AF = mybir.ActivationFunctionType
ALU = mybir.AluOpType

D = 1024
S = 8192
T = 4096
NT = 32
NCH = 8
DEPTH = 2
DFF = 2816
NFC = 22
INW = 3080
EPS = 1e-6
NEG = -30000.0
ENG = ("pe", "act", "dve", "pool", "sp")
SAME_ENGINE_SYNC = True


class Rec:
    __slots__ = ("eng", "fn", "deps", "needs_inc", "incval", "dma", "dma_thr", "kind", "bar")

    def __init__(self, eng, fn, dma=None, kind="op"):
        self.eng = eng
        self.fn = fn
        self.deps = ()
        self.needs_inc = False
        self.incval = None
        self.dma = dma
        self.dma_thr = None
        self.kind = kind
        self.bar = None


class Prog:
    def __init__(self, nc):
        self.nc = nc
        self.recs = []
        self.streams = {e: [] for e in ENG}
        self.lastw = {}
        self.readers = {}
        self.dma_cum = {}
        self.dma_keys = []
        self.mode = "dry"
        self.seq = 0
        self.cur = None
        self.eng = None
        self.known = {}
        self.esem = None
        self.dsem = None

    def _wait(self, sem, key, val):
        if self.known.get(key, 0) >= val:
            return
        self.known[key] = val
        self.eng.wait_ge(sem, val)

    def op(self, eng, fn, reads=(), writes=(), dma=None, kind="op", extra=()):
        if self.mode != "dry":
            rec = self.recs[self.seq]
            self.seq += 1
            assert rec.eng == eng and rec.kind == kind
            if eng != self.cur:
                return rec
            for d in rec.deps:
                if d.dma is not None:
                    self._wait(self.dsem[d.dma], ("d", d.dma), d.dma_thr)
                else:
                    if d.eng == eng and (eng == "pe" or not SAME_ENGINE_SYNC):
                        continue
                    self._wait(self.esem[d.eng], ("e", d.eng), d.incval)
            ins = fn(self.eng)
            if rec.dma is not None:
                if rec.kind == "cc":
                    ins.then_inc(self.dsem[rec.dma])
                else:
                    ins.then_inc(self.dsem[rec.dma], 16)
            elif rec.needs_inc:
                ins.then_inc(self.esem[eng], 1)
            return rec
        rec = Rec(eng, None, dma, kind)
        deps = set()
        for r in reads:
            w = self.lastw.get(r)
            if w is not None:
                deps.add(w)
        for w_ in writes:
            w = self.lastw.get(w_)
            if w is not None:
                deps.add(w)
            for rd in self.readers.get(w_, ()):
                deps.add(rd)
        for x_ in extra:
            deps.add(x_)
        deps.discard(rec)
        rec.deps = deps
        for d in deps:
            d.needs_inc = True
        if dma is not None:
            if dma not in self.dma_cum:
                self.dma_cum[dma] = 0
                self.dma_keys.append(dma)
            self.dma_cum[dma] += (1 if kind == "cc" else 16)
            rec.dma_thr = self.dma_cum[dma]
        for r in reads:
            self.readers.setdefault(r, []).append(rec)
        for w_ in writes:
            self.lastw[w_] = rec
            self.readers[w_] = []
        self.streams[eng].append(rec)
        self.recs.append(rec)
        return rec

    def barrier(self):
        if self.mode != "dry":
            rec = self.recs[self.seq]
            self.seq += 1
            assert rec.kind == "bar"
            for e2 in ENG:
                last = rec.bar["last"][e2]
                if last is not None and e2 != self.cur:
                    self._wait(self.esem[e2], ("e", e2), last.incval)
            for k, v in rec.bar["dma"].items():
                self._wait(self.dsem[k], ("d", k), v)
            return
        snap = {"last": {}, "dma": {k: v for k, v in self.dma_cum.items() if not (isinstance(k, tuple) and k[0] == "cv")}}
        for e in ENG:
            last = None
            for r in reversed(self.streams[e]):
                if r.dma is None:
                    last = r
                    break
            if last is not None:
                last.needs_inc = True
            snap["last"][e] = last
        rec = Rec(None, None, kind="bar")
        rec.bar = snap
        self.recs.append(rec)
        self.lastw = {k: v for k, v in self.lastw.items() if isinstance(k, tuple) and k[0] == "cvres"}
        self.readers = {}

    def run(self, gen):
        nc = self.nc
        self.mode = "dry"
        gen()
        for e in ENG:
            c = 0
            for r in self.streams[e]:
                if r.dma is None and r.needs_inc:
                    c += 1
                    r.incval = c
        with ExitStack() as es:
            self.esem = {e: es.enter_context(nc.semaphore("s_" + e)) for e in ENG}
            self.dsem = {k: es.enter_context(nc.semaphore("d_%d" % i)) for i, k in enumerate(self.dma_keys)}
            block = es.enter_context(nc.Block())

            def mk(e):
                def body(eng):
                    self.mode = "emit"
                    self.seq = 0
                    self.cur = e
                    self.eng = eng
                    self.known = {}
                    gen()
                    assert self.seq == len(self.recs)
                return body

            block.tensor(mk("pe"))
            block.scalar(mk("act"))
            block.vector(mk("dve"))
            block.gpsimd(mk("pool"))
            block.sync(mk("sp"))


def own_blocks(r):
    out = []
    for j in range(16):
        if r == 0:
            out += [4 * j, 4 * j + 3]
        else:
            out += [4 * j + 1, 4 * j + 2]
    return out


def make_masks(r):
    def G(rank, idx):
        j, e = idx // 2, idx % 2
        return 4 * j + (3 * e if rank == 0 else 1 + e)
    tri = np.where(np.arange(128)[:, None] <= np.arange(128)[None, :], 0.0, NEG).astype(np.float32)
    M = np.zeros((128, 8, 512), np.float32)
    for X in range(2):
        for m in range(4):
            for n in range(4):
                kg, qg = G(X, m), G(r, n)
                if kg < qg:
                    blk = 0.0
                elif kg == qg:
                    blk = tri
                else:
                    blk = NEG
                M[:, X * 4 + m, n * 128:(n + 1) * 128] = blk
    return M.astype(ml_dtypes.bfloat16)


def build(debug=False, nlayers=DEPTH, stop_after=None):
    nc = bass.Bass("TRN2", target_bir_lowering=False)
    P = Prog(nc)

    def din(name, shape, dt=F32):
        return nc.dram_tensor(name, list(shape), dt, kind="ExternalInput")

    x_in = din("x", [T, D])
    pos_in = din("pos", [128, NT], I32)
    masks_in = din("masks", [128, 8, 512], BF16)
    sel_in = din("sel", [8, 2])
    invf_in = din("invf", [128, 8])
    attn_pre_g = din("attn_pre_g", [DEPTH, D])
    w_in = din("w_in", [DEPTH, D, INW])
    forget_bias = din("forget_bias", [DEPTH, 8])
    lam_q1 = din("lam_q1", [DEPTH, 64])
    lam_k1 = din("lam_k1", [DEPTH, 64])
    lam_q2 = din("lam_q2", [DEPTH, 64])
    lam_k2 = din("lam_k2", [DEPTH, 64])
    diff_sub_g = din("diff_sub_g", [DEPTH, 128])
    w_out = din("w_out", [DEPTH, D, D])
    attn_post_g = din("attn_post_g", [DEPTH, D])
    ffn_pre_g = din("ffn_pre_g", [DEPTH, D])
    w_gate = din("w_gate", [DEPTH, D, DFF])
    w_up = din("w_up", [DEPTH, D, DFF])
    w_down = din("w_down", [DEPTH, DFF, D])
    ffn_post_g = din("ffn_post_g", [DEPTH, D])
    out = nc.dram_tensor("out", [T, D], F32, kind="ExternalOutput")

    def scr(name, shape, dt):
        return nc.dram_tensor(name, list(shape), dt)

    win_b = scr("win_b", [DEPTH, 128, 8, INW], BF16)
    wout_b = scr("wout_b", [DEPTH, 128, 8, D], BF16)
    wg_b = scr("wg_b", [DEPTH, 128, 8, DFF], BF16)
    wu_b = scr("wu_b", [DEPTH, 128, 8, DFF], BF16)
    wd_b = scr("wd_b", [DEPTH, 128, NFC, D], BF16)
    xs = scr("xs", [T, D], F32)
    x1s = scr("x1s", [T, D], F32)
    QT = scr("QT", [8, 128, T], BF16)
    KTo = scr("KTo", [4, 1024, 1024], BF16)
    KTa = scr("KTa", [4, 2048, 1024], BF16)
    VDo = scr("VDo", [T, 512], BF16)
    VDa = scr("VDa", [4, 2048, 512], BF16)
    VFo = scr("VFo", [T, 520], BF16)
    VFa = scr("VFa", [4, 2048, 520], BF16)
    LTo = scr("LTo", [8, T], F32)
    LTa = scr("LTa", [16, T], F32)
    CK = scr("CK", [8, 3, 2 * T], BF16)
    CQ = scr("CQ", [8, 3, T], BF16)
    MT = scr("MT", [D, T], BF16)

    dbg = {}
    if debug:
        for nm, shp, dt in (("d_QT", [8, 128, T], BF16), ("d_KTa", [4, 2048, 1024], BF16), ("d_VDa", [4, 2048, 512], BF16),
                            ("d_VFa", [4, 2048, 520], BF16), ("d_LTa", [16, T], F32), ("d_CK", [8, 3, 2 * T], BF16),
                            ("d_CQ", [8, 3, T], BF16), ("d_MT", [D, T], BF16), ("d_x1", [T, D], F32), ("d_xs", [T, D], F32),
                            ("d_cos", [128, NT, 8], F32), ("d_sin", [128, NT, 8], F32), ("d_ang", [128, NT, 8], F32)):
            dbg[nm] = nc.dram_tensor(nm, shp, dt, kind="ExternalOutput")

    op = P.op
    uid = [0]

    def nuid():
        uid[0] += 1
        return uid[0]

    acache = []
    acur = [0]

    def alloc(mk, name, stack):
        if P.mode == "dry":
            t = stack.enter_context(mk("%s_%d" % (name, nuid())))
            acache.append(t)
            return t
        t = acache[acur[0]]
        acur[0] += 1
        return t

    def gen():
        acur[0] = 0
        with ExitStack() as gs:
            def sb(name, shape, dt, stack=gs):
                return alloc(lambda nm: nc.sbuf_tensor(nm, list(shape), dt), name, stack)

            ident = sb("ident", [128, 128], BF16)
            ones32 = sb("ones32", [128, 128], F32)
            onesb = sb("onesb", [128, 128], BF16)
            c_eps = sb("c_eps", [128, 1], F32)
            c_one = sb("c_one", [128, 1], F32)
            sel = sb("sel_sb", [8, 2], F32)
            posi = sb("posi", [128, NT], I32)
            posf = sb("posf", [128, NT], F32)
            invf = sb("invf_sb", [128, 8], F32)
            ang = sb("ang", [128, NT, 8], F32)
            cosT = sb("cosT", [128, NT, 8], F32)
            sinT = sb("sinT", [128, NT, 8], F32)
            c_pi = sb("c_pi", [128, 1], F32)

            op("pool", lambda e: e.memset(ones32[:], 1.0), writes=["ones32"])
            op("pool", lambda e: e.memset(onesb[:], 1.0), writes=["onesb"])
            op("pool", lambda e: e.memset(c_eps[:], EPS), writes=["c_eps"])
            op("pool", lambda e: e.memset(c_one[:], 1.0), writes=["c_one"])
            op("pool", lambda e: e.memset(c_pi[:], math.pi), writes=["c_pi"])
            op("pool", lambda e: e.memset(ident[:], 1.0), writes=["ident"])
            op("pool", lambda e: e.affine_select(out=ident[:], in_=ident[:], pattern=[[-1, 128]], compare_op=ALU.is_equal,
                                                 fill=0.0, base=0, channel_multiplier=1), reads=["ident"], writes=["ident"])
            op("sp", lambda e: e.dma_start(out=sel[:], in_=sel_in[:, :]), writes=["sel"], dma="c0_1")
            op("sp", lambda e: e.dma_start(out=posi[:], in_=pos_in[:, :]), writes=["posi"], dma="c0_2")
            op("sp", lambda e: e.dma_start(out=invf[:], in_=invf_in[:, :]), writes=["invf"], dma="c0_3")
            op("dve", lambda e: e.tensor_copy(out=posf[:], in_=posi[:]), reads=["posi"], writes=["posf"])
            op("dve", lambda e: e.tensor_tensor(out=ang[:], in0=posf[:].unsqueeze(2).to_broadcast([128, NT, 8]),
                                                in1=invf[:].unsqueeze(1).to_broadcast([128, NT, 8]), op=ALU.mult),
               reads=["posf", "invf"], writes=["ang"])
            TWO_PI = 2.0 * math.pi
            angi = sb("angi", [128, NT, 8], I32)
            kf = sb("kf", [128, NT, 8], F32)
            mk = sb("mk", [128, NT, 8], F32)

            def sin_of(dst, shift, tag):
                op("dve", lambda e: e.tensor_scalar(out=mk[:], in0=ang[:], scalar1=shift, scalar2=1.0 / TWO_PI, op0=ALU.add, op1=ALU.mult),
                   reads=["ang"], writes=["mk"])
                op("dve", lambda e: e.tensor_copy(out=angi[:], in_=mk[:]), reads=["mk"], writes=["angi"])
                op("dve", lambda e: e.tensor_copy(out=kf[:], in_=angi[:]), reads=["angi"], writes=["kf"])
                op("dve", lambda e: e.tensor_scalar(out=dst[:], in0=ang[:], scalar1=shift, scalar2=None, op0=ALU.add), reads=["ang"], writes=[tag])
                op("dve", lambda e: e.scalar_tensor_tensor(out=dst[:], in0=kf[:], scalar=-TWO_PI, in1=dst[:], op0=ALU.mult, op1=ALU.add),
                   reads=["kf", tag], writes=[tag])
                op("dve", lambda e: e.tensor_scalar(out=mk[:], in0=dst[:], scalar1=math.pi, scalar2=-TWO_PI, op0=ALU.is_gt, op1=ALU.mult),
                   reads=[tag], writes=["mk"])
                op("dve", lambda e: e.tensor_tensor(out=dst[:], in0=dst[:], in1=mk[:], op=ALU.add), reads=[tag, "mk"], writes=[tag])
                op("dve", lambda e: e.tensor_scalar(out=mk[:], in0=dst[:], scalar1=-math.pi, scalar2=TWO_PI, op0=ALU.is_lt, op1=ALU.mult),
                   reads=[tag], writes=["mk"])
                op("dve", lambda e: e.tensor_tensor(out=dst[:], in0=dst[:], in1=mk[:], op=ALU.add), reads=[tag, "mk"], writes=[tag])
                op("dve", lambda e: e.tensor_scalar(out=dst[:], in0=dst[:], scalar1=-3.1415925, scalar2=3.1415925, op0=ALU.max, op1=ALU.min),
                   reads=[tag], writes=[tag])
                op("act", lambda e: e.activation(out=dst[:], in_=dst[:], func=AF.Sin), reads=[tag], writes=[tag])

            sin_of(sinT, 0.0, "sinT")
            sin_of(cosT, 0.5 * math.pi, "cosT")
            if debug:
                op("sp", lambda e: e.dma_start(out=dbg["d_cos"].ap(), in_=cosT[:]), reads=["cosT"], dma="dbg")
                op("sp", lambda e: e.dma_start(out=dbg["d_sin"].ap(), in_=sinT[:]), reads=["sinT"], dma="dbg")
                op("sp", lambda e: e.dma_start(out=dbg["d_ang"].ap(), in_=ang[:]), reads=["ang"], dma="dbg")

            def conv_in(l, rd=()):
                for kc in range(8):
                    op("pool", lambda e, kc=kc: e.dma_start(out=win_b[l, :, kc, :], in_=w_in[l, kc * 128:(kc + 1) * 128, :], max_dma_last_dim=8192),
                       extra=list(rd), writes=[("cvres", "win", l, kc)], dma=("cv", l, 0))

            def conv_rest(l, rd=()):
                for kc in range(8):
                    op("pool", lambda e, kc=kc: e.dma_start(out=wout_b[l, :, kc, :], in_=w_out[l, kc * 128:(kc + 1) * 128, :], max_dma_last_dim=8192),
                       extra=list(rd), writes=[("cvres", "wout", l, kc)], dma=("cv", l, 1))
                for kc in range(8):
                    op("pool", lambda e, kc=kc: e.dma_start(out=wg_b[l, :, kc, :], in_=w_gate[l, kc * 128:(kc + 1) * 128, :], max_dma_last_dim=8192),
                       extra=list(rd), writes=[("cvres", "wg", l, kc)], dma=("cv", l, 2))
                    op("pool", lambda e, kc=kc: e.dma_start(out=wu_b[l, :, kc, :], in_=w_up[l, kc * 128:(kc + 1) * 128, :], max_dma_last_dim=8192),
                       extra=list(rd), writes=[("cvres", "wu", l, kc)], dma=("cv", l, 4))
                for fc in range(NFC):
                    op("pool", lambda e, fc=fc: e.dma_start(out=wd_b[l, :, fc, :], in_=w_down[l, fc * 128:(fc + 1) * 128, :], max_dma_last_dim=8192),
                       extra=list(rd), writes=[("cvres", "wd", l, fc)], dma=("cv", l, 3))

            conv_in(0)
            if stop_after == 'conv':
                return

            for l in range(nlayers):
                xsrc = x_in if l == 0 else xs
                xdst = xs if l == 0 else out
                lam_init = 0.8 - 0.6 * math.exp(-0.3 * l)

                GRP = [[0, 1], [2, 3], [4, 5], [6, 7]]
                with ExitStack() as st:
                    winb = sb("winb", [128, 8, INW], BF16, st)
                    gpre = sb("gpre", [128, D], F32, st)
                    negb = sb("negb", [8, 1], F32, st)
                    xt = [sb("xt%d" % i, [128, D], F32, st) for i in range(2)]
                    junk = sb("junk", [128, D], BF16, st)
                    ss = [sb("ss%d" % i, [128, 1], F32, st) for i in range(2)]
                    rstd = [sb("rstd%d" % i, [128, 1], F32, st) for i in range(2)]
                    hb = [sb("hb%d" % i, [128, D], BF16, st) for i in range(2)]
                    hT = [sb("hT%d" % i, [128, 8, 128], BF16, st) for i in range(2)]
                    q32 = [sb("q32_%d" % i, [128, 8, 64], F32, st) for i in range(2)]
                    ra = [sb("ra%d" % i, [128, 8, 8], F32, st) for i in range(2)]
                    rb = [sb("rb%d" % i, [128, 8, 8], F32, st) for i in range(2)]
                    qtok = [sb("qtok%d" % i, [128, 2048], BF16, st) for i in range(2)]
                    qTs = [sb("qTs%d" % i, [128, 8, 512], BF16, st) for i in range(2)]
                    kTs = [sb("kTs%d" % i, [128, 8, 512], BF16, st) for i in range(2)]
                    vds = [sb("vds%d" % i, [128, 512], BF16, st) for i in range(2)]
                    vfs = [sb("vfs%d" % i, [128, 8, 65], BF16, st) for i in range(2)]
                    lst = [sb("lst%d" % i, [8, 512], F32, st) for i in range(2)]
                    etmp = sb("etmp", [8, 128], F32, st)
                    PS = [alloc(lambda nm: nc.psum_tensor(nm, [128, 1024], F32), "psA%d_%d" % (l, i), st) for i in range(4)]

                    def bank(b):
                        return PS[b // 2][:, (b % 2) * 512:(b % 2 + 1) * 512]

                    op("sp", lambda e: e.dma_start(out=winb[:], in_=win_b[l, :, :, :]), reads=[("cvres", "win", l, kc) for kc in range(8)], writes=["winb"], dma="winb")
                    op("sp", lambda e: e.dma_start(out=gpre[:], in_=attn_pre_g[l].partition_broadcast(128)), writes=["gpre"], dma="winb_g")
                    op("sp", lambda e: e.dma_start(out=negb[:], in_=forget_bias[l].rearrange("(h o) -> h o", o=1)), writes=["negb"], dma="winb_n")
                    op("dve", lambda e: e.tensor_scalar(out=negb[:], in0=negb[:], scalar1=-1.0, scalar2=None, op0=ALU.mult),
                       reads=["negb"], writes=["negb"])
                    for i in range(2):
                        op("dve", lambda e, i=i: e.memset(vfs[i][:], 1.0), writes=[("vfs", i)])

                    def load_x(t):
                        i = t % 2
                        op("sp", lambda e: e.dma_start(out=xt[i][:], in_=xsrc[t * 128:(t + 1) * 128, :]), writes=[("xt", i)], dma=("xt", i))

                    def a_norm(t):
                        i = t % 2
                        op("act", lambda e: e.activation(out=junk[:], in_=xt[i][:], func=AF.Square, accum_out=ss[i][:]),
                           reads=[("xt", i)], writes=[("ss", i)])
                        op("act", lambda e: e.activation(out=rstd[i][:], in_=ss[i][:], func=AF.Ln, scale=1.0 / D, bias=c_eps[:]),
                           reads=[("ss", i), "c_eps"], writes=[("rstd", i)])
                        op("act", lambda e: e.activation(out=rstd[i][:], in_=rstd[i][:], func=AF.Exp, scale=-0.5),
                           reads=[("rstd", i)], writes=[("rstd", i)])
                        op("dve", lambda e: e.scalar_tensor_tensor(out=hb[i][:], in0=xt[i][:], scalar=rstd[i][:], in1=gpre[:],
                                                                   op0=ALU.mult, op1=ALU.mult),
                           reads=[("xt", i), ("rstd", i), "gpre"], writes=[("hb", i)])

                    def a_proj(t):
                        i = t % 2
                        trv = bank(0).bitcast(BF16).rearrange("p (a b) -> p a b", a=8)
                        for kc in range(8):
                            op("pe", lambda e, kc=kc: e.transpose(trv[:, kc, :], hb[i][:, kc * 128:(kc + 1) * 128], ident[:]),
                               reads=[("hb", i), "ident"], writes=["bk0"])
                        op("dve", lambda e: e.tensor_copy(out=hT[i][:], in_=trv), reads=["bk0"], writes=[("hT", i)])
                        for cg in range(6):
                            for kc in range(8):
                                op("pe", lambda e, cg=cg, kc=kc: e.matmul(bank(1 + cg), lhsT=hT[i][:, kc, :], rhs=winb[:, kc, cg * 512:(cg + 1) * 512],
                                                                         start=(kc == 0), stop=(kc == 7)),
                                   reads=[("hT", i), "winb"], writes=["bk%d" % (1 + cg)])
                        for kc in range(8):
                            op("pe", lambda e, kc=kc: e.matmul(bank(7)[0:8, 0:128], lhsT=winb[:, kc, 3072:3080], rhs=hT[i][:, kc, :],
                                                               start=(kc == 0), stop=(kc == 7)),
                               reads=[("hT", i), "winb"], writes=["bk7"])

                    def a_evac(t):
                        i = t % 2
                        c, tc = t // 4, t % 4
                        ci = c % 2
                        op("act", lambda e: e.activation(out=vds[i][:], in_=bank(3), func=AF.Copy), reads=["bk3"], writes=[("vds", i)])
                        op("sp", lambda e: e.dma_start(out=VDo[t * 128:(t + 1) * 128, :], in_=vds[i][:]), reads=[("vds", i)], writes=[("VDo", t)], dma=("vds", i, t // 8))
                        op("act", lambda e: e.activation(out=vfs[i][:, :, 0:64], in_=bank(6).rearrange("p (h d) -> p h d", h=8), func=AF.Copy),
                           reads=["bk6"], writes=[("vfs", i)])
                        op("sp", lambda e: e.dma_start(out=VFo[t * 128:(t + 1) * 128, :], in_=vfs[i][:].rearrange("p h d -> p (h d)")),
                           reads=[("vfs", i)], writes=[("VFo", t)], dma=("vfs", i, t // 8))
                        op("act", lambda e: e.activation(out=qtok[i][:, 512:1024], in_=bank(4), func=AF.Copy, scale=0.125),
                           reads=["bk4"], writes=[("qtokb", i)])
                        op("dve", lambda e: e.tensor_copy(out=qtok[i][:, 1536:2048], in_=bank(5)), reads=["bk5"], writes=[("qtokd", i)])
                        for which, bk, scale, col0 in ((0, 1, 0.125, 0), (1, 2, 1.0, 1024)):
                            tmp = q32[which]
                            dst = qtok[i][:, col0:col0 + 512].rearrange("p (m d) -> p m d", m=8)
                            bkv = bank(bk).rearrange("p (m d) -> p m d", m=8)
                            cs = cosT[:, t, :].unsqueeze(1).to_broadcast([128, 8, 8])
                            sn = sinT[:, t, :].unsqueeze(1).to_broadcast([128, 8, 8])
                            wk = ("qtoka", i) if which == 0 else ("qtokc", i)
                            op("act", lambda e, dst=dst, bkv=bkv, scale=scale: e.activation(out=dst, in_=bkv, func=AF.Copy, scale=scale),
                               reads=["bk%d" % bk], writes=[wk])
                            op("act", lambda e, tmp=tmp, bkv=bkv, scale=scale: e.activation(out=tmp[:, :, 0:16], in_=bkv[:, :, 0:16], func=AF.Copy, scale=scale),
                               reads=["bk%d" % bk], writes=[("q32", which)])
                            op("dve", lambda e, tmp=tmp, cs=cs, which=which: e.tensor_tensor(out=ra[which][:], in0=tmp[:, :, 0:8], in1=cs, op=ALU.mult),
                               reads=[("q32", which)], writes=[("ra", which)])
                            op("dve", lambda e, tmp=tmp, sn=sn, which=which: e.tensor_tensor(out=rb[which][:], in0=tmp[:, :, 8:16], in1=sn, op=ALU.mult),
                               reads=[("q32", which)], writes=[("rb", which)])
                            op("dve", lambda e, dst=dst, which=which: e.tensor_tensor(out=dst[:, :, 0:8], in0=ra[which][:], in1=rb[which][:], op=ALU.subtract),
                               reads=[("ra", which), ("rb", which)], writes=[wk])
                            op("dve", lambda e, tmp=tmp, cs=cs, which=which: e.tensor_tensor(out=ra[which][:], in0=tmp[:, :, 8:16], in1=cs, op=ALU.mult),
                               reads=[("q32", which)], writes=[("ra", which)])
                            op("dve", lambda e, tmp=tmp, sn=sn, which=which: e.tensor_tensor(out=rb[which][:], in0=tmp[:, :, 0:8], in1=sn, op=ALU.mult),
                               reads=[("q32", which)], writes=[("rb", which)])
                            op("dve", lambda e, dst=dst, which=which: e.tensor_tensor(out=dst[:, :, 8:16], in0=ra[which][:], in1=rb[which][:], op=ALU.add),
                               reads=[("ra", which), ("rb", which)], writes=[wk])
                        op("act", lambda e: e.activation(out=etmp[:], in_=bank(7)[0:8, 0:128], func=AF.Exp, scale=-1.0, bias=negb[:]),
                           reads=["bk7", "negb"], writes=["etmp"])
                        op("act", lambda e: e.activation(out=lst[ci][:, tc * 128:(tc + 1) * 128], in_=etmp[:], func=AF.Ln, scale=1.0, bias=c_one[0:8, :]),
                           reads=["etmp", "c_one"], writes=[("lst", ci)])

                    def a_trqk(t):
                        i = t % 2
                        c, tc = t // 4, t % 4
                        ci = c % 2
                        qk_keys = [("qtoka", i), ("qtokb", i), ("qtokc", i), ("qtokd", i)]
                        for which, bk, stg in ((0, 0, qTs), (1, 7, kTs)):
                            tv = bank(bk).bitcast(BF16).rearrange("p (a b) -> p a b", a=8)
                            for pr in range(8):
                                op("pe", lambda e, tv=tv, pr=pr, which=which: e.transpose(tv[:, pr, :], qtok[i][:, which * 1024 + pr * 128: which * 1024 + (pr + 1) * 128], ident[:]),
                                   reads=qk_keys + ["ident"], writes=["bk%d" % bk])
                            op("dve" if which == 0 else "act",
                               (lambda e, tv=tv, stg=stg: e.tensor_copy(out=stg[ci][:, :, tc * 128:(tc + 1) * 128], in_=tv)) if which == 0 else
                               (lambda e, tv=tv, stg=stg: e.activation(out=stg[ci][:, :, tc * 128:(tc + 1) * 128], in_=tv, func=AF.Copy)),
                               reads=["bk%d" % bk], writes=[("stg", which, ci)])
                        if tc == 3:
                            op("sp", lambda e: e.dma_start(out=QT[:, :, c * 512:(c + 1) * 512].rearrange("a p t -> p a t"), in_=qTs[ci][:]),
                               reads=[("stg", 0, ci)], dma=("stg", 0, ci))
                            op("sp", lambda e: e.dma_start(out=KTo[c // 2, :, (c % 2) * 512:(c % 2 + 1) * 512].rearrange("(a p) t -> p a t", p=128), in_=kTs[ci][:]),
                               reads=[("stg", 1, ci)], writes=[("KTo", c)], dma=("stg", 1, ci, c // 2))
                            op("sp", lambda e: e.dma_start(out=LTo[:, c * 512:(c + 1) * 512], in_=lst[ci][:]),
                               reads=[("lst", ci)], writes=[("LTo", c)], dma=("lst", ci))
                            if c % 2 == 1:
                                pc = c // 2
                                op("pool", lambda e: e.collective_compute("AllGather", ALU.bypass, replica_groups=GRP,
                                                                          ins=[KTo[pc, :, :]], outs=[KTa[pc, :, :]]),
                                   reads=[("KTo", 2 * pc), ("KTo", 2 * pc + 1)], dma=("ag", 0, pc), kind="cc")
                                op("pool", lambda e: e.collective_compute("AllGather", ALU.bypass, replica_groups=GRP,
                                                                          ins=[VDo[pc * 1024:(pc + 1) * 1024, :]], outs=[VDa[pc, :, :]]),
                                   reads=[("VDo", tt) for tt in range(8 * pc, 8 * pc + 8)], dma=("ag", 1, pc), kind="cc")
                                op("pool", lambda e: e.collective_compute("AllGather", ALU.bypass, replica_groups=GRP,
                                                                          ins=[VFo[pc * 1024:(pc + 1) * 1024, :]], outs=[VFa[pc, :, :]]),
                                   reads=[("VFo", tt) for tt in range(8 * pc, 8 * pc + 8)], dma=("ag", 2, pc), kind="cc")
                            if c == NCH - 1:
                                op("pool", lambda e: e.collective_compute("AllGather", ALU.bypass, replica_groups=GRP, ins=[LTo[:, :]], outs=[LTa[:, :]]),
                                   reads=[("LTo", cc_) for cc_ in range(NCH)], dma=("ag", 3, 0), kind="cc")

                    load_x(0)
                    load_x(1)
                    a_norm(0)
                    for t in range(NT):
                        a_proj(t)
                        if t + 1 < NT:
                            a_norm(t + 1)
                        if t + 2 < NT:
                            load_x(t + 2)
                        a_evac(t)
                        if t >= 1:
                            a_trqk(t - 1)
                    a_trqk(NT - 1)
                    P.barrier()

                if debug and l == nlayers - 1:
                    op("sp", lambda e: e.dma_start(out=dbg["d_cos"].ap(), in_=cosT[:]), reads=["cosT"], dma="dbg")
                    op("sp", lambda e: e.dma_start(out=dbg["d_sin"].ap(), in_=sinT[:]), reads=["sinT"], dma="dbg")
                    P.barrier()
                if stop_after == 'A' and l == nlayers - 1:
                    return
                if stop_after == 'AG' and l == nlayers - 1:
                    return
                with ExitStack() as st:
                    cs_ = sb("cs", [8, S], F32, st)
                    cn = sb("cn", [8, S], F32, st)
                    prt = [sb("prt%d" % i, [8, S], BF16, st) for i in range(3)]
                    r1 = cs_
                    cq = [sb("cqp%d" % i, [8, T], BF16, st) for i in range(3)]
                    cqt = sb("cqt", [8, 16, 128], BF16, st)
                    gmap = ((0, 0, 0), (0, 1, 3), (1, 0, 1), (1, 1, 2))
                    csv = cs_[:].rearrange("h (j q p) -> h j q p", q=4, p=128)
                    for (rk, e_, q_) in gmap:
                        op("sp", lambda e, rk=rk, e_=e_, q_=q_: e.dma_start(
                            out=csv[:, :, q_, :], in_=LTa[rk * 8:(rk + 1) * 8, :].rearrange("h (j e p) -> h j e p", e=2, p=128)[:, :, e_, :]),
                           writes=["cs"], dma="cs")
                    op("dve", lambda e: e.tensor_tensor_scan(out=cn[:], data0=ones32[0:8, 0:1].to_broadcast([8, S]), data1=cs_[:], initial=0.0, op0=ALU.mult, op1=ALU.add),
                       reads=["cs", "ones32"], writes=["cn"])
                    op("dve", lambda e: e.tensor_copy(out=prt[0][:], in_=cn[:]), reads=["cn"], writes=["p0"])
                    op("dve", lambda e: e.tensor_tensor(out=r1[:], in0=cn[:], in1=prt[0][:], op=ALU.subtract), reads=["cn", "p0"], writes=["cs"])
                    op("dve", lambda e: e.tensor_copy(out=prt[1][:], in_=r1[:]), reads=["cs"], writes=["p1"])
                    op("dve", lambda e: e.tensor_tensor(out=cn[:], in0=r1[:], in1=prt[1][:], op=ALU.subtract), reads=["cs", "p1"], writes=["cn"])
                    op("dve", lambda e: e.tensor_copy(out=prt[2][:], in_=cn[:]), reads=["cn"], writes=["p2"])
                    for pi in range(3):
                        pv = prt[pi][:].rearrange("h (j q p) -> h j q p", q=4, p=128)
                        for (rk, e_, q_) in gmap:
                            op("sp", lambda e, pi=pi, pv=pv, rk=rk, e_=e_, q_=q_: e.dma_start(
                                out=CK[:, pi, rk * T:(rk + 1) * T].rearrange("h (j e p) -> h j e p", e=2, p=128)[:, :, e_, :], in_=pv[:, :, q_, :]),
                               reads=["p%d" % pi], dma="ckst")
                        cqv = cq[pi][:].rearrange("h (j e p) -> h j e p", e=2, p=128)
                        for e_ in range(2):
                            q0 = 0 if e_ == 0 else 3
                            q1 = 1 if e_ == 0 else 2
                            op("dve", lambda e, pv=pv, q0=q0: e.tensor_scalar(out=cqt[:], in0=pv[:, :, q0, :], scalar1=sel[:, 0:1], scalar2=None, op0=ALU.mult),
                               reads=["p%d" % pi, "sel"], writes=["cqt"])
                            op("dve", lambda e, pv=pv, q1=q1, cqv=cqv, e_=e_: e.scalar_tensor_tensor(out=cqv[:, :, e_, :], in0=pv[:, :, q1, :], scalar=sel[:, 1:2], in1=cqt[:],
                                                                                                     op0=ALU.mult, op1=ALU.add),
                               reads=["p%d" % pi, "sel", "cqt"], writes=[("cq", pi)])
                        op("sp", lambda e, pi=pi: e.dma_start(out=CQ[:, pi, :], in_=cq[pi][:]), reads=[("cq", pi)], dma="ckst")
                    if debug and l == nlayers - 1:
                        for nm, t_ in (("d_QT", QT), ("d_KTa", KTa), ("d_VDa", VDa), ("d_VFa", VFa), ("d_LTa", LTa)):
                            op("sp", lambda e, nm=nm, t_=t_: e.dma_start(out=dbg[nm].ap(), in_=t_.ap()), dma="dbg")
                    P.barrier()
                    if debug and l == nlayers - 1:
                        for nm, t_ in (("d_CK", CK), ("d_CQ", CQ)):
                            op("sp", lambda e, nm=nm, t_=t_: e.dma_start(out=dbg[nm].ap(), in_=t_.ap()), dma="dbg")
                        P.barrier()

                if stop_after == 'S' and l == nlayers - 1:
                    return
                with ExitStack() as st:
                    KP = [sb("KP%d" % i, [128, 2 * T], BF16, st) for i in range(4)]
                    QP = [sb("QP%d" % i, [128, T], BF16, st) for i in range(4)]
                    VV = [sb("VV%d" % i, [128, 64, 128], BF16, st) for i in range(2)]
                    PT = [sb("PT%d" % i, [128, 512], BF16, st) for i in range(8)]
                    masks = sb("masks_sb", [128, 8, 512], BF16, st)
                    PTP = [sb("PTP%d" % i, [128, 2, 512], BF16, st) for i in range(4)]
                    op("sp", lambda e: e.dma_start(out=masks[:], in_=masks_in[:, :, :]), writes=["masks"], dma="c0_0")
                    lamv = sb("lamv", [1, 4, 64], F32, st)
                    lamt = sb("lamt", [1, 64], F32, st)
                    lams = sb("lams", [1, 4], F32, st)
                    nlam = sb("nlam", [1, 1], F32, st)
                    nlam128 = sb("nlam128", [128, 1], F32, st)
                    lacc = [sb("lacc%d" % i, [128, 512], F32, st) for i in range(2)]
                    rrf = [sb("rrf%d" % i, [128, 512], F32, st) for i in range(2)]
                    gsub = sb("gsub", [128, 1], F32, st)
                    rr = [sb("rr%d" % i, [128, 512], F32, st) for i in range(2)]
                    bcs = [sb("bcs%d" % i, [128, 512], F32, st) for i in range(2)]
                    Dt = sb("Dt", [128, 512], F32, st)
                    D2 = sb("D2", [128, 512], F32, st)
                    mo = [sb("mo%d" % i, [128, 512], BF16, st) for i in range(2)]
                    PS = [alloc(lambda nm: nc.psum_tensor(nm, [128, 1024], F32), "psB%d_%d" % (l, i), st) for i in range(4)]

                    def bank(b):
                        return PS[b // 2][:, (b % 2) * 512:(b % 2 + 1) * 512]
                    SB = (0, 1, 2)

                    for i_, tns in enumerate((lam_q1, lam_k1, lam_q2, lam_k2)):
                        op("sp", lambda e, i_=i_, tns=tns: e.dma_start(out=lamv[:, i_, :], in_=tns[l:l + 1, :]), writes=["lamv"], dma="lam")
                    op("sp", lambda e: e.dma_start(out=gsub[:], in_=diff_sub_g[l].rearrange("(p o) -> p o", o=1)), writes=["gsub"], dma="lam_g")
                    op("dve", lambda e: e.tensor_scalar(out=gsub[:], in0=gsub[:], scalar1=(1.0 - lam_init), scalar2=None, op0=ALU.mult),
                       reads=["gsub"], writes=["gsub"])
                    for j_ in range(2):
                        op("dve", lambda e, j_=j_: e.tensor_tensor(out=lamt[:], in0=lamv[:, 2 * j_, :], in1=lamv[:, 2 * j_ + 1, :], op=ALU.mult),
                           reads=["lamv"], writes=["lamt"])
                        op("act", lambda e, j_=j_: e.activation(out=lamt[:], in_=lamt[:], func=AF.Copy, accum_out=lams[:, j_:j_ + 1]),
                           reads=["lamt"], writes=["lamt", "lams"])
                    op("act", lambda e: e.activation(out=lams[:, 2:4], in_=lams[:, 0:2], func=AF.Exp), reads=["lams"], writes=["lams"])
                    op("dve", lambda e: e.tensor_tensor(out=nlam[:], in0=lams[:, 3:4], in1=lams[:, 2:3], op=ALU.subtract), reads=["lams"], writes=["nlam"])
                    op("dve", lambda e: e.tensor_scalar(out=nlam[:], in0=nlam[:], scalar1=-lam_init, scalar2=None, op0=ALU.add), reads=["nlam"], writes=["nlam"])

                    def load_map(mi):
                        i = mpos[mi] % 4
                        pr, hf = mi // 2, mi % 2
                        r0 = 64 if (mi < 8 and hf == 1) else 0
                        for pc in range(4):
                            op("sp", lambda e, pc=pc: e.dma_start(out=KP[i][r0:r0 + 64, :].rearrange("p (r c t) -> p r c t", r=2, c=4)[:, :, pc, :],
                                                                  in_=KTa[pc, :, :].rearrange("(r q) t -> q r t", r=2)[pr * 128 + hf * 64:pr * 128 + (hf + 1) * 64, :, :]),
                               writes=[("KP", i, 0, pc)] + ([("KP", i, 1), ("KP", i, 2)] if (r0 and pc == 3) else []), dma=("KP", i))
                        op("sp", lambda e: e.dma_start(out=QP[i][r0:r0 + 64, :], in_=QT[pr, hf * 64:(hf + 1) * 64, :]),
                           writes=[("QP", i, 0)] + ([("QP", i, 1), ("QP", i, 2)] if r0 else []), dma=("QP", i))
                        if mi >= 8:
                            h = mi - 8
                            op("dve", lambda e: e.memset(KP[i][64:70, :], 1.0), writes=[("KP", i, 1), ("KP", i, 2)])
                            op("dve", lambda e: e.memset(QP[i][64:70, :], 1.0), writes=[("QP", i, 1), ("QP", i, 2)])
                            op("sp", lambda e: e.dma_start(out=KP[i][64:67, :], in_=CK[h, :, :]), reads=[("KP", i, 2)], writes=[("KP", i, 1)], dma=("KP", i))
                            op("sp", lambda e: e.dma_start(out=QP[i][67:70, :], in_=CQ[h, :, :]), reads=[("QP", i, 2)], writes=[("QP", i, 1)], dma=("QP", i))

                    def load_v(vi, diff, h):
                        i = vi % 2
                        for X in range(2):
                            for c_ in range(4):
                                b0 = X * 32 + c_ * 8
                                if diff:
                                    op("sp", lambda e, X=X, c_=c_, b0=b0: e.dma_start(
                                        out=VV[i][:, b0:b0 + 8, :],
                                        in_=VDa[c_, X * 1024:(X + 1) * 1024, h * 128:(h + 1) * 128].rearrange("(b p) d -> p b d", p=128)),
                                       writes=[("VV", i, b0)], dma=("VV", i))
                                else:
                                    op("sp", lambda e, X=X, c_=c_, b0=b0: e.dma_start(
                                        out=VV[i][:, b0:b0 + 8, 0:65],
                                        in_=VFa[c_, X * 1024:(X + 1) * 1024, h * 65:(h + 1) * 65].rearrange("(b p) d -> p b d", p=128)),
                                       writes=[("VV", i, b0)], dma=("VV", i))

                    units = [("d", h) for h in range(4)] + [("f", h) for h in range(8)]
                    map_seq = []
                    for (kind, h) in units:
                        map_seq += ([2 * h, 2 * h + 1] if kind == "d" else [8 + h])
                    mpos = {mi: k for k, mi in enumerate(map_seq)}
                    LA = 3

                    op("pe", lambda e: e.matmul(bank(7)[:, 0:1], lhsT=ones32[0:1, :], rhs=nlam[0:1, 0:1], start=True, stop=True),
                       reads=["nlam", "ones32"], writes=[("BK", 7)])
                    op("dve", lambda e: e.tensor_copy(out=nlam128[:], in_=bank(7)[:, 0:1]), reads=[("BK", 7)], writes=["nlam128"])
                    for i in range(2):
                        op("dve", lambda e, i=i: e.memset(VV[i][:], 0.0), writes=[("VV", i, b0) for b0 in range(0, 64, 8)])

                    deferred = []

                    def post_fox(h, chunk, j, accb, now):
                        op("dve", lambda e: e.reciprocal(out=rrf[j][64:65, :], in_=bank(accb)[64:65, :]), reads=[("BK", accb)], writes=[("rrf", j)])

                        def later():
                            op("pe", lambda e: e.matmul(bank(7)[0:64, :], lhsT=ones32[64:65, 0:64], rhs=rrf[j][64:65, :], start=True, stop=True),
                               reads=[("rrf", j), "ones32"], writes=[("BK", 7)])
                            op("dve", lambda e: e.tensor_copy(out=bcs[j][0:64, :], in_=bank(7)[0:64, :]), reads=[("BK", 7)], writes=[("bcs", j)])
                            op("dve", lambda e: e.tensor_tensor(out=mo[j][0:64, :], in0=bank(accb)[0:64, :], in1=bcs[j][0:64, :], op=ALU.mult),
                               reads=[("BK", accb), ("bcs", j)], writes=[("mo", j)])
                            op("sp", lambda e: e.dma_start(out=MT[512 + h * 64:512 + (h + 1) * 64, chunk * 512:(chunk + 1) * 512], in_=mo[j][0:64, :]),
                               reads=[("mo", j)], dma=("mo", j))
                        deferred.append((now + 4, later))

                    def post_diff(h, chunk, j, now):
                        op("dve", lambda e: e.reciprocal(out=rr[0][:], in_=bank(6)), reads=[("BK", 6)], writes=[("rr", 0)])
                        op("dve", lambda e: e.tensor_tensor(out=Dt[:], in0=bank(4), in1=rr[0][:], op=ALU.mult), reads=[("BK", 4), ("rr", 0)], writes=["Dt"])
                        op("dve", lambda e: e.reciprocal(out=rr[1][:], in_=bank(7)), reads=[("BK", 7)], writes=[("rr", 1)])
                        op("dve", lambda e: e.tensor_tensor(out=D2[:], in0=bank(5), in1=rr[1][:], op=ALU.mult), reads=[("BK", 5), ("rr", 1)], writes=["D2"])
                        op("dve", lambda e: e.scalar_tensor_tensor(out=Dt[:], in0=D2[:], scalar=nlam128[:, 0:1], in1=Dt[:], op0=ALU.mult, op1=ALU.add),
                           reads=["Dt", "D2", "nlam128"], writes=["Dt"])
                        op("dve", lambda e: e.tensor_tensor(out=D2[:], in0=Dt[:], in1=Dt[:], op=ALU.mult), reads=["Dt"], writes=["D2"])

                        def later():
                            op("pe", lambda e: e.matmul(bank(7), lhsT=ones32[:, :], rhs=D2[:], start=True, stop=True),
                               reads=["D2", "ones32"], writes=[("BK", 7)])
                            op("act", lambda e: e.activation(out=rr[0][:], in_=bank(7), func=AF.Ln, scale=1.0 / 128, bias=c_eps[:]),
                               reads=[("BK", 7), "c_eps"], writes=[("rr", 0)])
                            op("act", lambda e: e.activation(out=rr[0][:], in_=rr[0][:], func=AF.Exp, scale=-0.5), reads=[("rr", 0)], writes=[("rr", 0)])
                            op("dve", lambda e: e.scalar_tensor_tensor(out=mo[j][:], in0=Dt[:], scalar=gsub[:, 0:1], in1=rr[0][:], op0=ALU.mult, op1=ALU.mult),
                               reads=["Dt", "gsub", ("rr", 0)], writes=[("mo", j)])
                            op("sp", lambda e: e.dma_start(out=MT[h * 128:(h + 1) * 128, chunk * 512:(chunk + 1) * 512], in_=mo[j][:]),
                               reads=[("mo", j)], dma=("mo", j))
                        deferred.append((now + 8, later))

                    stream = []
                    for ui_, (kind, h) in enumerate(units):
                        my_maps = [2 * h, 2 * h + 1] if kind == "d" else [8 + h]
                        for chunk in range(NCH):
                            allu = [(X, kb) for X in range(2) for kb in range(4 * chunk + 4)]
                            full = [(X, kb) for (X, kb) in allu if kb - 4 * chunk <= 0]
                            part = [(X, kb) for (X, kb) in allu if kb - 4 * chunk > 0]
                            ul = full[:-1] + part + full[-1:]
                            for uj, (X, kb) in enumerate(ul):
                                for mj, mi in enumerate(my_maps):
                                    stream.append(dict(ui=ui_, kind=kind, h=h, mi=mi, mj=mj, chunk=chunk, X=X, kb=kb, first=(uj == 0), last=(uj == len(ul) - 1),
                                                       fin=(uj == len(ul) - 1 and mj == len(my_maps) - 1), c0=max(0, kb - 4 * chunk) * 128))
                    loaded_maps = 0
                    last_pv = [None]
                    dpair = [0]
                    loaded_v = 0
                    nS = len(stream)
                    pcount = 0
                    for idx in range(nS + LA):
                        if idx < nS:
                            u = stream[idx]
                            last_mi = (2 * u["h"] + 1) if u["kind"] == "d" else u["mi"]
                            want_m = min(len(map_seq), mpos[last_mi] + 3)
                            while loaded_maps < want_m:
                                load_map(map_seq[loaded_maps])
                                loaded_maps += 1
                            if loaded_v == 0:
                                load_v(0, units[0][0] == "d", units[0][1])
                                loaded_v = 1
                            def emit_qk(j_, s_):
                                uu = stream[j_]
                                sl = mpos[uu["mi"]] % 4
                                chunk, X, kb = uu["chunk"], uu["X"], uu["kb"]
                                if uu["kind"] == "d":
                                    r0 = 64 * uu["mj"]
                                    r1 = r0 + 64
                                else:
                                    r0, r1 = 0, 70
                                qv = QP[sl][r0:r1, chunk * 512:(chunk + 1) * 512]
                                kcol = X * T + kb * 128
                                m = kb - 4 * chunk
                                kq_reads = [("KP", sl, 0, pc_) for pc_ in range(4)] + [("KP", sl, 1), ("KP", sl, 2), ("QP", sl, 0), ("QP", sl, 1), ("QP", sl, 2)]
                                c0 = uu["c0"]
                                op("pe", lambda e: e.matmul(bank(s_)[:, c0:512], lhsT=KP[sl][r0:r1, kcol:kcol + 128], rhs=qv[:, c0:512], start=True, stop=(m < 0)),
                                   reads=kq_reads, writes=[("BK", s_)])

                            def emit_mask(j_, s_):
                                uu = stream[j_]
                                chunk, X, kb = uu["chunk"], uu["X"], uu["kb"]
                                m = kb - 4 * chunk
                                c0 = uu["c0"]
                                if m >= 0:
                                    op("pe", lambda e: e.matmul(bank(s_)[:, c0:c0 + 128], lhsT=ident[:], rhs=masks[:, X * 4 + m, c0:c0 + 128], start=False, stop=True),
                                       reads=["ident", "masks"], writes=[("BK", s_)])

                            if u["kind"] == "d":
                                if u["mj"] == 0:
                                    q_ = dpair[0] % 4
                                    sA = 2 * (dpair[0] % 2)
                                    dpair[0] += 1
                                    u["q"] = q_
                                    stream[idx + 1]["q"] = q_
                                    emit_qk(idx, sA)
                                    emit_qk(idx + 1, sA + 1)
                                    emit_mask(idx, sA)
                                    emit_mask(idx + 1, sA + 1)
                                    c0 = u["c0"]
                                    for mm_ in range(2):
                                        op("act", lambda e, mm_=mm_: e.activation(out=PTP[q_][:, mm_, c0:512], in_=bank(sA + mm_)[:, c0:512], func=AF.Exp),
                                           reads=[("BK", sA + mm_)], writes=[("PTP", q_, mm_)])
                            else:
                                s_ = idx % 4
                                p_ = idx % 8
                                emit_qk(idx, s_)
                                emit_mask(idx, s_)
                                c0 = u["c0"]
                                op("act", lambda e: e.activation(out=PT[p_][:, c0:512], in_=bank(s_)[:, c0:512], func=AF.Exp), reads=[("BK", s_)], writes=[("PT", p_)])
                        if idx == 100:
                            conv_rest(l, rd=[last_pv[0]])
                            if l + 1 < nlayers:
                                conv_in(l + 1, rd=[last_pv[0]])
                        if idx >= LA:
                            u = stream[idx - LA]
                            s_ = (idx - LA) % 3
                            p_ = (idx - LA) % 8
                            want_v = min(len(units), u["ui"] + 2)
                            while loaded_v < want_v:
                                load_v(loaded_v, units[loaded_v][0] == "d", units[loaded_v][1])
                                loaded_v += 1
                            vi = u["ui"] % 2
                            vb = u["X"] * 32 + u["kb"]
                            vkey = [("VV", vi, b0_) for b0_ in range(0, 64, 8)]
                            acc_bank = (4 + u["mj"]) if u["kind"] == "d" else (4 + u["chunk"] % 3)
                            first, last = u["first"], u["last"]
                            c0 = u["c0"]
                            if u["kind"] == "d":
                                ptv = PTP[u["q"]][:, u["mj"], c0:512]
                                ptkey = ("PTP", u["q"], u["mj"])
                            else:
                                ptv = PT[p_][:, c0:512]
                                ptkey = ("PT", p_)
                            last_pv[0] = op("pe", lambda e: e.matmul(bank(acc_bank)[:, c0:512], lhsT=VV[vi][:, vb, :], rhs=ptv, start=first, stop=last),
                                            reads=[ptkey] + vkey, writes=[("BK", acc_bank)])
                            if u["kind"] == "d":
                                l_bank = 6 + u["mj"]
                                mj = u["mj"]
                                if mj == 0:
                                    op("pe", lambda e: e.matmul(bank(l_bank)[:, c0:512], lhsT=onesb[:, :], rhs=ptv, start=first, stop=last),
                                       reads=[ptkey, "onesb"], writes=[("BK", l_bank)])
                                elif first:
                                    op("dve", lambda e: e.tensor_copy(out=lacc[mj][:], in_=ptv), reads=[ptkey], writes=[("lacc", mj)])
                                else:
                                    op("dve", lambda e: e.tensor_tensor(out=lacc[mj][:, c0:512], in0=lacc[mj][:, c0:512], in1=ptv, op=ALU.add),
                                       reads=[ptkey, ("lacc", mj)], writes=[("lacc", mj)])
                                if last and mj == 1:
                                    op("pe", lambda e: e.matmul(bank(l_bank), lhsT=ones32[:, :], rhs=lacc[mj][:], start=True, stop=True),
                                       reads=[("lacc", mj), "ones32"], writes=[("BK", l_bank)])
                            if u["fin"]:
                                j = pcount % 2
                                pcount += 1
                                if u["kind"] == "f":
                                    post_fox(u["h"], u["chunk"], j, acc_bank, idx)
                                else:
                                    post_diff(u["h"], u["chunk"], j, idx)
                        while deferred and (deferred[0][0] <= idx or idx == nS + LA - 1):
                            deferred.pop(0)[1]()
                    if debug and l == nlayers - 1:
                        P.barrier()
                        op("sp", lambda e: e.dma_start(out=dbg["d_MT"].ap(), in_=MT.ap()), dma="dbg")
                    P.barrier()

                if stop_after == 'B' and l == nlayers - 1:
                    return
                with ExitStack() as st:
                    wout = sb("wout", [128, 8, D], BF16, st)
                    wdn = sb("wdn", [128, NFC, D], BF16, st)
                    gpost = sb("gpost", [128, D], F32, st)
                    gfpre = sb("gfpre", [128, D], F32, st)
                    gfpost = sb("gfpost", [128, D], F32, st)
                    mt = [sb("mt%d" % i, [128, 8, 512], BF16, st) for i in range(2)]
                    xt = [sb("xtc%d" % i, [128, D], F32, st) for i in range(2)]
                    x1k = [sb("x1k%d" % i, [128, D], F32, st) for i in range(8)]
                    junk = sb("junkc", [128, D], BF16, st)
                    ss = [sb("ssc%d" % i, [128, 1], F32, st) for i in range(4)]
                    rstd = [sb("rstdc%d" % i, [128, 1], F32, st) for i in range(4)]
                    hb = [sb("hbc%d" % i, [128, D], BF16, st) for i in range(2)]
                    hT = [sb("hTc%d" % i, [128, 8, 512], BF16, st) for i in range(2)]
                    AT = sb("AT", [128, NFC, 512], BF16, st)
                    wgp = [sb("wgp%d" % i, [128, 8, 256], BF16, st) for i in range(2)]
                    wup = [sb("wup%d" % i, [128, 8, 256], BF16, st) for i in range(2)]
                    sg = [sb("sg%d" % i, [128, 512], BF16, st) for i in range(2)]
                    PS = [alloc(lambda nm: nc.psum_tensor(nm, [128, 1024], F32), "psC%d_%d" % (l, i), st) for i in range(4)]

                    def bank(b):
                        return PS[b // 2][:, (b % 2) * 512:(b % 2 + 1) * 512]
                    op("sp", lambda e: e.dma_start(out=wout[:], in_=wout_b[l, :, :, :]), reads=[("cvres", "wout", l, kc) for kc in range(8)], writes=["wout"], dma="wc_wout")
                    op("sp", lambda e: e.dma_start(out=wdn[:], in_=wd_b[l, :, :, :]), reads=[("cvres", "wd", l, fc) for fc in range(NFC)], writes=["wdn"], dma="wc_wdn")
                    op("sp", lambda e: e.dma_start(out=gpost[:], in_=attn_post_g[l].partition_broadcast(128)), writes=["gpost"], dma="wc_gpost")
                    op("sp", lambda e: e.dma_start(out=gfpre[:], in_=ffn_pre_g[l].partition_broadcast(128)), writes=["gfpre"], dma="wc_gfpre")
                    op("sp", lambda e: e.dma_start(out=gfpost[:], in_=ffn_post_g[l].partition_broadcast(128)), writes=["gfpost"], dma="wc_gfpost")

                    def rms(src_ap, src_reads, i):
                        op("act", lambda e: e.activation(out=junk[:], in_=src_ap, func=AF.Square, accum_out=ss[i][:]), reads=src_reads, writes=[("ss", i)])
                        op("act", lambda e: e.activation(out=rstd[i][:], in_=ss[i][:], func=AF.Ln, scale=1.0 / D, bias=c_eps[:]),
                           reads=[("ss", i), "c_eps"], writes=[("rstd", i)])
                        op("act", lambda e: e.activation(out=rstd[i][:], in_=rstd[i][:], func=AF.Exp, scale=-0.5), reads=[("rstd", i)], writes=[("rstd", i)])

                    def load_mt(c):
                        op("sp", lambda e: e.dma_start(out=mt[c % 2][:], in_=MT[:, c * 512:(c + 1) * 512].rearrange("(a p) t -> p a t", p=128)),
                           writes=[("mt", c % 2)], dma=("mt", c % 2))

                    def s1_outproj(c, t):
                        tg = 4 * c + t
                        i = t % 2
                        op("sp", lambda e: e.dma_start(out=xt[i][:], in_=xsrc[tg * 128:(tg + 1) * 128, :]), writes=[("xt", i)], dma=("xt", i))
                        for half in range(2):
                            for kc in range(8):
                                op("pe", lambda e, half=half, kc=kc: e.matmul(bank(half), lhsT=mt[c % 2][:, kc, t * 128:(t + 1) * 128],
                                                                             rhs=wout[:, kc, half * 512:(half + 1) * 512], start=(kc == 0), stop=(kc == 7)),
                                   reads=[("mt", c % 2), "wout"], writes=[("bk", half)])

                    def s1_chain(c, t):
                        i = t % 2
                        xk = x1k[(c % 2) * 4 + t]
                        xkey = ("x1k", (c % 2) * 4 + t)
                        rms(PS[0][:], [("bk", 0), ("bk", 1)], 0)
                        op("dve", lambda e: e.scalar_tensor_tensor(out=xk[:], in0=PS[0][:], scalar=rstd[0][:], in1=gpost[:], op0=ALU.mult, op1=ALU.mult),
                           reads=[("bk", 0), ("bk", 1), ("rstd", 0), "gpost"], writes=[xkey])
                        op("dve", lambda e: e.tensor_tensor(out=xk[:], in0=xk[:], in1=xt[i][:], op=ALU.add), reads=[xkey, ("xt", i)], writes=[xkey])
                        rms(xk[:], [xkey], 1)
                        op("dve", lambda e: e.scalar_tensor_tensor(out=hb[i][:], in0=xk[:], scalar=rstd[1][:], in1=gfpre[:], op0=ALU.mult, op1=ALU.mult),
                           reads=[xkey, ("rstd", 1), "gfpre"], writes=[("hb", i)])

                    def s1_transpose(c, t):
                        i = t % 2
                        trv = bank(4).bitcast(BF16).rearrange("p (a b) -> p a b", a=8)
                        for kc in range(8):
                            op("pe", lambda e, kc=kc: e.transpose(trv[:, kc, :], hb[i][:, kc * 128:(kc + 1) * 128], ident[:]),
                               reads=[("hb", i), "ident"], writes=[("bk", 4)])
                        op("act", lambda e: e.activation(out=hT[c % 2][:, :, t * 128:(t + 1) * 128], in_=trv, func=AF.Copy), reads=[("bk", 4)], writes=[("hT", c % 2)])

                    def s2_piece(c, g):
                        gi = g % 2
                        op("sp", lambda e: e.dma_start(out=wgp[gi][:], in_=wg_b[l, :, :, g * 256:(g + 1) * 256]), reads=[("cvres", "wg", l, kc) for kc in range(8)], writes=[("wgp", gi)], dma=("wgp", gi))
                        op("sp", lambda e: e.dma_start(out=wup[gi][:], in_=wu_b[l, :, :, g * 256:(g + 1) * 256]), reads=[("cvres", "wu", l, kc) for kc in range(8)], writes=[("wup", gi)], dma=("wup", gi))
                        for fcl in range(2):
                            fc = 2 * g + fcl
                            gb, ub = ((5, 6) if fc % 2 == 0 else (7, 4))
                            for kc in range(8):
                                op("pe", lambda e, gb=gb, kc=kc, fcl=fcl: e.matmul(bank(gb), lhsT=wgp[gi][:, kc, fcl * 128:(fcl + 1) * 128], rhs=hT[c % 2][:, kc, :],
                                                                                   start=(kc == 0), stop=(kc == 7)),
                                   reads=[("wgp", gi), ("hT", c % 2)], writes=[("bk", gb)])
                            for kc in range(8):
                                op("pe", lambda e, ub=ub, kc=kc, fcl=fcl: e.matmul(bank(ub), lhsT=wup[gi][:, kc, fcl * 128:(fcl + 1) * 128], rhs=hT[c % 2][:, kc, :],
                                                                                   start=(kc == 0), stop=(kc == 7)),
                                   reads=[("wup", gi), ("hT", c % 2)], writes=[("bk", ub)])
                            sgi = fc % 2
                            op("act", lambda e, gb=gb, sgi=sgi: e.activation(out=sg[sgi][:], in_=bank(gb), func=AF.Silu), reads=[("bk", gb)], writes=[("sg", sgi)])
                            op("dve", lambda e, ub=ub, sgi=sgi, fc=fc: e.tensor_tensor(out=AT[:, fc, :], in0=bank(ub), in1=sg[sgi][:], op=ALU.mult),
                               reads=[("bk", ub), ("sg", sgi)], writes=["AT"])

                    def s3_mm(c, t):
                        yb = 2 * (t % 2)
                        for half in range(2):
                            for fc in range(NFC):
                                op("pe", lambda e, half=half, fc=fc: e.matmul(bank(yb + half), lhsT=AT[:, fc, t * 128:(t + 1) * 128],
                                                                             rhs=wdn[:, fc, half * 512:(half + 1) * 512], start=(fc == 0), stop=(fc == NFC - 1)),
                                   reads=["AT", "wdn"], writes=[("bk", yb + half)])

                    def s3_post(c, t):
                        tg = 4 * c + t
                        yb = 2 * (t % 2)
                        i = t % 2
                        xk = x1k[(c % 2) * 4 + t]
                        xkey = ("x1k", (c % 2) * 4 + t)
                        rms(PS[t % 2][:], [("bk", yb), ("bk", yb + 1)], 2 + i)
                        op("dve", lambda e: e.scalar_tensor_tensor(out=xt[i][:], in0=PS[t % 2][:], scalar=rstd[2 + i][:], in1=gfpost[:], op0=ALU.mult, op1=ALU.mult),
                           reads=[("bk", yb), ("bk", yb + 1), ("rstd", 2 + i), "gfpost"], writes=[("xt", i)])
                        op("dve", lambda e: e.tensor_tensor(out=xt[i][:], in0=xt[i][:], in1=xk[:], op=ALU.add), reads=[xkey, ("xt", i)], writes=[("xt", i)])
                        op("sp", lambda e: e.dma_start(out=xdst[tg * 128:(tg + 1) * 128, :], in_=xt[i][:]), reads=[("xt", i)], dma=("xo", i))

                    load_mt(0)
                    for t in range(4):
                        s1_outproj(0, t)
                        s1_chain(0, t)
                        s1_transpose(0, t)
                    for c in range(NCH):
                        nxt = c + 1 if c + 1 < NCH else None
                        if nxt is not None:
                            load_mt(nxt)
                        for g in range(11):
                            s2_piece(c, g)
                            if nxt is not None:
                                if g in (0, 2, 4, 6):
                                    tt = g // 2
                                    s1_outproj(nxt, tt)
                                    s1_chain(nxt, tt)
                                if g in (2, 4, 6, 8):
                                    s1_transpose(nxt, g // 2 - 1)
                        s3_mm(c, 0)
                        for t in range(4):
                            if t + 1 < 4:
                                s3_mm(c, t + 1)
                            s3_post(c, t)
                    P.barrier()
    P.run(gen)
    return nc


_NC_CACHE = {}


def kernel(**inputs):
    x = np.asarray(inputs["x"], dtype=np.float32)
    pos = np.asarray(inputs["positions"]).astype(np.int32)
    B = x.shape[0]
    assert B == 4 and x.shape[1] == S and x.shape[2] == D
    inv_freq = (1.0 / (np.float32(500000.0) ** (np.arange(0, 16, 2, dtype=np.float32) / np.float32(16)))).astype(np.float32)
    invf = np.ascontiguousarray(np.broadcast_to(inv_freq[None, :], (128, 8))).astype(np.float32)
    if "nc" not in _NC_CACHE:
        _NC_CACHE["nc"] = build()
    nc = _NC_CACHE["nc"]
    shared = {}
    for k in ("attn_pre_g", "w_in", "forget_bias", "lam_q1", "lam_k1", "lam_q2", "lam_k2", "diff_sub_g", "w_out",
              "attn_post_g", "ffn_pre_g", "w_gate", "w_up", "w_down", "ffn_post_g"):
        shared[k] = np.ascontiguousarray(np.asarray(inputs[k], dtype=np.float32))
    in_maps = []
    idxs = []
    for core in range(8):
        b, r = core // 2, core % 2
        blocks = own_blocks(r)
        tok = np.concatenate([np.arange(g * 128, (g + 1) * 128) for g in blocks])
        idxs.append((b, tok))
        m = dict(shared)
        m["x"] = np.ascontiguousarray(x[b][tok])
        m["pos"] = np.ascontiguousarray(pos[b][tok].reshape(NT, 128).T)
        m["masks"] = make_masks(r)
        selv = np.zeros((8, 2), np.float32)
        selv[:, r] = -1.0
        m["sel"] = selv
        m["invf"] = invf
        in_maps.append(m)
    res = run_bass_kernel_spmd(nc, in_maps, core_ids=list(range(8)))
    out = np.empty((B, S, D), np.float32)
    for core in range(8):
        b, tok = idxs[core]
        out[b][tok] = np.asarray(res.results[core]["out"], dtype=np.float32)
    return out
```

```python
import math
from contextlib import ExitStack
import numpy as np
import ml_dtypes
import concourse.bass as bass
import concourse.mybir as mybir
from concourse.bass_utils import run_bass_kernel_spmd

F32 = mybir.dt.float32
BF16 = mybir.dt.bfloat16
I32 = mybir.dt.int32
AF = mybir.ActivationFunctionType
ALU = mybir.AluOpType

D = 1024
S = 8192
T = 4096
NT = 32
NCH = 8
DEPTH = 2
DFF = 2816
NFC = 22
INW = 3080
EPS = 1e-6
NEG = -30000.0
ENG = ("pe", "act", "dve", "pool", "sp")
SAME_ENGINE_SYNC = True


class Rec:
    __slots__ = ("eng", "fn", "deps", "needs_inc", "incval", "dma", "dma_thr", "kind", "bar")

    def __init__(self, eng, fn, dma=None, kind="op"):
        self.eng = eng
        self.fn = fn
        self.deps = ()
        self.needs_inc = False
        self.incval = None
        self.dma = dma
        self.dma_thr = None
        self.kind = kind
        self.bar = None


class Prog:
    def __init__(self, nc):
        self.nc = nc
        self.recs = []
        self.streams = {e: [] for e in ENG}
        self.lastw = {}
        self.readers = {}
        self.dma_cum = {}
        self.dma_keys = []
        self.mode = "dry"
        self.seq = 0
        self.cur = None
        self.eng = None
        self.known = {}
        self.esem = None
        self.dsem = None

    def _wait(self, sem, key, val):
        if self.known.get(key, 0) >= val:
            return
        self.known[key] = val
        self.eng.wait_ge(sem, val)

    def op(self, eng, fn, reads=(), writes=(), dma=None, kind="op", extra=()):
        if self.mode != "dry":
            rec = self.recs[self.seq]
            self.seq += 1
            assert rec.eng == eng and rec.kind == kind
            if eng != self.cur:
                return rec
            for d in rec.deps:
                if d.dma is not None:
                    self._wait(self.dsem[d.dma], ("d", d.dma), d.dma_thr)
                else:
                    if d.eng == eng and (eng == "pe" or not SAME_ENGINE_SYNC):
                        continue
                    self._wait(self.esem[d.eng], ("e", d.eng), d.incval)
            ins = fn(self.eng)
            if rec.dma is not None:
                if rec.kind == "cc":
                    ins.then_inc(self.dsem[rec.dma])
                else:
                    ins.then_inc(self.dsem[rec.dma], 16)
            elif rec.needs_inc:
                ins.then_inc(self.esem[eng], 1)
            return rec
        rec = Rec(eng, None, dma, kind)
        deps = set()
        for r in reads:
            w = self.lastw.get(r)
            if w is not None:
                deps.add(w)
        for w_ in writes:
            w = self.lastw.get(w_)
            if w is not None:
                deps.add(w)
            for rd in self.readers.get(w_, ()):
                deps.add(rd)
        for x_ in extra:
            deps.add(x_)
        deps.discard(rec)
        rec.deps = deps
        for d in deps:
            d.needs_inc = True
        if dma is not None:
            if dma not in self.dma_cum:
                self.dma_cum[dma] = 0
                self.dma_keys.append(dma)
            self.dma_cum[dma] += (1 if kind == "cc" else 16)
            rec.dma_thr = self.dma_cum[dma]
        for r in reads:
            self.readers.setdefault(r, []).append(rec)
        for w_ in writes:
            self.lastw[w_] = rec
            self.readers[w_] = []
        self.streams[eng].append(rec)
        self.recs.append(rec)
        return rec

    def barrier(self):
        if self.mode != "dry":
            rec = self.recs[self.seq]
            self.seq += 1
            assert rec.kind == "bar"
            for e2 in ENG:
                last = rec.bar["last"][e2]
                if last is not None and e2 != self.cur:
                    self._wait(self.esem[e2], ("e", e2), last.incval)
            for k, v in rec.bar["dma"].items():
                self._wait(self.dsem[k], ("d", k), v)
            return
        snap = {"last": {}, "dma": {k: v for k, v in self.dma_cum.items() if not (isinstance(k, tuple) and k[0] == "cv")}}
        for e in ENG:
            last = None
            for r in reversed(self.streams[e]):
                if r.dma is None:
                    last = r
                    break
            if last is not None:
                last.needs_inc = True
            snap["last"][e] = last
        rec = Rec(None, None, kind="bar")
        rec.bar = snap
        self.recs.append(rec)
        self.lastw = {k: v for k, v in self.lastw.items() if isinstance(k, tuple) and k[0] == "cvres"}
        self.readers = {}

    def run(self, gen):
        nc = self.nc
        self.mode = "dry"
        gen()
        for e in ENG:
            c = 0
            for r in self.streams[e]:
                if r.dma is None and r.needs_inc:
                    c += 1
                    r.incval = c
        with ExitStack() as es:
            self.esem = {e: es.enter_context(nc.semaphore("s_" + e)) for e in ENG}
            self.dsem = {k: es.enter_context(nc.semaphore("d_%d" % i)) for i, k in enumerate(self.dma_keys)}
            block = es.enter_context(nc.Block())

            def mk(e):
                def body(eng):
                    self.mode = "emit"
                    self.seq = 0
                    self.cur = e
                    self.eng = eng
                    self.known = {}
                    gen()
                    assert self.seq == len(self.recs)
                return body

            block.tensor(mk("pe"))
            block.scalar(mk("act"))
            block.vector(mk("dve"))
            block.gpsimd(mk("pool"))
            block.sync(mk("sp"))


def own_blocks(r):
    out = []
    for j in range(16):
        if r == 0:
            out += [4 * j, 4 * j + 3]
        else:
            out += [4 * j + 1, 4 * j + 2]
    return out


def make_masks(r):
    def G(rank, idx):
        j, e = idx // 2, idx % 2
        return 4 * j + (3 * e if rank == 0 else 1 + e)
    tri = np.where(np.arange(128)[:, None] <= np.arange(128)[None, :], 0.0, NEG).astype(np.float32)
    M = np.zeros((128, 8, 512), np.float32)
    for X in range(2):
        for m in range(4):
            for n in range(4):
                kg, qg = G(X, m), G(r, n)
                if kg < qg:
                    blk = 0.0
                elif kg == qg:
                    blk = tri
                else:
                    blk = NEG
                M[:, X * 4 + m, n * 128:(n + 1) * 128] = blk
    return M.astype(ml_dtypes.bfloat16)


def build(debug=False, nlayers=DEPTH, stop_after=None):
    nc = bass.Bass("TRN2", target_bir_lowering=False)
    P = Prog(nc)

    def din(name, shape, dt=F32):
        return nc.dram_tensor(name, list(shape), dt, kind="ExternalInput")

    x_in = din("x", [T, D])
    pos_in = din("pos", [128, NT], I32)
    masks_in = din("masks", [128, 8, 512], BF16)
    sel_in = din("sel", [8, 2])
    invf_in = din("invf", [128, 8])
    attn_pre_g = din("attn_pre_g", [DEPTH, D])
    w_in = din("w_in", [DEPTH, D, INW])
    forget_bias = din("forget_bias", [DEPTH, 8])
    lam_q1 = din("lam_q1", [DEPTH, 64])
    lam_k1 = din("lam_k1", [DEPTH, 64])
    lam_q2 = din("lam_q2", [DEPTH, 64])
    lam_k2 = din("lam_k2", [DEPTH, 64])
    diff_sub_g = din("diff_sub_g", [DEPTH, 128])
    w_out = din("w_out", [DEPTH, D, D])
    attn_post_g = din("attn_post_g", [DEPTH, D])
    ffn_pre_g = din("ffn_pre_g", [DEPTH, D])
    w_gate = din("w_gate", [DEPTH, D, DFF])
    w_up = din("w_up", [DEPTH, D, DFF])
    w_down = din("w_down", [DEPTH, DFF, D])
    ffn_post_g = din("ffn_post_g", [DEPTH, D])
    out = nc.dram_tensor("out", [T, D], F32, kind="ExternalOutput")

    def scr(name, shape, dt):
        return nc.dram_tensor(name, list(shape), dt)

    win_b = scr("win_b", [DEPTH, 128, 8, INW], BF16)
    wout_b = scr("wout_b", [DEPTH, 128, 8, D], BF16)
    wg_b = scr("wg_b", [DEPTH, 128, 8, DFF], BF16)
    wu_b = scr("wu_b", [DEPTH, 128, 8, DFF], BF16)
    wd_b = scr("wd_b", [DEPTH, 128, NFC, D], BF16)
    xs = scr("xs", [T, D], F32)
    x1s = scr("x1s", [T, D], F32)
    QT = scr("QT", [8, 128, T], BF16)
    KTo = scr("KTo", [4, 1024, 1024], BF16)
    KTa = scr("KTa", [4, 2048, 1024], BF16)
    VDo = scr("VDo", [T, 512], BF16)
    VDa = scr("VDa", [4, 2048, 512], BF16)
    VFo = scr("VFo", [T, 520], BF16)
    VFa = scr("VFa", [4, 2048, 520], BF16)
    LTo = scr("LTo", [8, T], F32)
    LTa = scr("LTa", [16, T], F32)
    CK = scr("CK", [8, 3, 2 * T], BF16)
    CQ = scr("CQ", [8, 3, T], BF16)
    MT = scr("MT", [D, T], BF16)

    dbg = {}
    if debug:
        for nm, shp, dt in (("d_QT", [8, 128, T], BF16), ("d_KTa", [4, 2048, 1024], BF16), ("d_VDa", [4, 2048, 512], BF16),
                            ("d_VFa", [4, 2048, 520], BF16), ("d_LTa", [16, T], F32), ("d_CK", [8, 3, 2 * T], BF16),
                            ("d_CQ", [8, 3, T], BF16), ("d_MT", [D, T], BF16), ("d_x1", [T, D], F32), ("d_xs", [T, D], F32),
                            ("d_cos", [128, NT, 8], F32), ("d_sin", [128, NT, 8], F32), ("d_ang", [128, NT, 8], F32)):
            dbg[nm] = nc.dram_tensor(nm, shp, dt, kind="ExternalOutput")

    op = P.op
    uid = [0]

    def nuid():
        uid[0] += 1
        return uid[0]

    acache = []
    acur = [0]

    def alloc(mk, name, stack):
        if P.mode == "dry":
            t = stack.enter_context(mk("%s_%d" % (name, nuid())))
            acache.append(t)
            return t
        t = acache[acur[0]]
        acur[0] += 1
        return t

    def gen():
        acur[0] = 0
        with ExitStack() as gs:
            def sb(name, shape, dt, stack=gs):
                return alloc(lambda nm: nc.sbuf_tensor(nm, list(shape), dt), name, stack)

            ident = sb("ident", [128, 128], BF16)
            ones32 = sb("ones32", [128, 128], F32)
            onesb = sb("onesb", [128, 128], BF16)
            c_eps = sb("c_eps", [128, 1], F32)
            c_one = sb("c_one", [128, 1], F32)
            sel = sb("sel_sb", [8, 2], F32)
            posi = sb("posi", [128, NT], I32)
            posf = sb("posf", [128, NT], F32)
            invf = sb("invf_sb", [128, 8], F32)
            ang = sb("ang", [128, NT, 8], F32)
            cosT = sb("cosT", [128, NT, 8], F32)
            sinT = sb("sinT", [128, NT, 8], F32)
            c_pi = sb("c_pi", [128, 1], F32)

            op("pool", lambda e: e.memset(ones32[:], 1.0), writes=["ones32"])
            op("pool", lambda e: e.memset(onesb[:], 1.0), writes=["onesb"])
            op("pool", lambda e: e.memset(c_eps[:], EPS), writes=["c_eps"])
            op("pool", lambda e: e.memset(c_one[:], 1.0), writes=["c_one"])
            op("pool", lambda e: e.memset(c_pi[:], math.pi), writes=["c_pi"])
            op("pool", lambda e: e.memset(ident[:], 1.0), writes=["ident"])
            op("pool", lambda e: e.affine_select(out=ident[:], in_=ident[:], pattern=[[-1, 128]], compare_op=ALU.is_equal,
                                                 fill=0.0, base=0, channel_multiplier=1), reads=["ident"], writes=["ident"])
            op("sp", lambda e: e.dma_start(out=sel[:], in_=sel_in[:, :]), writes=["sel"], dma="c0_1")
            op("sp", lambda e: e.dma_start(out=posi[:], in_=pos_in[:, :]), writes=["posi"], dma="c0_2")
            op("sp", lambda e: e.dma_start(out=invf[:], in_=invf_in[:, :]), writes=["invf"], dma="c0_3")
            op("dve", lambda e: e.tensor_copy(out=posf[:], in_=posi[:]), reads=["posi"], writes=["posf"])
            op("dve", lambda e: e.tensor_tensor(out=ang[:], in0=posf[:].unsqueeze(2).to_broadcast([128, NT, 8]),
                                                in1=invf[:].unsqueeze(1).to_broadcast([128, NT, 8]), op=ALU.mult),
               reads=["posf", "invf"], writes=["ang"])
            TWO_PI = 2.0 * math.pi
            angi = sb("angi", [128, NT, 8], I32)
            kf = sb("kf", [128, NT, 8], F32)
            mk = sb("mk", [128, NT, 8], F32)

            def sin_of(dst, shift, tag):
                op("dve", lambda e: e.tensor_scalar(out=mk[:], in0=ang[:], scalar1=shift, scalar2=1.0 / TWO_PI, op0=ALU.add, op1=ALU.mult),
                   reads=["ang"], writes=["mk"])
                op("dve", lambda e: e.tensor_copy(out=angi[:], in_=mk[:]), reads=["mk"], writes=["angi"])
                op("dve", lambda e: e.tensor_copy(out=kf[:], in_=angi[:]), reads=["angi"], writes=["kf"])
                op("dve", lambda e: e.tensor_scalar(out=dst[:], in0=ang[:], scalar1=shift, scalar2=None, op0=ALU.add), reads=["ang"], writes=[tag])
                op("dve", lambda e: e.scalar_tensor_tensor(out=dst[:], in0=kf[:], scalar=-TWO_PI, in1=dst[:], op0=ALU.mult, op1=ALU.add),
                   reads=["kf", tag], writes=[tag])
                op("dve", lambda e: e.tensor_scalar(out=mk[:], in0=dst[:], scalar1=math.pi, scalar2=-TWO_PI, op0=ALU.is_gt, op1=ALU.mult),
                   reads=[tag], writes=["mk"])
                op("dve", lambda e: e.tensor_tensor(out=dst[:], in0=dst[:], in1=mk[:], op=ALU.add), reads=[tag, "mk"], writes=[tag])
                op("dve", lambda e: e.tensor_scalar(out=mk[:], in0=dst[:], scalar1=-math.pi, scalar2=TWO_PI, op0=ALU.is_lt, op1=ALU.mult),
                   reads=[tag], writes=["mk"])
                op("dve", lambda e: e.tensor_tensor(out=dst[:], in0=dst[:], in1=mk[:], op=ALU.add), reads=[tag, "mk"], writes=[tag])
                op("dve", lambda e: e.tensor_scalar(out=dst[:], in0=dst[:], scalar1=-3.1415925, scalar2=3.1415925, op0=ALU.max, op1=ALU.min),
                   reads=[tag], writes=[tag])
                op("act", lambda e: e.activation(out=dst[:], in_=dst[:], func=AF.Sin), reads=[tag], writes=[tag])

            sin_of(sinT, 0.0, "sinT")
            sin_of(cosT, 0.5 * math.pi, "cosT")
            if debug:
                op("sp", lambda e: e.dma_start(out=dbg["d_cos"].ap(), in_=cosT[:]), reads=["cosT"], dma="dbg")
                op("sp", lambda e: e.dma_start(out=dbg["d_sin"].ap(), in_=sinT[:]), reads=["sinT"], dma="dbg")
                op("sp", lambda e: e.dma_start(out=dbg["d_ang"].ap(), in_=ang[:]), reads=["ang"], dma="dbg")

            def conv_in(l, rd=()):
                for kc in range(8):
                    op("pool", lambda e, kc=kc: e.dma_start(out=win_b[l, :, kc, :], in_=w_in[l, kc * 128:(kc + 1) * 128, :], max_dma_last_dim=8192),
                       extra=list(rd), writes=[("cvres", "win", l, kc)], dma=("cv", l, 0))

            def conv_rest(l, rd=()):
                for kc in range(8):
                    op("pool", lambda e, kc=kc: e.dma_start(out=wout_b[l, :, kc, :], in_=w_out[l, kc * 128:(kc + 1) * 128, :], max_dma_last_dim=8192),
                       extra=list(rd), writes=[("cvres", "wout", l, kc)], dma=("cv", l, 1))
                for kc in range(8):
                    op("pool", lambda e, kc=kc: e.dma_start(out=wg_b[l, :, kc, :], in_=w_gate[l, kc * 128:(kc + 1) * 128, :], max_dma_last_dim=8192),
                       extra=list(rd), writes=[("cvres", "wg", l, kc)], dma=("cv", l, 2))
                    op("pool", lambda e, kc=kc: e.dma_start(out=wu_b[l, :, kc, :], in_=w_up[l, kc * 128:(kc + 1) * 128, :], max_dma_last_dim=8192),
                       extra=list(rd), writes=[("cvres", "wu", l, kc)], dma=("cv", l, 4))
                for fc in range(NFC):
                    op("pool", lambda e, fc=fc: e.dma_start(out=wd_b[l, :, fc, :], in_=w_down[l, fc * 128:(fc + 1) * 128, :], max_dma_last_dim=8192),
                       extra=list(rd), writes=[("cvres", "wd", l, fc)], dma=("cv", l, 3))

            conv_in(0)
            if stop_after == 'conv':
                return

            for l in range(nlayers):
                xsrc = x_in if l == 0 else xs
                xdst = xs if l == 0 else out
                lam_init = 0.8 - 0.6 * math.exp(-0.3 * l)

                GRP = [[0, 1], [2, 3], [4, 5], [6, 7]]
                with ExitStack() as st:
                    winb = sb("winb", [128, 8, INW], BF16, st)
                    gpre = sb("gpre", [128, D], F32, st)
                    negb = sb("negb", [8, 1], F32, st)
                    xt = [sb("xt%d" % i, [128, D], F32, st) for i in range(2)]
                    junk = sb("junk", [128, D], BF16, st)
                    ss = [sb("ss%d" % i, [128, 1], F32, st) for i in range(2)]
                    rstd = [sb("rstd%d" % i, [128, 1], F32, st) for i in range(2)]
                    hb = [sb("hb%d" % i, [128, D], BF16, st) for i in range(2)]
                    hT = [sb("hT%d" % i, [128, 8, 128], BF16, st) for i in range(2)]
                    q32 = [sb("q32_%d" % i, [128, 8, 64], F32, st) for i in range(2)]
                    ra = [sb("ra%d" % i, [128, 8, 8], F32, st) for i in range(2)]
                    rb = [sb("rb%d" % i, [128, 8, 8], F32, st) for i in range(2)]
                    qtok = [sb("qtok%d" % i, [128, 2048], BF16, st) for i in range(2)]
                    qTs = [sb("qTs%d" % i, [128, 8, 512], BF16, st) for i in range(2)]
                    kTs = [sb("kTs%d" % i, [128, 8, 512], BF16, st) for i in range(2)]
                    vds = [sb("vds%d" % i, [128, 512], BF16, st) for i in range(2)]
                    vfs = [sb("vfs%d" % i, [128, 8, 65], BF16, st) for i in range(2)]
                    lst = [sb("lst%d" % i, [8, 512], F32, st) for i in range(2)]
                    etmp = sb("etmp", [8, 128], F32, st)
                    PS = [alloc(lambda nm: nc.psum_tensor(nm, [128, 1024], F32), "psA%d_%d" % (l, i), st) for i in range(4)]

                    def bank(b):
                        return PS[b // 2][:, (b % 2) * 512:(b % 2 + 1) * 512]

                    op("sp", lambda e: e.dma_start(out=winb[:], in_=win_b[l, :, :, :]), reads=[("cvres", "win", l, kc) for kc in range(8)], writes=["winb"], dma="winb")
                    op("sp", lambda e: e.dma_start(out=gpre[:], in_=attn_pre_g[l].partition_broadcast(128)), writes=["gpre"], dma="winb_g")
                    op("sp", lambda e: e.dma_start(out=negb[:], in_=forget_bias[l].rearrange("(h o) -> h o", o=1)), writes=["negb"], dma="winb_n")
                    op("dve", lambda e: e.tensor_scalar(out=negb[:], in0=negb[:], scalar1=-1.0, scalar2=None, op0=ALU.mult),
                       reads=["negb"], writes=["negb"])
                    for i in range(2):
                        op("dve", lambda e, i=i: e.memset(vfs[i][:], 1.0), writes=[("vfs", i)])

                    def load_x(t):
                        i = t % 2
                        op("sp", lambda e: e.dma_start(out=xt[i][:], in_=xsrc[t * 128:(t + 1) * 128, :]), writes=[("xt", i)], dma=("xt", i))

                    def a_norm(t):
                        i = t % 2
                        op("act", lambda e: e.activation(out=junk[:], in_=xt[i][:], func=AF.Square, accum_out=ss[i][:]),
                           reads=[("xt", i)], writes=[("ss", i)])
                        op("act", lambda e: e.activation(out=rstd[i][:], in_=ss[i][:], func=AF.Ln, scale=1.0 / D, bias=c_eps[:]),
                           reads=[("ss", i), "c_eps"], writes=[("rstd", i)])
                        op("act", lambda e: e.activation(out=rstd[i][:], in_=rstd[i][:], func=AF.Exp, scale=-0.5),
                           reads=[("rstd", i)], writes=[("rstd", i)])
                        op("dve", lambda e: e.scalar_tensor_tensor(out=hb[i][:], in0=xt[i][:], scalar=rstd[i][:], in1=gpre[:],
                                                                   op0=ALU.mult, op1=ALU.mult),
                           reads=[("xt", i), ("rstd", i), "gpre"], writes=[("hb", i)])

                    def a_proj(t):
                        i = t % 2
                        trv = bank(0).bitcast(BF16).rearrange("p (a b) -> p a b", a=8)
                        for kc in range(8):
                            op("pe", lambda e, kc=kc: e.transpose(trv[:, kc, :], hb[i][:, kc * 128:(kc + 1) * 128], ident[:]),
                               reads=[("hb", i), "ident"], writes=["bk0"])
                        op("dve", lambda e: e.tensor_copy(out=hT[i][:], in_=trv), reads=["bk0"], writes=[("hT", i)])
                        for cg in range(6):
                            for kc in range(8):
                                op("pe", lambda e, cg=cg, kc=kc: e.matmul(bank(1 + cg), lhsT=hT[i][:, kc, :], rhs=winb[:, kc, cg * 512:(cg + 1) * 512],
                                                                         start=(kc == 0), stop=(kc == 7)),
                                   reads=[("hT", i), "winb"], writes=["bk%d" % (1 + cg)])
                        for kc in range(8):
                            op("pe", lambda e, kc=kc: e.matmul(bank(7)[0:8, 0:128], lhsT=winb[:, kc, 3072:3080], rhs=hT[i][:, kc, :],
                                                               start=(kc == 0), stop=(kc == 7)),
                               reads=[("hT", i), "winb"], writes=["bk7"])

                    def a_evac(t):
                        i = t % 2
                        c, tc = t // 4, t % 4
                        ci = c % 2
                        op("act", lambda e: e.activation(out=vds[i][:], in_=bank(3), func=AF.Copy), reads=["bk3"], writes=[("vds", i)])
                        op("sp", lambda e: e.dma_start(out=VDo[t * 128:(t + 1) * 128, :], in_=vds[i][:]), reads=[("vds", i)], writes=[("VDo", t)], dma=("vds", i, t // 8))
                        op("act", lambda e: e.activation(out=vfs[i][:, :, 0:64], in_=bank(6).rearrange("p (h d) -> p h d", h=8), func=AF.Copy),
                           reads=["bk6"], writes=[("vfs", i)])
                        op("sp", lambda e: e.dma_start(out=VFo[t * 128:(t + 1) * 128, :], in_=vfs[i][:].rearrange("p h d -> p (h d)")),
                           reads=[("vfs", i)], writes=[("VFo", t)], dma=("vfs", i, t // 8))
                        op("act", lambda e: e.activation(out=qtok[i][:, 512:1024], in_=bank(4), func=AF.Copy, scale=0.125),
                           reads=["bk4"], writes=[("qtokb", i)])
                        op("dve", lambda e: e.tensor_copy(out=qtok[i][:, 1536:2048], in_=bank(5)), reads=["bk5"], writes=[("qtokd", i)])
                        for which, bk, scale, col0 in ((0, 1, 0.125, 0), (1, 2, 1.0, 1024)):
                            tmp = q32[which]
                            dst = qtok[i][:, col0:col0 + 512].rearrange("p (m d) -> p m d", m=8)
                            bkv = bank(bk).rearrange("p (m d) -> p m d", m=8)
                            cs = cosT[:, t, :].unsqueeze(1).to_broadcast([128, 8, 8])
                            sn = sinT[:, t, :].unsqueeze(1).to_broadcast([128, 8, 8])
                            wk = ("qtoka", i) if which == 0 else ("qtokc", i)
                            op("act", lambda e, dst=dst, bkv=bkv, scale=scale: e.activation(out=dst, in_=bkv, func=AF.Copy, scale=scale),
                               reads=["bk%d" % bk], writes=[wk])
                            op("act", lambda e, tmp=tmp, bkv=bkv, scale=scale: e.activation(out=tmp[:, :, 0:16], in_=bkv[:, :, 0:16], func=AF.Copy, scale=scale),
                               reads=["bk%d" % bk], writes=[("q32", which)])
                            op("dve", lambda e, tmp=tmp, cs=cs, which=which: e.tensor_tensor(out=ra[which][:], in0=tmp[:, :, 0:8], in1=cs, op=ALU.mult),
                               reads=[("q32", which)], writes=[("ra", which)])
                            op("dve", lambda e, tmp=tmp, sn=sn, which=which: e.tensor_tensor(out=rb[which][:], in0=tmp[:, :, 8:16], in1=sn, op=ALU.mult),
                               reads=[("q32", which)], writes=[("rb", which)])
                            op("dve", lambda e, dst=dst, which=which: e.tensor_tensor(out=dst[:, :, 0:8], in0=ra[which][:], in1=rb[which][:], op=ALU.subtract),
                               reads=[("ra", which), ("rb", which)], writes=[wk])
                            op("dve", lambda e, tmp=tmp, cs=cs, which=which: e.tensor_tensor(out=ra[which][:], in0=tmp[:, :, 8:16], in1=cs, op=ALU.mult),
                               reads=[("q32", which)], writes=[("ra", which)])
                            op("dve", lambda e, tmp=tmp, sn=sn, which=which: e.tensor_tensor(out=rb[which][:], in0=tmp[:, :, 0:8], in1=sn, op=ALU.mult),
                               reads=[("q32", which)], writes=[("rb", which)])
                            op("dve", lambda e, dst=dst, which=which: e.tensor_tensor(out=dst[:, :, 8:16], in0=ra[which][:], in1=rb[which][:], op=ALU.add),
                               reads=[("ra", which), ("rb", which)], writes=[wk])
                        op("act", lambda e: e.activation(out=etmp[:], in_=bank(7)[0:8, 0:128], func=AF.Exp, scale=-1.0, bias=negb[:]),
                           reads=["bk7", "negb"], writes=["etmp"])
                        op("act", lambda e: e.activation(out=lst[ci][:, tc * 128:(tc + 1) * 128], in_=etmp[:], func=AF.Ln, scale=1.0, bias=c_one[0:8, :]),
                           reads=["etmp", "c_one"], writes=[("lst", ci)])

                    def a_trqk(t):
                        i = t % 2
                        c, tc = t // 4, t % 4
                        ci = c % 2
                        qk_keys = [("qtoka", i), ("qtokb", i), ("qtokc", i), ("qtokd", i)]
                        for which, bk, stg in ((0, 0, qTs), (1, 7, kTs)):
                            tv = bank(bk).bitcast(BF16).rearrange("p (a b) -> p a b", a=8)
                            for pr in range(8):
                                op("pe", lambda e, tv=tv, pr=pr, which=which: e.transpose(tv[:, pr, :], qtok[i][:, which * 1024 + pr * 128: which * 1024 + (pr + 1) * 128], ident[:]),
                                   reads=qk_keys + ["ident"], writes=["bk%d" % bk])
                            op("dve" if which == 0 else "act",
                               (lambda e, tv=tv, stg=stg: e.tensor_copy(out=stg[ci][:, :, tc * 128:(tc + 1) * 128], in_=tv)) if which == 0 else
                               (lambda e, tv=tv, stg=stg: e.activation(out=stg[ci][:, :, tc * 128:(tc + 1) * 128], in_=tv, func=AF.Copy)),
                               reads=["bk%d" % bk], writes=[("stg", which, ci)])
                        if tc == 3:
                            op("sp", lambda e: e.dma_start(out=QT[:, :, c * 512:(c + 1) * 512].rearrange("a p t -> p a t"), in_=qTs[ci][:]),
                               reads=[("stg", 0, ci)], dma=("stg", 0, ci))
                            op("sp", lambda e: e.dma_start(out=KTo[c // 2, :, (c % 2) * 512:(c % 2 + 1) * 512].rearrange("(a p) t -> p a t", p=128), in_=kTs[ci][:]),
                               reads=[("stg", 1, ci)], writes=[("KTo", c)], dma=("stg", 1, ci, c // 2))
                            op("sp", lambda e: e.dma_start(out=LTo[:, c * 512:(c + 1) * 512], in_=lst[ci][:]),
                               reads=[("lst", ci)], writes=[("LTo", c)], dma=("lst", ci))
                            if c % 2 == 1:
                                pc = c // 2
                                op("pool", lambda e: e.collective_compute("AllGather", ALU.bypass, replica_groups=GRP,
                                                                          ins=[KTo[pc, :, :]], outs=[KTa[pc, :, :]]),
                                   reads=[("KTo", 2 * pc), ("KTo", 2 * pc + 1)], dma=("ag", 0, pc), kind="cc")
                                op("pool", lambda e: e.collective_compute("AllGather", ALU.bypass, replica_groups=GRP,
                                                                          ins=[VDo[pc * 1024:(pc + 1) * 1024, :]], outs=[VDa[pc, :, :]]),
                                   reads=[("VDo", tt) for tt in range(8 * pc, 8 * pc + 8)], dma=("ag", 1, pc), kind="cc")
                                op("pool", lambda e: e.collective_compute("AllGather", ALU.bypass, replica_groups=GRP,
                                                                          ins=[VFo[pc * 1024:(pc + 1) * 1024, :]], outs=[VFa[pc, :, :]]),
                                   reads=[("VFo", tt) for tt in range(8 * pc, 8 * pc + 8)], dma=("ag", 2, pc), kind="cc")
                            if c == NCH - 1:
                                op("pool", lambda e: e.collective_compute("AllGather", ALU.bypass, replica_groups=GRP, ins=[LTo[:, :]], outs=[LTa[:, :]]),
                                   reads=[("LTo", cc_) for cc_ in range(NCH)], dma=("ag", 3, 0), kind="cc")

                    load_x(0)
                    load_x(1)
                    a_norm(0)
                    for t in range(NT):
                        a_proj(t)
                        if t + 1 < NT:
                            a_norm(t + 1)
                        if t + 2 < NT:
                            load_x(t + 2)
                        a_evac(t)
                        if t >= 1:
                            a_trqk(t - 1)
                    a_trqk(NT - 1)
                    P.barrier()

                if debug and l == nlayers - 1:
                    op("sp", lambda e: e.dma_start(out=dbg["d_cos"].ap(), in_=cosT[:]), reads=["cosT"], dma="dbg")
                    op("sp", lambda e: e.dma_start(out=dbg["d_sin"].ap(), in_=sinT[:]), reads=["sinT"], dma="dbg")
                    P.barrier()
                if stop_after == 'A' and l == nlayers - 1:
                    return
                if stop_after == 'AG' and l == nlayers - 1:
                    return
                with ExitStack() as st:
                    cs_ = sb("cs", [8, S], F32, st)
                    cn = sb("cn", [8, S], F32, st)
                    prt = [sb("prt%d" % i, [8, S], BF16, st) for i in range(3)]
                    r1 = cs_
                    cq = [sb("cqp%d" % i, [8, T], BF16, st) for i in range(3)]
                    cqt = sb("cqt", [8, 16, 128], BF16, st)
                    gmap = ((0, 0, 0), (0, 1, 3), (1, 0, 1), (1, 1, 2))
                    csv = cs_[:].rearrange("h (j q p) -> h j q p", q=4, p=128)
                    for (rk, e_, q_) in gmap:
                        op("sp", lambda e, rk=rk, e_=e_, q_=q_: e.dma_start(
                            out=csv[:, :, q_, :], in_=LTa[rk * 8:(rk + 1) * 8, :].rearrange("h (j e p) -> h j e p", e=2, p=128)[:, :, e_, :]),
                           writes=["cs"], dma="cs")
                    op("dve", lambda e: e.tensor_tensor_scan(out=cn[:], data0=ones32[0:8, 0:1].to_broadcast([8, S]), data1=cs_[:], initial=0.0, op0=ALU.mult, op1=ALU.add),
                       reads=["cs", "ones32"], writes=["cn"])
                    op("dve", lambda e: e.tensor_copy(out=prt[0][:], in_=cn[:]), reads=["cn"], writes=["p0"])
                    op("dve", lambda e: e.tensor_tensor(out=r1[:], in0=cn[:], in1=prt[0][:], op=ALU.subtract), reads=["cn", "p0"], writes=["cs"])
                    op("dve", lambda e: e.tensor_copy(out=prt[1][:], in_=r1[:]), reads=["cs"], writes=["p1"])
                    op("dve", lambda e: e.tensor_tensor(out=cn[:], in0=r1[:], in1=prt[1][:], op=ALU.subtract), reads=["cs", "p1"], writes=["cn"])
                    op("dve", lambda e: e.tensor_copy(out=prt[2][:], in_=cn[:]), reads=["cn"], writes=["p2"])
                    for pi in range(3):
                        pv = prt[pi][:].rearrange("h (j q p) -> h j q p", q=4, p=128)
                        for (rk, e_, q_) in gmap:
                            op("sp", lambda e, pi=pi, pv=pv, rk=rk, e_=e_, q_=q_: e.dma_start(
                                out=CK[:, pi, rk * T:(rk + 1) * T].rearrange("h (j e p) -> h j e p", e=2, p=128)[:, :, e_, :], in_=pv[:, :, q_, :]),
                               reads=["p%d" % pi], dma="ckst")
                        cqv = cq[pi][:].rearrange("h (j e p) -> h j e p", e=2, p=128)
                        for e_ in range(2):
                            q0 = 0 if e_ == 0 else 3
                            q1 = 1 if e_ == 0 else 2
                            op("dve", lambda e, pv=pv, q0=q0: e.tensor_scalar(out=cqt[:], in0=pv[:, :, q0, :], scalar1=sel[:, 0:1], scalar2=None, op0=ALU.mult),
                               reads=["p%d" % pi, "sel"], writes=["cqt"])
                            op("dve", lambda e, pv=pv, q1=q1, cqv=cqv, e_=e_: e.scalar_tensor_tensor(out=cqv[:, :, e_, :], in0=pv[:, :, q1, :], scalar=sel[:, 1:2], in1=cqt[:],
                                                                                                     op0=ALU.mult, op1=ALU.add),
                               reads=["p%d" % pi, "sel", "cqt"], writes=[("cq", pi)])
                        op("sp", lambda e, pi=pi: e.dma_start(out=CQ[:, pi, :], in_=cq[pi][:]), reads=[("cq", pi)], dma="ckst")
                    if debug and l == nlayers - 1:
                        for nm, t_ in (("d_QT", QT), ("d_KTa", KTa), ("d_VDa", VDa), ("d_VFa", VFa), ("d_LTa", LTa)):
                            op("sp", lambda e, nm=nm, t_=t_: e.dma_start(out=dbg[nm].ap(), in_=t_.ap()), dma="dbg")
                    P.barrier()
                    if debug and l == nlayers - 1:
                        for nm, t_ in (("d_CK", CK), ("d_CQ", CQ)):
                            op("sp", lambda e, nm=nm, t_=t_: e.dma_start(out=dbg[nm].ap(), in_=t_.ap()), dma="dbg")
                        P.barrier()

                if stop_after == 'S' and l == nlayers - 1:
                    return
                with ExitStack() as st:
                    KP = [sb("KP%d" % i, [128, 2 * T], BF16, st) for i in range(4)]
                    QP = [sb("QP%d" % i, [128, T], BF16, st) for i in range(4)]
                    VV = [sb("VV%d" % i, [128, 64, 128], BF16, st) for i in range(2)]
                    PT = [sb("PT%d" % i, [128, 512], BF16, st) for i in range(8)]
                    masks = sb("masks_sb", [128, 8, 512], BF16, st)
                    PTP = [sb("PTP%d" % i, [128, 2, 512], BF16, st) for i in range(4)]
                    op("sp", lambda e: e.dma_start(out=masks[:], in_=masks_in[:, :, :]), writes=["masks"], dma="c0_0")
                    lamv = sb("lamv", [1, 4, 64], F32, st)
                    lamt = sb("lamt", [1, 64], F32, st)
                    lams = sb("lams", [1, 4], F32, st)
                    nlam = sb("nlam", [1, 1], F32, st)
                    nlam128 = sb("nlam128", [128, 1], F32, st)
                    lacc = [sb("lacc%d" % i, [128, 512], F32, st) for i in range(2)]
                    rrf = [sb("rrf%d" % i, [128, 512], F32, st) for i in range(2)]
                    gsub = sb("gsub", [128, 1], F32, st)
                    rr = [sb("rr%d" % i, [128, 512], F32, st) for i in range(2)]
                    bcs = [sb("bcs%d" % i, [128, 512], F32, st) for i in range(2)]
                    Dt = sb("Dt", [128, 512], F32, st)
                    D2 = sb("D2", [128, 512], F32, st)
                    mo = [sb("mo%d" % i, [128, 512], BF16, st) for i in range(2)]
                    PS = [alloc(lambda nm: nc.psum_tensor(nm, [128, 1024], F32), "psB%d_%d" % (l, i), st) for i in range(4)]

                    def bank(b):
                        return PS[b // 2][:, (b % 2) * 512:(b % 2 + 1) * 512]
                    SB = (0, 1, 2)

                    for i_, tns in enumerate((lam_q1, lam_k1, lam_q2, lam_k2)):
                        op("sp", lambda e, i_=i_, tns=tns: e.dma_start(out=lamv[:, i_, :], in_=tns[l:l + 1, :]), writes=["lamv"], dma="lam")
                    op("sp", lambda e: e.dma_start(out=gsub[:], in_=diff_sub_g[l].rearrange("(p o) -> p o", o=1)), writes=["gsub"], dma="lam_g")
                    op("dve", lambda e: e.tensor_scalar(out=gsub[:], in0=gsub[:], scalar1=(1.0 - lam_init), scalar2=None, op0=ALU.mult),
                       reads=["gsub"], writes=["gsub"])
                    for j_ in range(2):
                        op("dve", lambda e, j_=j_: e.tensor_tensor(out=lamt[:], in0=lamv[:, 2 * j_, :], in1=lamv[:, 2 * j_ + 1, :], op=ALU.mult),
                           reads=["lamv"], writes=["lamt"])
                        op("act", lambda e, j_=j_: e.activation(out=lamt[:], in_=lamt[:], func=AF.Copy, accum_out=lams[:, j_:j_ + 1]),
                           reads=["lamt"], writes=["lamt", "lams"])
                    op("act", lambda e: e.activation(out=lams[:, 2:4], in_=lams[:, 0:2], func=AF.Exp), reads=["lams"], writes=["lams"])
                    op("dve", lambda e: e.tensor_tensor(out=nlam[:], in0=lams[:, 3:4], in1=lams[:, 2:3], op=ALU.subtract), reads=["lams"], writes=["nlam"])
                    op("dve", lambda e: e.tensor_scalar(out=nlam[:], in0=nlam[:], scalar1=-lam_init, scalar2=None, op0=ALU.add), reads=["nlam"], writes=["nlam"])

                    def load_map(mi):
                        i = mpos[mi] % 4
                        pr, hf = mi // 2, mi % 2
                        r0 = 64 if (mi < 8 and hf == 1) else 0
                        for pc in range(4):
                            op("sp", lambda e, pc=pc: e.dma_start(out=KP[i][r0:r0 + 64, :].rearrange("p (r c t) -> p r c t", r=2, c=4)[:, :, pc, :],
                                                                  in_=KTa[pc, :, :].rearrange("(r q) t -> q r t", r=2)[pr * 128 + hf * 64:pr * 128 + (hf + 1) * 64, :, :]),
                               writes=[("KP", i, 0, pc)] + ([("KP", i, 1), ("KP", i, 2)] if (r0 and pc == 3) else []), dma=("KP", i))
                        op("sp", lambda e: e.dma_start(out=QP[i][r0:r0 + 64, :], in_=QT[pr, hf * 64:(hf + 1) * 64, :]),
                           writes=[("QP", i, 0)] + ([("QP", i, 1), ("QP", i, 2)] if r0 else []), dma=("QP", i))
                        if mi >= 8:
                            h = mi - 8
                            op("dve", lambda e: e.memset(KP[i][64:70, :], 1.0), writes=[("KP", i, 1), ("KP", i, 2)])
                            op("dve", lambda e: e.memset(QP[i][64:70, :], 1.0), writes=[("QP", i, 1), ("QP", i, 2)])
                            op("sp", lambda e: e.dma_start(out=KP[i][64:67, :], in_=CK[h, :, :]), reads=[("KP", i, 2)], writes=[("KP", i, 1)], dma=("KP", i))
                            op("sp", lambda e: e.dma_start(out=QP[i][67:70, :], in_=CQ[h, :, :]), reads=[("QP", i, 2)], writes=[("QP", i, 1)], dma=("QP", i))

                    def load_v(vi, diff, h):
                        i = vi % 2
                        for X in range(2):
                            for c_ in range(4):
                                b0 = X * 32 + c_ * 8
                                if diff:
                                    op("sp", lambda e, X=X, c_=c_, b0=b0: e.dma_start(
                                        out=VV[i][:, b0:b0 + 8, :],
                                        in_=VDa[c_, X * 1024:(X + 1) * 1024, h * 128:(h + 1) * 128].rearrange("(b p) d -> p b d", p=128)),
                                       writes=[("VV", i, b0)], dma=("VV", i))
                                else:
                                    op("sp", lambda e, X=X, c_=c_, b0=b0: e.dma_start(
                                        out=VV[i][:, b0:b0 + 8, 0:65],
                                        in_=VFa[c_, X * 1024:(X + 1) * 1024, h * 65:(h + 1) * 65].rearrange("(b p) d -> p b d", p=128)),
                                       writes=[("VV", i, b0)], dma=("VV", i))

                    units = [("d", h) for h in range(4)] + [("f", h) for h in range(8)]
                    map_seq = []
                    for (kind, h) in units:
                        map_seq += ([2 * h, 2 * h + 1] if kind == "d" else [8 + h])
                    mpos = {mi: k for k, mi in enumerate(map_seq)}
                    LA = 3

                    op("pe", lambda e: e.matmul(bank(7)[:, 0:1], lhsT=ones32[0:1, :], rhs=nlam[0:1, 0:1], start=True, stop=True),
                       reads=["nlam", "ones32"], writes=[("BK", 7)])
                    op("dve", lambda e: e.tensor_copy(out=nlam128[:], in_=bank(7)[:, 0:1]), reads=[("BK", 7)], writes=["nlam128"])
                    for i in range(2):
                        op("dve", lambda e, i=i: e.memset(VV[i][:], 0.0), writes=[("VV", i, b0) for b0 in range(0, 64, 8)])

                    deferred = []

                    def post_fox(h, chunk, j, accb, now):
                        op("dve", lambda e: e.reciprocal(out=rrf[j][64:65, :], in_=bank(accb)[64:65, :]), reads=[("BK", accb)], writes=[("rrf", j)])

                        def later():
                            op("pe", lambda e: e.matmul(bank(7)[0:64, :], lhsT=ones32[64:65, 0:64], rhs=rrf[j][64:65, :], start=True, stop=True),
                               reads=[("rrf", j), "ones32"], writes=[("BK", 7)])
                            op("dve", lambda e: e.tensor_copy(out=bcs[j][0:64, :], in_=bank(7)[0:64, :]), reads=[("BK", 7)], writes=[("bcs", j)])
                            op("dve", lambda e: e.tensor_tensor(out=mo[j][0:64, :], in0=bank(accb)[0:64, :], in1=bcs[j][0:64, :], op=ALU.mult),
                               reads=[("BK", accb), ("bcs", j)], writes=[("mo", j)])
                            op("sp", lambda e: e.dma_start(out=MT[512 + h * 64:512 + (h + 1) * 64, chunk * 512:(chunk + 1) * 512], in_=mo[j][0:64, :]),
                               reads=[("mo", j)], dma=("mo", j))
                        deferred.append((now + 4, later))

                    def post_diff(h, chunk, j, now):
                        op("dve", lambda e: e.reciprocal(out=rr[0][:], in_=bank(6)), reads=[("BK", 6)], writes=[("rr", 0)])
                        op("dve", lambda e: e.tensor_tensor(out=Dt[:], in0=bank(4), in1=rr[0][:], op=ALU.mult), reads=[("BK", 4), ("rr", 0)], writes=["Dt"])
                        op("dve", lambda e: e.reciprocal(out=rr[1][:], in_=bank(7)), reads=[("BK", 7)], writes=[("rr", 1)])
                        op("dve", lambda e: e.tensor_tensor(out=D2[:], in0=bank(5), in1=rr[1][:], op=ALU.mult), reads=[("BK", 5), ("rr", 1)], writes=["D2"])
                        op("dve", lambda e: e.scalar_tensor_tensor(out=Dt[:], in0=D2[:], scalar=nlam128[:, 0:1], in1=Dt[:], op0=ALU.mult, op1=ALU.add),
                           reads=["Dt", "D2", "nlam128"], writes=["Dt"])
                        op("dve", lambda e: e.tensor_tensor(out=D2[:], in0=Dt[:], in1=Dt[:], op=ALU.mult), reads=["Dt"], writes=["D2"])

                        def later():
                            op("pe", lambda e: e.matmul(bank(7), lhsT=ones32[:, :], rhs=D2[:], start=True, stop=True),
                               reads=["D2", "ones32"], writes=[("BK", 7)])
                            op("act", lambda e: e.activation(out=rr[0][:], in_=bank(7), func=AF.Ln, scale=1.0 / 128, bias=c_eps[:]),
                               reads=[("BK", 7), "c_eps"], writes=[("rr", 0)])
                            op("act", lambda e: e.activation(out=rr[0][:], in_=rr[0][:], func=AF.Exp, scale=-0.5), reads=[("rr", 0)], writes=[("rr", 0)])
                            op("dve", lambda e: e.scalar_tensor_tensor(out=mo[j][:], in0=Dt[:], scalar=gsub[:, 0:1], in1=rr[0][:], op0=ALU.mult, op1=ALU.mult),
                               reads=["Dt", "gsub", ("rr", 0)], writes=[("mo", j)])
                            op("sp", lambda e: e.dma_start(out=MT[h * 128:(h + 1) * 128, chunk * 512:(chunk + 1) * 512], in_=mo[j][:]),
                               reads=[("mo", j)], dma=("mo", j))
                        deferred.append((now + 8, later))

                    stream = []
                    for ui_, (kind, h) in enumerate(units):
                        my_maps = [2 * h, 2 * h + 1] if kind == "d" else [8 + h]
                        for chunk in range(NCH):
                            allu = [(X, kb) for X in range(2) for kb in range(4 * chunk + 4)]
                            full = [(X, kb) for (X, kb) in allu if kb - 4 * chunk <= 0]
                            part = [(X, kb) for (X, kb) in allu if kb - 4 * chunk > 0]
                            ul = full[:-1] + part + full[-1:]
                            for uj, (X, kb) in enumerate(ul):
                                for mj, mi in enumerate(my_maps):
                                    stream.append(dict(ui=ui_, kind=kind, h=h, mi=mi, mj=mj, chunk=chunk, X=X, kb=kb, first=(uj == 0), last=(uj == len(ul) - 1),
                                                       fin=(uj == len(ul) - 1 and mj == len(my_maps) - 1), c0=max(0, kb - 4 * chunk) * 128))
                    loaded_maps = 0
                    last_pv = [None]
                    dpair = [0]
                    bcnt = [0]
                    loaded_v = 0
                    nS = len(stream)
                    pcount = 0
                    for idx in range(nS + LA):
                        if idx < nS:
                            u = stream[idx]
                            last_mi = (2 * u["h"] + 1) if u["kind"] == "d" else u["mi"]
                            want_m = min(len(map_seq), mpos[last_mi] + 3)
                            while loaded_maps < want_m:
                                load_map(map_seq[loaded_maps])
                                loaded_maps += 1
                            if loaded_v == 0:
                                load_v(0, units[0][0] == "d", units[0][1])
                                loaded_v = 1
                            def emit_qk(j_, s_):
                                uu = stream[j_]
                                sl = mpos[uu["mi"]] % 4
                                chunk, X, kb = uu["chunk"], uu["X"], uu["kb"]
                                if uu["kind"] == "d":
                                    r0 = 64 * uu["mj"]
                                    r1 = r0 + 64
                                else:
                                    r0, r1 = 0, 70
                                qv = QP[sl][r0:r1, chunk * 512:(chunk + 1) * 512]
                                kcol = X * T + kb * 128
                                m = kb - 4 * chunk
                                kq_reads = [("KP", sl, 0, pc_) for pc_ in range(4)] + [("KP", sl, 1), ("KP", sl, 2), ("QP", sl, 0), ("QP", sl, 1), ("QP", sl, 2)]
                                c0 = uu["c0"]
                                op("pe", lambda e: e.matmul(bank(s_)[:, c0:512], lhsT=KP[sl][r0:r1, kcol:kcol + 128], rhs=qv[:, c0:512], start=True, stop=(m < 0)),
                                   reads=kq_reads, writes=[("BK", s_)])

                            def emit_mask(j_, s_):
                                uu = stream[j_]
                                chunk, X, kb = uu["chunk"], uu["X"], uu["kb"]
                                m = kb - 4 * chunk
                                c0 = uu["c0"]
                                if m >= 0:
                                    op("pe", lambda e: e.matmul(bank(s_)[:, c0:c0 + 128], lhsT=ident[:], rhs=masks[:, X * 4 + m, c0:c0 + 128], start=False, stop=True),
                                       reads=["ident", "masks"], writes=[("BK", s_)])

                            if u["kind"] == "d":
                                if u["mj"] == 0:
                                    q_ = dpair[0] % 4
                                    sA = 2 * (dpair[0] % 2)
                                    dpair[0] += 1
                                    u["q"] = q_
                                    stream[idx + 1]["q"] = q_
                                    emit_qk(idx, sA)
                                    emit_qk(idx + 1, sA + 1)
                                    emit_mask(idx, sA)
                                    emit_mask(idx + 1, sA + 1)
                                    c0 = u["c0"]
                                    for mm_ in range(2):
                                        op("act", lambda e, mm_=mm_: e.activation(out=PTP[q_][:, mm_, c0:512], in_=bank(sA + mm_)[:, c0:512], func=AF.Exp),
                                           reads=[("BK", sA + mm_)], writes=[("PTP", q_, mm_)])
                            else:
                                s_ = idx % 4
                                p_ = idx % 8
                                emit_qk(idx, s_)
                                emit_mask(idx, s_)
                                c0 = u["c0"]
                                op("act", lambda e: e.activation(out=PT[p_][:, c0:512], in_=bank(s_)[:, c0:512], func=AF.Exp), reads=[("BK", s_)], writes=[("PT", p_)])
                        if idx == 100:
                            conv_rest(l, rd=[last_pv[0]])
                            if l + 1 < nlayers:
                                conv_in(l + 1, rd=[last_pv[0]])
                        if idx >= LA:
                            u = stream[idx - LA]
                            s_ = (idx - LA) % 3
                            p_ = (idx - LA) % 8
                            want_v = min(len(units), u["ui"] + 2)
                            while loaded_v < want_v:
                                load_v(loaded_v, units[loaded_v][0] == "d", units[loaded_v][1])
                                loaded_v += 1
                            vi = u["ui"] % 2
                            vb = u["X"] * 32 + u["kb"]
                            vkey = [("VV", vi, b0_) for b0_ in range(0, 64, 8)]
                            acc_bank = (4 + u["mj"]) if u["kind"] == "d" else (4 + u["chunk"] % 3)
                            first, last = u["first"], u["last"]
                            c0 = u["c0"]
                            if u["kind"] == "d":
                                ptv = PTP[u["q"]][:, u["mj"], c0:512]
                                ptkey = ("PTP", u["q"], u["mj"])
                            else:
                                ptv = PT[p_][:, c0:512]
                                ptkey = ("PT", p_)
                            last_pv[0] = op("pe", lambda e: e.matmul(bank(acc_bank)[:, c0:512], lhsT=VV[vi][:, vb, :], rhs=ptv, start=first, stop=last),
                                            reads=[ptkey] + vkey, writes=[("BK", acc_bank)])
                            if u["kind"] == "d":
                                l_bank = 6 + u["mj"]
                                mj = u["mj"]
                                if mj == 0:
                                    op("pe", lambda e: e.matmul(bank(l_bank)[:, c0:512], lhsT=onesb[:, :], rhs=ptv, start=first, stop=last),
                                       reads=[ptkey, "onesb"], writes=[("BK", l_bank)])
                                elif first:
                                    bcnt[0] = 0
                                    op("dve", lambda e: e.tensor_copy(out=lacc[1][:], in_=ptv), reads=[ptkey], writes=[("lacc", 1)])
                                    op("dve", lambda e: e.memset(lacc[0][:], 0.0), writes=[("lacc", 0)])
                                else:
                                    bcnt[0] += 1
                                    a_ = bcnt[0] % 2
                                    op("dve", lambda e: e.tensor_tensor(out=lacc[a_][:, c0:512], in0=lacc[a_][:, c0:512], in1=ptv, op=ALU.add),
                                       reads=[ptkey, ("lacc", a_)], writes=[("lacc", a_)])
                                if last and mj == 1:
                                    op("pe", lambda e: e.matmul(bank(l_bank), lhsT=ones32[:, :], rhs=lacc[1][:], start=True, stop=False),
                                       reads=[("lacc", 1), "ones32"], writes=[("BK", l_bank)])
                                    op("pe", lambda e: e.matmul(bank(l_bank), lhsT=ones32[:, :], rhs=lacc[0][:], start=False, stop=True),
                                       reads=[("lacc", 0), "ones32"], writes=[("BK", l_bank)])
                            if u["fin"]:
                                j = pcount % 2
                                pcount += 1
                                if u["kind"] == "f":
                                    post_fox(u["h"], u["chunk"], j, acc_bank, idx)
                                else:
                                    post_diff(u["h"], u["chunk"], j, idx)
                        while deferred and (deferred[0][0] <= idx or idx == nS + LA - 1):
                            deferred.pop(0)[1]()
                    if debug and l == nlayers - 1:
                        P.barrier()
                        op("sp", lambda e: e.dma_start(out=dbg["d_MT"].ap(), in_=MT.ap()), dma="dbg")
                    P.barrier()

                if stop_after == 'B' and l == nlayers - 1:
                    return
                with ExitStack() as st:
                    wout = sb("wout", [128, 8, D], BF16, st)
                    wdn = sb("wdn", [128, NFC, D], BF16, st)
                    gpost = sb("gpost", [128, D], F32, st)
                    gfpre = sb("gfpre", [128, D], F32, st)
                    gfpost = sb("gfpost", [128, D], F32, st)
                    mt = [sb("mt%d" % i, [128, 8, 512], BF16, st) for i in range(2)]
                    xt = [sb("xtc%d" % i, [128, D], F32, st) for i in range(2)]
                    x1k = [sb("x1k%d" % i, [128, D], F32, st) for i in range(8)]
                    junk = sb("junkc", [128, D], BF16, st)
                    ss = [sb("ssc%d" % i, [128, 1], F32, st) for i in range(4)]
                    rstd = [sb("rstdc%d" % i, [128, 1], F32, st) for i in range(4)]
                    hb = [sb("hbc%d" % i, [128, D], BF16, st) for i in range(2)]
                    hT = [sb("hTc%d" % i, [128, 8, 512], BF16, st) for i in range(2)]
                    AT = sb("AT", [128, NFC, 512], BF16, st)
                    wgp = [sb("wgp%d" % i, [128, 8, 256], BF16, st) for i in range(2)]
                    wup = [sb("wup%d" % i, [128, 8, 256], BF16, st) for i in range(2)]
                    sg = [sb("sg%d" % i, [128, 512], BF16, st) for i in range(2)]
                    PS = [alloc(lambda nm: nc.psum_tensor(nm, [128, 1024], F32), "psC%d_%d" % (l, i), st) for i in range(4)]

                    def bank(b):
                        return PS[b // 2][:, (b % 2) * 512:(b % 2 + 1) * 512]
                    op("sp", lambda e: e.dma_start(out=wout[:], in_=wout_b[l, :, :, :]), reads=[("cvres", "wout", l, kc) for kc in range(8)], writes=["wout"], dma="wc_wout")
                    op("sp", lambda e: e.dma_start(out=wdn[:], in_=wd_b[l, :, :, :]), reads=[("cvres", "wd", l, fc) for fc in range(NFC)], writes=["wdn"], dma="wc_wdn")
                    op("sp", lambda e: e.dma_start(out=gpost[:], in_=attn_post_g[l].partition_broadcast(128)), writes=["gpost"], dma="wc_gpost")
                    op("sp", lambda e: e.dma_start(out=gfpre[:], in_=ffn_pre_g[l].partition_broadcast(128)), writes=["gfpre"], dma="wc_gfpre")
                    op("sp", lambda e: e.dma_start(out=gfpost[:], in_=ffn_post_g[l].partition_broadcast(128)), writes=["gfpost"], dma="wc_gfpost")

                    def rms(src_ap, src_reads, i):
                        op("act", lambda e: e.activation(out=junk[:], in_=src_ap, func=AF.Square, accum_out=ss[i][:]), reads=src_reads, writes=[("ss", i)])
                        op("act", lambda e: e.activation(out=rstd[i][:], in_=ss[i][:], func=AF.Ln, scale=1.0 / D, bias=c_eps[:]),
                           reads=[("ss", i), "c_eps"], writes=[("rstd", i)])
                        op("act", lambda e: e.activation(out=rstd[i][:], in_=rstd[i][:], func=AF.Exp, scale=-0.5), reads=[("rstd", i)], writes=[("rstd", i)])

                    def load_mt(c):
                        op("sp", lambda e: e.dma_start(out=mt[c % 2][:], in_=MT[:, c * 512:(c + 1) * 512].rearrange("(a p) t -> p a t", p=128)),
                           writes=[("mt", c % 2)], dma=("mt", c % 2))

                    def s1_outproj(c, t):
                        tg = 4 * c + t
                        i = t % 2
                        op("sp", lambda e: e.dma_start(out=xt[i][:], in_=xsrc[tg * 128:(tg + 1) * 128, :]), writes=[("xt", i)], dma=("xt", i))
                        for half in range(2):
                            for kc in range(8):
                                op("pe", lambda e, half=half, kc=kc: e.matmul(bank(half), lhsT=mt[c % 2][:, kc, t * 128:(t + 1) * 128],
                                                                             rhs=wout[:, kc, half * 512:(half + 1) * 512], start=(kc == 0), stop=(kc == 7)),
                                   reads=[("mt", c % 2), "wout"], writes=[("bk", half)])

                    def s1_chain(c, t):
                        i = t % 2
                        xk = x1k[(c % 2) * 4 + t]
                        xkey = ("x1k", (c % 2) * 4 + t)
                        rms(PS[0][:], [("bk", 0), ("bk", 1)], 0)
                        op("dve", lambda e: e.scalar_tensor_tensor(out=xk[:], in0=PS[0][:], scalar=rstd[0][:], in1=gpost[:], op0=ALU.mult, op1=ALU.mult),
                           reads=[("bk", 0), ("bk", 1), ("rstd", 0), "gpost"], writes=[xkey])
                        op("dve", lambda e: e.tensor_tensor(out=xk[:], in0=xk[:], in1=xt[i][:], op=ALU.add), reads=[xkey, ("xt", i)], writes=[xkey])
                        rms(xk[:], [xkey], 1)
                        op("dve", lambda e: e.scalar_tensor_tensor(out=hb[i][:], in0=xk[:], scalar=rstd[1][:], in1=gfpre[:], op0=ALU.mult, op1=ALU.mult),
                           reads=[xkey, ("rstd", 1), "gfpre"], writes=[("hb", i)])

                    def s1_transpose(c, t):
                        i = t % 2
                        trv = bank(4).bitcast(BF16).rearrange("p (a b) -> p a b", a=8)
                        for kc in range(8):
                            op("pe", lambda e, kc=kc: e.transpose(trv[:, kc, :], hb[i][:, kc * 128:(kc + 1) * 128], ident[:]),
                               reads=[("hb", i), "ident"], writes=[("bk", 4)])
                        op("act", lambda e: e.activation(out=hT[c % 2][:, :, t * 128:(t + 1) * 128], in_=trv, func=AF.Copy), reads=[("bk", 4)], writes=[("hT", c % 2)])

                    def s2_piece(c, g):
                        gi = g % 2
                        op("sp", lambda e: e.dma_start(out=wgp[gi][:], in_=wg_b[l, :, :, g * 256:(g + 1) * 256]), reads=[("cvres", "wg", l, kc) for kc in range(8)], writes=[("wgp", gi)], dma=("wgp", gi))
                        op("sp", lambda e: e.dma_start(out=wup[gi][:], in_=wu_b[l, :, :, g * 256:(g + 1) * 256]), reads=[("cvres", "wu", l, kc) for kc in range(8)], writes=[("wup", gi)], dma=("wup", gi))
                        for fcl in range(2):
                            fc = 2 * g + fcl
                            gb, ub = ((5, 6) if fc % 2 == 0 else (7, 4))
                            for kc in range(8):
                                op("pe", lambda e, gb=gb, kc=kc, fcl=fcl: e.matmul(bank(gb), lhsT=wgp[gi][:, kc, fcl * 128:(fcl + 1) * 128], rhs=hT[c % 2][:, kc, :],
                                                                                   start=(kc == 0), stop=(kc == 7)),
                                   reads=[("wgp", gi), ("hT", c % 2)], writes=[("bk", gb)])
                            for kc in range(8):
                                op("pe", lambda e, ub=ub, kc=kc, fcl=fcl: e.matmul(bank(ub), lhsT=wup[gi][:, kc, fcl * 128:(fcl + 1) * 128], rhs=hT[c % 2][:, kc, :],
                                                                                   start=(kc == 0), stop=(kc == 7)),
                                   reads=[("wup", gi), ("hT", c % 2)], writes=[("bk", ub)])
                            sgi = fc % 2
                            op("act", lambda e, gb=gb, sgi=sgi: e.activation(out=sg[sgi][:], in_=bank(gb), func=AF.Silu), reads=[("bk", gb)], writes=[("sg", sgi)])
                            op("dve", lambda e, ub=ub, sgi=sgi, fc=fc: e.tensor_tensor(out=AT[:, fc, :], in0=bank(ub), in1=sg[sgi][:], op=ALU.mult),
                               reads=[("bk", ub), ("sg", sgi)], writes=["AT"])

                    def s3_mm(c, t):
                        yb = 2 * (t % 2)
                        for half in range(2):
                            for fc in range(NFC):
                                op("pe", lambda e, half=half, fc=fc: e.matmul(bank(yb + half), lhsT=AT[:, fc, t * 128:(t + 1) * 128],
                                                                             rhs=wdn[:, fc, half * 512:(half + 1) * 512], start=(fc == 0), stop=(fc == NFC - 1)),
                                   reads=["AT", "wdn"], writes=[("bk", yb + half)])

                    def s3_post(c, t):
                        tg = 4 * c + t
                        yb = 2 * (t % 2)
                        i = t % 2
                        xk = x1k[(c % 2) * 4 + t]
                        xkey = ("x1k", (c % 2) * 4 + t)
                        rms(PS[t % 2][:], [("bk", yb), ("bk", yb + 1)], 2 + i)
                        op("dve", lambda e: e.scalar_tensor_tensor(out=xt[i][:], in0=PS[t % 2][:], scalar=rstd[2 + i][:], in1=gfpost[:], op0=ALU.mult, op1=ALU.mult),
                           reads=[("bk", yb), ("bk", yb + 1), ("rstd", 2 + i), "gfpost"], writes=[("xt", i)])
                        op("dve", lambda e: e.tensor_tensor(out=xt[i][:], in0=xt[i][:], in1=xk[:], op=ALU.add), reads=[xkey, ("xt", i)], writes=[("xt", i)])
                        op("sp", lambda e: e.dma_start(out=xdst[tg * 128:(tg + 1) * 128, :], in_=xt[i][:]), reads=[("xt", i)], dma=("xo", i))

                    load_mt(0)
                    for t in range(4):
                        s1_outproj(0, t)
                        s1_chain(0, t)
                        s1_transpose(0, t)
                    for c in range(NCH):
                        nxt = c + 1 if c + 1 < NCH else None
                        if nxt is not None:
                            load_mt(nxt)
                        for g in range(11):
                            s2_piece(c, g)
                            if nxt is not None:
                                if g in (0, 2, 4, 6):
                                    tt = g // 2
                                    s1_outproj(nxt, tt)
                                    s1_chain(nxt, tt)
                                if g in (2, 4, 6, 8):
                                    s1_transpose(nxt, g // 2 - 1)
                        s3_mm(c, 0)
                        for t in range(4):
                            if t + 1 < 4:
                                s3_mm(c, t + 1)
                            s3_post(c, t)
                    P.barrier()
    P.run(gen)
    return nc


_NC_CACHE = {}


def kernel(**inputs):
    x = np.asarray(inputs["x"], dtype=np.float32)
    pos = np.asarray(inputs["positions"]).astype(np.int32)
    B = x.shape[0]
    assert B == 4 and x.shape[1] == S and x.shape[2] == D
    inv_freq = (1.0 / (np.float32(500000.0) ** (np.arange(0, 16, 2, dtype=np.float32) / np.float32(16)))).astype(np.float32)
    invf = np.ascontiguousarray(np.broadcast_to(inv_freq[None, :], (128, 8))).astype(np.float32)
    if "nc" not in _NC_CACHE:
        _NC_CACHE["nc"] = build()
    nc = _NC_CACHE["nc"]
    shared = {}
    for k in ("attn_pre_g", "w_in", "forget_bias", "lam_q1", "lam_k1", "lam_q2", "lam_k2", "diff_sub_g", "w_out",
              "attn_post_g", "ffn_pre_g", "w_gate", "w_up", "w_down", "ffn_post_g"):
        shared[k] = np.ascontiguousarray(np.asarray(inputs[k], dtype=np.float32))
    in_maps = []
    idxs = []
    for core in range(8):
        b, r = core // 2, core % 2
        blocks = own_blocks(r)
        tok = np.concatenate([np.arange(g * 128, (g + 1) * 128) for g in blocks])
        idxs.append((b, tok))
        m = dict(shared)
        m["x"] = np.ascontiguousarray(x[b][tok])
        m["pos"] = np.ascontiguousarray(pos[b][tok].reshape(NT, 128).T)
        m["masks"] = make_masks(r)
        selv = np.zeros((8, 2), np.float32)
        selv[:, r] = -1.0
        m["sel"] = selv
        m["invf"] = invf
        in_maps.append(m)
    res = run_bass_kernel_spmd(nc, in_maps, core_ids=list(range(8)))
    out = np.empty((B, S, D), np.float32)
    for core in range(8):
        b, tok = idxs[core]
        out[b][tok] = np.asarray(res.results[core]["out"], dtype=np.float32)
    return out
```
